# Optimizing a Trainium2 kernel written in Bass

```python
import jax, jax.numpy as jnp
from jax import lax
import numpy as np

D_MODEL = 1024
BATCH = 8
SEQ = 2048
DEPTH = 2
DEC_BATCH = 128
DEC_SEQ = 8
PAST_LEN = 16384
PAGE_SIZE = 128

N_RET_HEADS = 8
RET_DK = 64
RET_DV = 128
RET_QK = N_RET_HEADS * RET_DK
RET_V = N_RET_HEADS * RET_DV
RET_CHUNK = 128
ROPE_BASE = 10000.0
CONF_W = 512
CONF_K = 31
SC_W = 512
SC_K = 3
N_BRANCH = 3
D_FF = 2816
IN_COLS = 2 * RET_QK + 2 * RET_V + 2 * CONF_W + 3 * SC_W + N_BRANCH * D_MODEL
N_ADA = 9

kernel_name = "hybrid_retention_conformer_shortconv_step"


def rms_norm(x, g, eps=1e-6):
    xf = x.astype(jnp.float32)
    y = xf * lax.rsqrt(jnp.mean(xf * xf, axis=-1, keepdims=True) + eps)
    return (y * g.astype(jnp.float32)).astype(x.dtype)


def layer_norm(x, g, b, eps=1e-5):
    xf = x.astype(jnp.float32)
    mu = jnp.mean(xf, axis=-1, keepdims=True)
    var = jnp.mean(jnp.square(xf - mu), axis=-1, keepdims=True)
    y = (xf - mu) * lax.rsqrt(var + eps)
    return (y * g.astype(jnp.float32) + b.astype(jnp.float32)).astype(x.dtype)


def modulate(h, shift, scale):
    return h * (1.0 + scale) + shift


def swiglu(h, w1, w3, w2):
    return (jax.nn.silu(h @ w1) * (h @ w3)) @ w2


def rotary(x, pos):
    half = x.shape[-1] // 2
    freqs = ROPE_BASE ** (-jnp.arange(half, dtype=jnp.float32) / half)
    ang = pos[:, None] * freqs[None, :]
    cos = jnp.cos(ang)[None, :, None, :]
    sin = jnp.sin(ang)[None, :, None, :]
    x1, x2 = x[..., :half], x[..., half:]
    return jnp.concatenate([x1 * cos - x2 * sin, x1 * sin + x2 * cos], axis=-1)


def retention_log_gamma():
    return jnp.log(1.0 - jnp.exp2(-5.0 - jnp.arange(N_RET_HEADS, dtype=jnp.float32)))


def retention(q, k, v, S0, chunk):
    B, T, H, DK = q.shape
    DV = v.shape[-1]
    n = T // chunk
    lg = retention_log_gamma()
    idx = jnp.arange(chunk, dtype=jnp.float32)
    diff = idx[:, None] - idx[None, :]
    causal = diff >= 0
    decay = jnp.where(causal[None], jnp.exp(jnp.maximum(diff, 0.0)[None] * lg[:, None, None]), 0.0)
    q_decay = jnp.exp((idx + 1.0)[None, :] * lg[:, None]).T
    k_decay = jnp.exp((chunk - 1.0 - idx)[None, :] * lg[:, None])
    chunk_decay = jnp.exp(chunk * lg)

    def to_chunks(a):
        return a.reshape(B, n, chunk, H, a.shape[-1]).transpose(1, 0, 2, 3, 4)

    def step(S, inp):
        qc, kc, vc = inp
        scores = jnp.einsum('bihd,bjhd->bhij', qc, kc) * decay[None]
        o_intra = jnp.einsum('bhij,bjhe->bihe', scores, vc)
        o_cross = jnp.einsum('bihd,bhde->bihe', qc, S) * q_decay[None, :, :, None]
        S_new = S * chunk_decay[None, :, None, None] + jnp.einsum('bjhd,bjhe,hj->bhde', kc, vc, k_decay)
        return S_new, o_intra + o_cross

    S_fin, o = lax.scan(step, S0, (to_chunks(q), to_chunks(k), to_chunks(v)))
    o = o.transpose(1, 0, 2, 3, 4).reshape(B, T, H, DV)
    return o, S_fin


def head_group_norm(o, g, eps=1e-5):
    mu = jnp.mean(o, axis=-1, keepdims=True)
    var = jnp.mean(jnp.square(o - mu), axis=-1, keepdims=True)
    y = (o - mu) * lax.rsqrt(var + eps)
    B, T = o.shape[:2]
    return y.reshape(B, T, -1) * g.astype(jnp.float32)


def causal_dwconv(u, buf, w):
    K = w.shape[0]
    padded = jnp.concatenate([buf.astype(u.dtype), u], axis=1)
    out = lax.conv_general_dilated(
        padded, w[:, None, :].astype(u.dtype), window_strides=(1,), padding='VALID',
        dimension_numbers=('NWC', 'WIO', 'NWC'), feature_group_count=u.shape[-1])
    return out, padded[:, -(K - 1):]


def mixer(h, S0, conf_buf, sc_buf, pos0, w):
    B, T, _ = h.shape
    z = h @ w['w_in']
    sizes = [RET_QK, RET_QK, RET_V, RET_V, CONF_W, CONF_W, SC_W, SC_W, SC_W]
    splits = list(np.cumsum(sizes))
    q, k, v, g, ca, cb, sb, sc_c, sx, gl = jnp.split(z, splits, axis=-1)

    pos = pos0 + jnp.arange(T, dtype=jnp.float32)
    qf = rotary(q.astype(jnp.float32).reshape(B, T, N_RET_HEADS, RET_DK), pos) * (RET_DK ** -0.5)
    kf = rotary(k.astype(jnp.float32).reshape(B, T, N_RET_HEADS, RET_DK), pos)
    vf = v.astype(jnp.float32).reshape(B, T, N_RET_HEADS, RET_DV)
    chunk = RET_CHUNK if T % RET_CHUNK == 0 else T
    o, S_new = retention(qf, kf, vf, S0.astype(jnp.float32), chunk)
    o = head_group_norm(o, w['ret_gn_g']).astype(h.dtype)
    br_ret = (jax.nn.silu(g) * o) @ w['w_ret_out']

    u = ca * jax.nn.sigmoid(cb)
    cv, conf_new = causal_dwconv(u, conf_buf, w['conf_conv_w'])
    cv = layer_norm(cv + w['conf_conv_b'], w['conf_ln_g'], w['conf_ln_b'])
    br_conf = jax.nn.silu(cv) @ w['w_conf_out']

    us = sc_c * sx
    sv, sc_new = causal_dwconv(us, sc_buf, w['sc_conv_w'])
    br_sc = (sb * sv) @ w['w_sc_out']

    gates = jax.nn.sigmoid(gl + w['b_gate'])
    g_ret, g_conf, g_sc = jnp.split(gates, N_BRANCH, axis=-1)
    m = g_ret * br_ret + g_conf * br_conf + g_sc * br_sc
    return m @ w['w_o'], S_new.astype(S0.dtype), conf_new, sc_new


def trunk(x, c, state_ret, state_conf, state_sconv, pos0, p):
    rets, confs, scs = [], [], []
    for l in range(DEPTH):
        w = {name: arr[l] for name, arr in p.items() if name != 'g_final'}
        ada = jax.nn.silu(c) @ w['w_ada'] + w['b_ada']
        ada = ada[:, None, :]
        sh1, sc1, gt1, sh2, sc2, gt2, sh3, sc3, gt3 = jnp.split(ada, N_ADA, axis=-1)
        h = modulate(rms_norm(x, w['g_ffn1']), sh1, sc1)
        x = x + 0.5 * gt1 * swiglu(h, w['w1_a'], w['w3_a'], w['w2_a'])
        h = modulate(rms_norm(x, w['g_mix']), sh2, sc2)
        mo, S_new, conf_new, sc_new = mixer(h, state_ret[l], state_conf[l], state_sconv[l], pos0, w)
        x = x + gt2 * mo
        h = modulate(rms_norm(x, w['g_ffn2']), sh3, sc3)
        x = x + 0.5 * gt3 * swiglu(h, w['w1_b'], w['w3_b'], w['w2_b'])
        rets.append(S_new); confs.append(conf_new); scs.append(sc_new)
    y = rms_norm(x, p['g_final'])
    return y, jnp.stack(rets), jnp.stack(confs), jnp.stack(scs)


def setup_inputs(seed: int = 0) -> dict:
    key = jax.random.key(seed)
    ks = iter(jax.random.split(key, 64))
    f32 = jnp.float32

    def nrm(shape, scale):
        return jax.random.normal(next(ks), shape, f32) * scale

    def gain(shape):
        return 1.0 + 0.1 * jax.random.normal(next(ks), shape, f32)

    D, L = D_MODEL, DEPTH
    return {
        "x_prompt": nrm((BATCH, SEQ, D), 1.0),
        "x_sample": nrm((DEC_BATCH, DEC_SEQ, D), 1.0),
        "c_prompt": nrm((BATCH, D), 1.0),
        "c_sample": nrm((DEC_BATCH, D), 1.0),
        "state_ret": nrm((L, DEC_BATCH, N_RET_HEADS, RET_DK, RET_DV), 1.0),
        "state_conf": nrm((L, DEC_BATCH, CONF_K - 1, CONF_W), 1.0),
        "state_sconv": nrm((L, DEC_BATCH, SC_K - 1, SC_W), 1.0),
        "w_ada": nrm((L, D, N_ADA * D), 0.5 * D ** -0.5),
        "b_ada": nrm((L, N_ADA * D), 0.01),
        "g_ffn1": gain((L, D)),
        "w1_a": nrm((L, D, D_FF), D ** -0.5),
        "w3_a": nrm((L, D, D_FF), D ** -0.5),
        "w2_a": nrm((L, D_FF, D), D_FF ** -0.5),
        "g_mix": gain((L, D)),
        "w_in": nrm((L, D, IN_COLS), D ** -0.5),
        "b_gate": nrm((L, N_BRANCH * D), 0.01),
        "ret_gn_g": gain((L, RET_V)),
        "w_ret_out": nrm((L, RET_V, D), RET_V ** -0.5),
        "conf_conv_w": nrm((L, CONF_K, CONF_W), CONF_K ** -0.5),
        "conf_conv_b": nrm((L, CONF_W), 0.01),
        "conf_ln_g": gain((L, CONF_W)),
        "conf_ln_b": nrm((L, CONF_W), 0.01),
        "w_conf_out": nrm((L, CONF_W, D), CONF_W ** -0.5),
        "sc_conv_w": nrm((L, SC_K, SC_W), SC_K ** -0.5),
        "w_sc_out": nrm((L, SC_W, D), SC_W ** -0.5),
        "w_o": nrm((L, D, D), D ** -0.5),
        "g_ffn2": gain((L, D)),
        "w1_b": nrm((L, D, D_FF), D ** -0.5),
        "w3_b": nrm((L, D, D_FF), D ** -0.5),
        "w2_b": nrm((L, D_FF, D), D_FF ** -0.5),
        "g_final": gain((D,)),
    }


def reference(x_prompt, x_sample, c_prompt, c_sample, state_ret, state_conf, state_sconv,
              w_ada, b_ada, g_ffn1, w1_a, w3_a, w2_a, g_mix, w_in, b_gate, ret_gn_g, w_ret_out,
              conf_conv_w, conf_conv_b, conf_ln_g, conf_ln_b, w_conf_out, sc_conv_w, w_sc_out, w_o,
              g_ffn2, w1_b, w3_b, w2_b, g_final):
    p = dict(w_ada=w_ada, b_ada=b_ada, g_ffn1=g_ffn1, w1_a=w1_a, w3_a=w3_a, w2_a=w2_a,
             g_mix=g_mix, w_in=w_in, b_gate=b_gate, ret_gn_g=ret_gn_g, w_ret_out=w_ret_out,
             conf_conv_w=conf_conv_w, conf_conv_b=conf_conv_b, conf_ln_g=conf_ln_g,
             conf_ln_b=conf_ln_b, w_conf_out=w_conf_out, sc_conv_w=sc_conv_w, w_sc_out=w_sc_out,
             w_o=w_o, g_ffn2=g_ffn2, w1_b=w1_b, w3_b=w3_b, w2_b=w2_b, g_final=g_final)
    B = x_prompt.shape[0]
    ret0 = jnp.zeros((DEPTH, B, N_RET_HEADS, RET_DK, RET_DV), state_ret.dtype)
    conf0 = jnp.zeros((DEPTH, B, CONF_K - 1, CONF_W), x_prompt.dtype)
    sc0 = jnp.zeros((DEPTH, B, SC_K - 1, SC_W), x_prompt.dtype)
    y_prompt, ret_p, conf_p, sc_p = trunk(x_prompt, c_prompt, ret0, conf0, sc0, 0, p)
    y_sample, ret_s, conf_s, sc_s = trunk(x_sample, c_sample, state_ret, state_conf, state_sconv, PAST_LEN, p)
    return (y_prompt, y_sample, ret_p, ret_s, conf_p, conf_s, sc_p, sc_s)
```

```python
import os
import numpy as np
import concourse.bass as bass
import concourse.mybir as mybir
from concourse.bass_utils import run_bass_kernel_spmd

F32 = mybir.dt.float32
BF16 = mybir.dt.bfloat16
AF = mybir.ActivationFunctionType
ALU = mybir.AluOpType

D = 1024
DFF = 2816
NT = 2176
NP = 2048
NS = 17
L = 2
INC = 8704
GROUPS = [(0, 512), (512, 512), (1024, 512), (1536, 512), (2048, 128)]
SB_BASE = 16640
SB_END = 229376
ANNOTATE = bool(int(os.environ.get('ANNOTATE', '0')))

VR_PER_LAYER = 276
VR_TOTAL = 2 * VR_PER_LAYER + 8
VO = dict(b_ada=0, g_ffn1=72, g_mix=80, g_ffn2=88, b_gate=96, gn=120, ccw=128, ccb=252, lng=256, lnb=260, scw=264)


def vrow(l, name, i=0):
    return l * VR_PER_LAYER + VO[name] + i


class Sched:
    ENGS = ("pe", "act", "dve", "pool", "sp")

    def __init__(self, nc):
        self.nc = nc
        self.ops = {e: [] for e in self.ENGS}
        self.cnt = {e: 0 for e in self.ENGS}
        self.last_w = {}
        self.readers = {}
        self.dma_sems = {}
        self.eng_sem = {}
        self.bar = []
        self.phase = 'setup'

    def _deps(self, reads, writes):
        deps = list(self.bar)
        for r in reads:
            t = self.last_w.get(r)
            if t is not None:
                deps.append(t)
        for w in writes:
            t = self.last_w.get(w)
            if t is not None:
                deps.append(t)
            deps.extend(self.readers.get(w, ()))
        return deps

    def _commit(self, tok, reads, writes):
        for r in reads:
            self.readers.setdefault(r, []).append(tok)
        for w in writes:
            self.last_w[w] = tok
            self.readers[w] = []

    @staticmethod
    def _split(reads, writes):
        r2 = [r for r in reads if not r.startswith("ps")]
        w2 = list(writes) + [r for r in reads if r.startswith("ps")]
        return r2, w2

    def op(self, eng, fn, reads=(), writes=()):
        reads, writes = self._split(reads, writes)
        deps = self._deps(reads, writes)
        self.cnt[eng] += 1
        tok = ("c", eng, self.cnt[eng])
        self.ops[eng].append(dict(fn=fn, deps=deps, tok=tok, ph=self.phase))
        self._commit(tok, reads, writes)
        return tok

    def dma(self, eng, fn, key, reads=(), writes=()):
        deps = self._deps(reads, writes)
        if key not in self.dma_sems:
            self.dma_sems[key] = [self.nc.alloc_semaphore("d_" + key), 0]
        ent = self.dma_sems[key]
        ent[1] += 16
        tok = ("d", key, ent[1])
        self.ops[eng].append(dict(fn=fn, deps=deps, tok=tok, ph=self.phase))
        self._commit(tok, reads, writes)
        return tok

    def barrier(self):
        toks = []
        for e in self.ENGS:
            if self.cnt[e]:
                toks.append(("c", e, self.cnt[e]))
        for k, (s, v) in self.dma_sems.items():
            if v:
                toks.append(("d", k, v))
        self.bar = toks

    def emit(self, final_waits=()):
        nc = self.nc
        for e in self.ENGS:
            self.eng_sem[e] = nc.alloc_semaphore("e_" + e)

        def run(ename, eng):
            waited = {}
            for o in self.ops[ename]:
                need = {}
                for t in o["deps"]:
                    if t[0] == "c":
                        if t[1] == ename and ename == "pe":
                            continue
                        sem = self.eng_sem[t[1]]
                    else:
                        sem = self.dma_sems[t[1]][0]
                    k = sem.num
                    if t[2] > need.get(k, (None, 0))[1]:
                        need[k] = (sem, t[2])
                for k, (sem, v) in need.items():
                    if waited.get(k, 0) >= v:
                        continue
                    eng.wait_ge(sem, v)
                    waited[k] = v
                ins = o["fn"](eng)
                if ANNOTATE:
                    ins.annotate(o["ph"])
                t = o["tok"]
                if t[0] == "c":
                    ins.then_inc(self.eng_sem[ename], 1)
                else:
                    ins.then_inc(self.dma_sems[t[1]][0], 16)
            if ename == "sp":
                for t in final_waits:
                    if t[0] == "c":
                        eng.wait_ge(self.eng_sem[t[1]], t[2])
                    else:
                        eng.wait_ge(self.dma_sems[t[1]][0], t[2])

        with nc.Block() as block:
            @block.tensor
            def _(e):
                run("pe", e)

            @block.scalar
            def _(e):
                run("act", e)

            @block.vector
            def _(e):
                run("dve", e)

            @block.gpsimd
            def _(e):
                run("pool", e)

            @block.sync
            def _(e):
                run("sp", e)


class Arena:
    def __init__(self, nc, base, end, tag):
        self.nc, self.base, self.end, self.tag, self.cur, self.n = nc, base, end, tag, base, 0

    def alloc(self, name, shape, dtype):
        esz = 4 if dtype == F32 else 2
        nbytes = int(np.prod(shape[1:])) * esz
        nbytes = (nbytes + 63) // 64 * 64
        assert self.cur + nbytes <= self.end, (self.tag, name, self.cur + nbytes - self.end)
        t = self.nc.alloc_sbuf_tensor_at("%s_%s_%d" % (self.tag, name, self.n), list(shape), dtype, offset=self.cur)
        self.n += 1
        self.cur += nbytes
        return t


def I(name, *a, **kw):
    return lambda e: getattr(e, name)(*a, **kw)


def bc(ap, shape):
    return ap.to_broadcast(list(shape))


def build_program(mixer_on=True, n_layers=L):
    nc = bass.Bass("TRN2", target_bir_lowering=False)
    S = Sched(nc)

    def din(name, shape, dt=F32):
        return nc.dram_tensor(name, list(shape), dt, kind="ExternalInput").ap()

    def dout(name, shape):
        return nc.dram_tensor(name, list(shape), F32, kind="ExternalOutput").ap()

    xp = din("xp", [NP, D])
    xs = din("xs", [128, D])
    cc = din("cc", [NS, D])
    vecs = din("vecs", [VR_TOTAL, 128])
    ident_d = din("ident", [128, 128])
    w_ada = din("w_ada", [L, D, 9 * D])
    w1 = [din("w1_a", [L, D, DFF]), din("w1_b", [L, D, DFF])]
    w3 = [din("w3_a", [L, D, DFF]), din("w3_b", [L, D, DFF])]
    w2 = [din("w2_a", [L, DFF, D]), din("w2_b", [L, DFF, D])]
    w_in = din("w_in", [L, D, INC])
    w_ret_out = din("w_ret_out", [L, D, D])
    w_conf_out = din("w_conf_out", [L, 512, D])
    w_sc_out = din("w_sc_out", [L, 512, D])
    w_o = din("w_o", [L, D, D])
    st_ret = din("st_ret", [L, 16, 8, 64, 128])
    st_conf = din("st_conf", [L, 16, 30, 512])
    st_sc = din("st_sc", [L, 16, 2, 512])
    rot_d = din("rot", [17, 128, 3, 32])
    dmask_d = din("dmask", [2, 128, 8, 128])
    qk_dec_d = din("qkdec", [128, 2, 2, 8])
    zmask_d = din("zmask", [128, 8, 128])
    kmask_d = din("kmask", [128, 16])
    yp = dout("yp", [NP, D])
    ys = dout("ys", [128, D])
    retp = dout("retp", [L, 8, 64, 128])
    rets = dout("rets", [L, 16, 8, 64, 128])
    confp = dout("confp", [L, 30, 512])
    confs = dout("confs", [L, 16, 30, 512])
    scp = dout("scp", [L, 2, 512])
    scs = dout("scs", [L, 16, 2, 512])
    finals = []

    P = Arena(nc, SB_BASE, SB_END, "P")
    xT = P.alloc("xT", [128, 8, NT], F32)
    hT = P.alloc("hT", [128, 8, NT], BF16)
    mod = P.alloc("mod", [128, 9, 8, NS], F32)
    vecT = P.alloc("vecT", [128, VR_TOTAL], F32)
    ident = P.alloc("ident", [128, 128], F32)
    identb = P.alloc("identb", [128, 128], BF16)
    onesb = P.alloc("onesb", [128, 128], BF16)
    scT = P.alloc("scT", [128, 8, NS], BF16)
    mhalf = P.alloc("mhalf", [128, 512], F32)
    MT_BASE = P.cur
    mT = P.alloc("mT", [128, 8, NT], BF16)
    PH_BASE = P.cur

    banks = [nc.alloc_psum_tensor("bank%d" % i, [128, 512], F32) for i in range(8)]
    bn = ["ps%d" % i for i in range(8)]

    S.dma("sp", I("dma_start", out=ident[:], in_=ident_d), key="c_id", writes=["ident"])
    S.op("act", I("copy", out=identb[:], in_=ident[:]), reads=["ident"], writes=["identb"])
    S.op("pool", I("memset", onesb[:], 1.0), writes=["onesb"])
    S.op("pool", I("memset", mhalf[:], -0.5), writes=["mhalf"])

    A0 = Arena(nc, PH_BASE, SB_END, "A0")
    xstage = [A0.alloc("xst%d" % i, [128, D], F32) for i in range(2)]
    vst = A0.alloc("vst", [112, 5, 128], F32)
    cst = A0.alloc("cst", [NS, D], F32)
    cT = A0.alloc("cT", [128, 8, NS], F32)

    for ti in range(17):
        st = xstage[ti % 2]
        src = xp[ti * 128:(ti + 1) * 128, :] if ti < 16 else xs
        S.dma("sp", I("dma_start", out=st[:], in_=src), key="xst%d" % (ti % 2),
              writes=["xst%d" % (ti % 2)])
        for hb in range(2):
            b = banks[(ti % 2) * 2 + hb]
            for c4 in range(4):
                c = hb * 4 + c4
                S.op("pe", I("transpose", out=b[:, c4 * 128:(c4 + 1) * 128], in_=st[:, c * 128:(c + 1) * 128], identity=ident[:]),
                    reads=["xst%d" % (ti % 2), "ident"], writes=[bn[(ti % 2) * 2 + hb]])
            eng = "act" if hb == 0 else "dve"
            dst = xT[:, hb * 4:hb * 4 + 4, ti * 128:(ti + 1) * 128]
            srcp = b[:].rearrange("p (a b) -> p a b", a=4)
            if eng == "act":
                S.op("act", I("copy", out=dst, in_=srcp),
                     reads=[bn[(ti % 2) * 2 + hb]], writes=["xT:%d" % (ti // 4)])
            else:
                S.op("dve", I("tensor_copy", out=dst, in_=srcp),
                     reads=[bn[(ti % 2) * 2 + hb]], writes=["xT:%d" % (ti // 4)])

    S.dma("sp", I("dma_start", out=vst[:], in_=vecs.rearrange("(a r) f -> r a f", r=112)), key="vst", writes=["vst"])
    for a in range(5):
        b = banks[4 + (a % 2)]
        S.op("pe", I("transpose", out=b[:, 0:112], in_=vst[:, a, :], identity=ident[0:112, 0:112]),
             reads=["vst", "ident"], writes=[bn[4 + (a % 2)]])
        S.op("dve", I("tensor_copy", out=vecT[:, a * 112:(a + 1) * 112], in_=b[:, 0:112]),
             reads=[bn[4 + (a % 2)]], writes=["vecT"])
    S.dma("sp", I("dma_start", out=cst[:], in_=cc), key="cst", writes=["cst"])
    for c in range(8):
        S.op("pe", I("transpose", out=banks[6][:, c * NS:(c + 1) * NS], in_=cst[:, c * 128:(c + 1) * 128],
                                              identity=ident[0:NS, 0:NS]), reads=["cst", "ident"], writes=[bn[6]])
    S.op("act", I("activation", out=scT[:].rearrange("p a b -> p (a b)"), in_=banks[6][:, 0:8 * NS], func=AF.Silu),
         reads=[bn[6]], writes=["scT"])

    wq = {"n": 0}

    def xg(g):
        return "xT:%d" % g

    def norm_mod(AR, ia, ib, tag, lazy=False):
        sq = [AR.alloc("sq%d" % i, [128, 512], BF16) for i in range(2)]
        vs = [AR.alloc("v%d" % i, [128, 512], F32) for i in range(2)]
        rstds = [AR.alloc("rstd%d" % i, [128, 512], F32) for i in range(2)]
        tt = [AR.alloc("tt%d" % i, [128, 512], F32) for i in range(2)]

        def n_sq(g):
            t0, n = GROUPS[g]
            pb = 6 + (g % 2)
            for kc in range(8):
                s_ = sq[kc % 2]
                S.op("act", I("activation", out=s_[:, 0:n], in_=xT[:, kc, t0:t0 + n], func=AF.Square),
                     reads=[xg(g)], writes=["sq%d" % (kc % 2)])
                S.op("pe", I("matmul", banks[pb][:, 0:n], lhsT=onesb[:], rhs=s_[:, 0:n], start=(kc == 0), stop=(kc == 7)),
                     reads=["sq%d" % (kc % 2), "onesb"], writes=[bn[pb]])

        def n_rstd(g):
            t0, n = GROUPS[g]
            pb = 6 + (g % 2)
            v, rstd = vs[g % 2], rstds[g % 2]
            S.op("dve", I("tensor_scalar", out=v[:, 0:n], in0=banks[pb][:, 0:n], scalar1=1.0 / D, scalar2=1e-6,
                          op0=ALU.mult, op1=ALU.add), reads=[bn[pb]], writes=["v%d" % (g % 2)])
            S.op("act", I("activation", out=rstd[:, 0:n], in_=v[:, 0:n], func=AF.Sqrt), reads=["v%d" % (g % 2)], writes=["rstd%d" % (g % 2)])
            S.op("dve", I("reciprocal", out=rstd[:, 0:n], in_=rstd[:, 0:n]), writes=["rstd%d" % (g % 2)])

        def n_apply(g):
            t0, n = GROUPS[g]
            rstd = rstds[g % 2]
            rsn = "rstd%d" % (g % 2)
            for kc in range(8):
                t_ = tt[kc % 2]
                S.op("dve", I("tensor_tensor", out=t_[:, 0:n], in0=xT[:, kc, t0:t0 + n], in1=rstd[:, 0:n], op=ALU.mult),
                     reads=[xg(g), rsn], writes=["tt%d" % (kc % 2)])
                if g < 4:
                    S.op("act", I("activation", out=hT[:, kc, t0:t0 + n], in_=t_[:, 0:n], func=AF.Identity,
                                  scale=mod[:, ia, kc, 0:1], bias=mod[:, ib, kc, 0:1]),
                         reads=["tt%d" % (kc % 2), "mod"], writes=["hT:%d" % g])
                else:
                    tv = t_[:, 0:128].rearrange("p (s i) -> p s i", s=16)
                    S.op("dve", I("tensor_tensor", out=tv, in0=tv, in1=bc(mod[:, ia, kc, 1:17].unsqueeze(2), [128, 16, 8]), op=ALU.mult),
                         reads=["mod"], writes=["tt%d" % (kc % 2)])
                    S.op("dve", I("tensor_tensor", out=hT[:, kc, t0:t0 + n].rearrange("p (s i) -> p s i", s=16), in0=tv,
                                  in1=bc(mod[:, ib, kc, 1:17].unsqueeze(2), [128, 16, 8]), op=ALU.add),
                         reads=["mod", "tt%d" % (kc % 2)], writes=["hT:%d" % g])

        def norm_group(g):
            n_sq(g)
            n_rstd(g)
            n_apply(g)

        if lazy:
            return norm_group
        ng = len(GROUPS)
        n_sq(0)
        for g in range(ng):
            if g + 1 < ng:
                n_sq(g + 1)
            n_rstd(g)
            n_apply(g)

    def resid_add(pbank, d, g, ig):
        t0, n = GROUPS[g]
        if g < 4:
            S.op("dve", I("scalar_tensor_tensor", out=xT[:, d, t0:t0 + n], in0=banks[pbank][:, 0:n],
                                                          scalar=mod[:, ig, d, 0:1], in1=xT[:, d, t0:t0 + n],
                                                          op0=ALU.mult, op1=ALU.add),
                 reads=[bn[pbank], "mod", xg(g)], writes=[xg(g)])
        else:
            xv = xT[:, d, t0:t0 + n].rearrange("p (s i) -> p s i", s=16)
            pv = banks[pbank][:, 0:n].rearrange("p (s i) -> p s i", s=16)
            S.op("dve", I("tensor_tensor", out=pv, in0=pv, in1=bc(mod[:, ig, d, 1:17].unsqueeze(2), [128, 16, 8]),
                                                   op=ALU.mult), reads=[bn[pbank], "mod"], writes=[bn[pbank]])
            S.op("dve", I("tensor_tensor", out=xv, in0=xv, in1=pv, op=ALU.add),
                 reads=[bn[pbank], xg(g)], writes=[xg(g)])

    def ada_layer(l):
        S.barrier()
        S.phase = 'ada%d' % l
        AR = Arena(nc, PH_BASE, SB_END, "ada%d" % l)
        ring = [AR.alloc("w%d" % i, [128, 8, 768], BF16) for i in range(3)]
        adaT = AR.alloc("adaT", [128, 72, NS], F32)
        for blk in range(12):
            slot = wq["n"] % 3
            wq["n"] += 1
            wt = ring[slot]
            src = w_ada[l, :, blk * 768:(blk + 1) * 768].rearrange("(kc p) n -> p kc n", p=128)
            S.dma("pool", I("dma_start", out=wt[:], in_=src), key="wr%d" % slot, writes=["wr%d" % slot])
            pb = 4 + (blk % 2)
            for j6 in range(6):
                for kc in range(8):
                    S.op("pe", I("matmul", banks[pb][:, j6 * NS:(j6 + 1) * NS], lhsT=wt[:, kc, j6 * 128:(j6 + 1) * 128], rhs=scT[:, kc, :],
                        start=(kc == 0), stop=(kc == 7)), reads=["wr%d" % slot, "scT"], writes=[bn[pb]])
            j0 = blk * 6
            S.op("dve", I("tensor_tensor", out=adaT[:, j0:j0 + 6, :], in0=banks[pb][:, 0:6 * NS].rearrange("p (a b) -> p a b", a=6),
                in1=bc(vecT[:, vrow(l, "b_ada", j0):vrow(l, "b_ada", j0) + 6].unsqueeze(2), [128, 6, NS]), op=ALU.add),
                reads=[bn[pb], "vecT"], writes=["adaT"])
        for k, (gname, half) in enumerate((("g_ffn1", 0.5), ("g_mix", 1.0), ("g_ffn2", 0.5))):
            sh = adaT[:, (3 * k) * 8:(3 * k) * 8 + 8, :]
            sc = adaT[:, (3 * k + 1) * 8:(3 * k + 1) * 8 + 8, :]
            gt = adaT[:, (3 * k + 2) * 8:(3 * k + 2) * 8 + 8, :]
            gv = bc(vecT[:, vrow(l, gname):vrow(l, gname) + 8].unsqueeze(2), [128, 8, NS])
            S.op("dve", I("scalar_tensor_tensor", out=mod[:, 3 * k, :, :], in0=sc, scalar=1.0, in1=gv,
                                                                          op0=ALU.add, op1=ALU.mult),
                 reads=["adaT", "vecT"], writes=["mod"])
            S.op("dve", I("tensor_copy", out=mod[:, 3 * k + 1, :, :], in_=sh), reads=["adaT"], writes=["mod"])
            S.op("dve", I("tensor_scalar", out=mod[:, 3 * k + 2, :, :], in0=gt, scalar1=half,
                                                                       scalar2=None, op0=ALU.mult),
                 reads=["adaT"], writes=["mod"])

    def ffn(l, which):
        S.barrier()
        S.phase = 'ffn%d%d' % (l, which)
        AR = Arena(nc, MT_BASE, SB_END, "ffn%d%d" % (l, which))
        FS = 4
        ring = [(AR.alloc("w1_%d" % i, [128, 8, FS * 128], BF16), AR.alloc("w3_%d" % i, [128, 8, FS * 128], BF16),
                 AR.alloc("w2_%d" % i, [128, FS, D], BF16)) for i in range(2)]
        gT = AR.alloc("gT", [128, FS, NT], BF16)
        sl = [AR.alloc("s%d" % i, [128, 512], F32) for i in range(2)]
        k3 = 0 if which == 0 else 2
        ia, ib, ig = 3 * k3, 3 * k3 + 1, 3 * k3 + 2
        W1, W3, W2 = w1[which], w3[which], w2[which]
        stages = [(f0, min(FS, 22 - f0)) for f0 in range(0, 22, FS)]

        def load(si):
            f0, nf = stages[si]
            slot = si % 2
            a, b_, c_ = ring[slot]
            s1 = W1[l, :, f0 * 128:(f0 + nf) * 128].rearrange("(kc p) n -> p kc n", p=128)
            s3 = W3[l, :, f0 * 128:(f0 + nf) * 128].rearrange("(kc p) n -> p kc n", p=128)
            s2 = W2[l, f0 * 128:(f0 + nf) * 128, :].rearrange("(f p) n -> p f n", p=128)
            for dst, src in ((a[:, :, 0:nf * 128], s1), (b_[:, :, 0:nf * 128], s3), (c_[:, 0:nf, :], s2)):
                S.dma("pool", I("dma_start", out=dst, in_=src), key="fw%d" % slot, writes=["fw%d" % slot])

        load(0)
        norm_group = norm_mod(AR, ia, ib, "f", lazy=True)
        norm_group(0)
        load(1)
        cnt = 0
        yc = 0
        for si, (f0, nf) in enumerate(stages):
            slot = si % 2
            a, b_, c_ = ring[slot]
            wn = "fw%d" % slot
            for g, (t0, n) in enumerate(GROUPS):
                if si == 0 and g + 1 < len(GROUPS):
                    norm_group(g + 1)
                for f in range(nf):
                    pu1, pu3 = (cnt % 2) * 2, (cnt % 2) * 2 + 1
                    s_ = sl[cnt % 2]
                    sn = "sl%d" % (cnt % 2)
                    cnt += 1
                    for kc in range(8):
                        S.op("pe", I("matmul", banks[pu1][:, 0:n], lhsT=a[:, kc, f * 128:(f + 1) * 128], rhs=hT[:, kc, t0:t0 + n],
                                     start=(kc == 0), stop=(kc == 7)), reads=[wn, "hT:%d" % g], writes=[bn[pu1]])
                    for kc in range(8):
                        S.op("pe", I("matmul", banks[pu3][:, 0:n], lhsT=b_[:, kc, f * 128:(f + 1) * 128], rhs=hT[:, kc, t0:t0 + n],
                                     start=(kc == 0), stop=(kc == 7)), reads=[wn, "hT:%d" % g], writes=[bn[pu3]])
                    S.op("act", I("activation", out=s_[:, 0:n], in_=banks[pu1][:, 0:n], func=AF.Silu), reads=[bn[pu1]], writes=[sn])
                    S.op("dve", I("tensor_tensor", out=gT[:, f, t0:t0 + n], in0=banks[pu3][:, 0:n], in1=s_[:, 0:n], op=ALU.mult),
                         reads=[bn[pu3], sn], writes=["gT:%d:%d" % (f, g)])
                for d in range(8):
                    pb = 4 + (yc % 4)
                    yc += 1
                    for f in range(nf):
                        S.op("pe", I("matmul", banks[pb][:, 0:n], lhsT=c_[:, f, d * 128:(d + 1) * 128], rhs=gT[:, f, t0:t0 + n],
                                     start=(f == 0), stop=(f == nf - 1)), reads=[wn, "gT:%d:%d" % (f, g)], writes=[bn[pb]])
                    resid_add(pb, d, g, ig)
            if si + 2 < len(stages):
                load(si + 2)

    COL = dict(q=0, k=512, v=1024, g=2048, ca=3072, cb=3584, sb=4096, scc=4608, sx=5120, gl=5632)
    BRS = os.environ.get("BR", "RSC")

    def wcols(l, c0, n):
        return w_in[l, :, c0:c0 + n].rearrange("(kc p) n -> p kc n", p=128)

    def merge(l, b, srcT, nk, Wout, first, inplace, base):
        S.barrier()
        S.phase = 'merge%d%d' % (l, b)
        AR = Arena(nc, base, SB_END, "mg%d%d" % (l, b))
        wo_t = AR.alloc("wo", [128, nk, D], BF16)
        wg_t = AR.alloc("wg", [128, 8, D], BF16)
        sg = [AR.alloc("sg%d" % i, [128, 512], F32) for i in range(2)]
        tmp = AR.alloc("tmp", [128, 512], F32)
        mtmp = AR.alloc("mtmp", [128, 8, 512], BF16) if inplace else None
        for j in range(4):
            S.dma("pool", I("dma_start", out=wo_t[:, :, j * 256:(j + 1) * 256],
                            in_=Wout[l, :, j * 256:(j + 1) * 256].rearrange("(kc p) n -> p kc n", p=128)), key="mwo%d" % j, writes=["mwo%d" % j])
            S.dma("pool", I("dma_start", out=wg_t[:, :, j * 256:(j + 1) * 256], in_=wcols(l, COL["gl"] + b * D + j * 256, 256)),
                  key="mwg%d" % j, writes=["mwg%d" % j])
        cnt = 0
        for g, (t0, n) in enumerate(GROUPS):
            for d in range(8):
                pa, pb = (cnt % 2) * 2, (cnt % 2) * 2 + 1
                sg_ = sg[cnt % 2]
                sgn = "msg%d" % (cnt % 2)
                cnt += 1
                for kc in range(nk):
                    S.op("pe", I("matmul", banks[pa][:, 0:n], lhsT=wo_t[:, kc, d * 128:(d + 1) * 128], rhs=srcT[:, kc, t0:t0 + n],
                                 start=(kc == 0), stop=(kc == nk - 1)), reads=["mwo%d" % (d // 2), "src:%d" % g], writes=[bn[pa]])
                for kc in range(8):
                    S.op("pe", I("matmul", banks[pb][:, 0:n], lhsT=wg_t[:, kc, d * 128:(d + 1) * 128], rhs=hT[:, kc, t0:t0 + n],
                                 start=(kc == 0), stop=(kc == 7)), reads=["mwg%d" % (d // 2), "hT:%d" % g], writes=[bn[pb]])
                br = vrow(l, "b_gate", b * 8 + d)
                S.op("act", I("activation", out=sg_[:, 0:n], in_=banks[pb][:, 0:n], func=AF.Sigmoid, bias=vecT[:, br:br + 1]),
                     reads=[bn[pb], "vecT"], writes=[sgn])
                if inplace:
                    S.op("dve", I("tensor_tensor", out=mtmp[:, d, 0:n], in0=banks[pa][:, 0:n], in1=sg_[:, 0:n], op=ALU.mult),
                         reads=[bn[pa], sgn], writes=["mtmp"])
                elif first:
                    S.op("dve", I("tensor_tensor", out=mT[:, d, t0:t0 + n], in0=banks[pa][:, 0:n], in1=sg_[:, 0:n], op=ALU.mult),
                         reads=[bn[pa], sgn], writes=["m:%d" % g])
                else:
                    S.op("dve", I("tensor_tensor", out=tmp[:, 0:n], in0=banks[pa][:, 0:n], in1=sg_[:, 0:n], op=ALU.mult),
                         reads=[bn[pa], sgn], writes=["mtmpf"])
                    S.op("dve", I("tensor_tensor", out=mT[:, d, t0:t0 + n], in0=mT[:, d, t0:t0 + n], in1=tmp[:, 0:n], op=ALU.add),
                         reads=["mtmpf", "m:%d" % g], writes=["m:%d" % g])
            if inplace:
                S.op("act", I("copy", out=mT[:, :, t0:t0 + n], in_=mtmp[:, :, 0:n]), reads=["mtmp", "src:%d" % g],
                     writes=["m:%d" % g, "src:%d" % g])


    LG = np.log(np.float32(1.0) - np.exp2(-5.0 - np.arange(8, dtype=np.float32))).astype(np.float32)

    def branch_r(l):
        S.barrier()
        S.phase = 'brR%d' % l
        AR = Arena(nc, PH_BASE, SB_END, "br%d" % l)
        WT_OFF = AR.cur
        wt = AR.alloc("wqkvg", [128, 8, 1536], BF16)
        S0all = nc.alloc_sbuf_tensor_at("br%d_S0all" % l, [128, 4, 8, 128], F32, offset=WT_OFF)
        dm = AR.alloc("dm", [128, 4, 128], F32)
        zm = AR.alloc("zm", [128, 8, 128], BF16)
        km = AR.alloc("km", [128, 16], BF16)
        qkd = AR.alloc("qkd", [128, 2, 2, 8], F32)
        rot = [AR.alloc("rot%d" % i, [128, 3, 32], F32) for i in range(2)]
        ta = AR.alloc("ta", [128, 512], F32)
        tb = AR.alloc("tb", [128, 512], F32)
        qkrot = AR.alloc("qkrot", [128, 512], BF16)
        qrot, krot = qkrot[:, 0:256], qkrot[:, 256:512]
        qkdt = [AR.alloc("qkdt%d" % i, [128, 512], BF16) for i in range(2)]
        qd = [t[:, 0:256] for t in qkdt]
        kd = [t[:, 256:512] for t in qkdt]
        nmr = AR.alloc("nmr", [128, 4], F32)
        qT = [AR.alloc("qT%d" % i, [64, 4, 128], BF16) for i in range(2)]
        qdT = [AR.alloc("qdT%d" % i, [64, 4, 128], BF16) for i in range(2)]
        kT = [AR.alloc("kT%d" % i, [64, 4, 128], BF16) for i in range(2)]
        vb = [AR.alloc("vb%d" % i, [128, 512], BF16) for i in range(2)]
        sg = AR.alloc("sg", [128, 512], F32)
        AT = AR.alloc("AT", [128, 4, 128], BF16)
        st6 = AR.alloc("st6", [128, 4, 6], F32)
        mv = AR.alloc("mv", [128, 4, 2], F32)
        rsd = AR.alloc("rsd", [128, 4], F32)
        rrs = [AR.alloc("rr%d" % i, [128, 512], BF16) for i in range(2)]
        Sst = AR.alloc("Sst", [64, 4, 128], F32)
        Sb = AR.alloc("Sb", [64, 4, 128], BF16)
        S0bs = [AR.alloc("S0b%d" % i, [128, 8, 128], BF16) for i in range(2)]
        Zqs = [AR.alloc("Zq%d" % i, [128, 8, 128], BF16) for i in range(2)]
        qdd = AR.alloc("qdd", [128, 2, 64], BF16)
        B3b = banks[3][:].bitcast(BF16)
        B4b = banks[4][:].bitcast(BF16)
        B5b = banks[5][:].bitcast(BF16)
        S.dma("pool", I("dma_start", out=zm[:], in_=zmask_d), key="c_zm", writes=["zm"])
        S.dma("pool", I("dma_start", out=km[:], in_=kmask_d), key="c_km", writes=["km"])
        S.dma("sp", I("dma_start", out=qkd[:], in_=qk_dec_d), key="c_qkd", writes=["qkd"])

        def stage_a(hh, ti, part):
            S.phase = 'brR%d:%d:%02d' % (l, hh, ti)
            g = ti // 4
            c0 = ti * 128
            pi = 0 if ti < 16 else 1
            i2 = ti % 2
            rt = rot[i2]
            rn = "rot%d" % i2
            if part == 1:
              S.dma("sp", I("dma_start", out=rt[:], in_=rot_d[ti]), key=rn, writes=[rn])
              for (pb, o0, n_, c_) in ((0, 0, 256, 0), (0, 256, 256, 256), (1, 512, 512, 0)):
                for kc in range(8):
                    S.op("pe", I("matmul", banks[pb][:, c_:c_ + n_], lhsT=hT[:, kc, c0:c0 + 128], rhs=wt[:, kc, o0:o0 + n_],
                                 start=(kc == 0), stop=(kc == 7)), reads=["rw", "hT:%d" % g], writes=[bn[pb]])
              S.op("act", I("copy", out=vb[i2][:], in_=banks[1][:]), reads=[bn[1]], writes=["vb%d" % i2])
            if part == 2:
                X = banks[0][:]
                X16 = X.rearrange("p (a r) -> p a r", r=32)
                X8 = X.rearrange("p (h t r) -> p h t r", h=8, t=2)
                ta16 = ta[:].rearrange("p (a r) -> p a r", r=32)
                tb8 = tb[:].rearrange("p (h t r) -> p h t r", h=8, t=2)
                S.op("dve", I("tensor_tensor", out=ta16, in0=X16, in1=bc(rt[:, 0, :].unsqueeze(1), [128, 16, 32]), op=ALU.mult),
                     reads=[bn[0], rn], writes=["ta"])
                S.op("dve", I("tensor_tensor", out=tb8[:, :, 0, :], in0=X8[:, :, 1, :], in1=bc(rt[:, 2, :].unsqueeze(1), [128, 8, 32]),
                              op=ALU.mult), reads=[bn[0], rn], writes=["tb"])
                S.op("dve", I("tensor_tensor", out=tb8[:, :, 1, :], in0=X8[:, :, 0, :], in1=bc(rt[:, 1, :].unsqueeze(1), [128, 8, 32]),
                              op=ALU.mult), reads=[bn[0], rn], writes=["tb"])
                S.op("dve", I("tensor_tensor", out=ta[:], in0=ta[:], in1=tb[:], op=ALU.add), reads=["tb"], writes=["ta"])
                S.op("act", I("copy", out=qkrot[:], in_=ta[:]), reads=["ta"], writes=["qkrot"])
                S.op("dve", I("tensor_tensor", out=qkdt[i2][:].rearrange("p (w h d) -> p w h d", w=2, h=4),
                              in0=ta[:].rearrange("p (w h d) -> p w h d", w=2, h=4),
                              in1=bc(qkd[:, pi, :, 4 * hh:4 * hh + 4].unsqueeze(3), [128, 2, 4, 64]), op=ALU.mult),
                     reads=["ta", "qkd"], writes=["qkdt%d" % i2])
            if part != 3:
                return
            for h in range(4):
                S.op("pe", I("transpose", out=B3b[0:64, h * 128:(h + 1) * 128], in_=qkrot[:, h * 64:(h + 1) * 64], identity=identb[:]),
                     reads=["qkrot", "identb"], writes=[bn[3]])
                S.op("pe", I("transpose", out=B3b[0:64, 512 + h * 128:512 + (h + 1) * 128], in_=qkdt[i2][:, h * 64:(h + 1) * 64],
                             identity=identb[:]), reads=["qkdt%d" % i2, "identb"], writes=[bn[3]])
                S.op("pe", I("transpose", out=B4b[0:64, h * 128:(h + 1) * 128], in_=qkrot[:, 256 + h * 64:256 + (h + 1) * 64], identity=identb[:]),
                     reads=["qkrot", "identb"], writes=[bn[4]])
            S.op("act", I("copy", out=qT[i2][:].rearrange("p h t -> p (h t)"), in_=B3b[0:64, 0:512]), reads=[bn[3]], writes=["qT%d" % i2])
            S.op("act", I("copy", out=qdT[i2][:].rearrange("p h t -> p (h t)"), in_=B3b[0:64, 512:1024]), reads=[bn[3]], writes=["qdT%d" % i2])
            S.op("act", I("copy", out=kT[i2][:].rearrange("p h t -> p (h t)"), in_=B4b[0:64, 0:512]), reads=[bn[4]], writes=["kT%d" % i2])

        def stage_b(hh, ti, part):
            S.phase = 'brR%d:%d:%02d' % (l, hh, ti)
            g = ti // 4
            c0 = ti * 128
            pi = 0 if ti < 16 else 1
            i2 = ti % 2
            vbn, qTn, qdTn, kTn, qdn, kdn = ("vb%d" % i2, "qT%d" % i2, "qdT%d" % i2, "kT%d" % i2, "qkdt%d" % i2, "qkdt%d" % i2)
            rr = rrs[i2]
            rrn = "rr%d" % i2
            if part == 1:
              if ti == 16:
                S.dma("sp", I("dma_start", out=dm[:], in_=dmask_d[1, :, 4 * hh:4 * hh + 4, :]), key="c_dm", writes=["dm"])
              for kc in range(8):
                S.op("pe", I("matmul", banks[2][:], lhsT=hT[:, kc, c0:c0 + 128], rhs=wt[:, kc, 1024:1536], start=(kc == 0), stop=(kc == 7)),
                     reads=["rw", "hT:%d" % g], writes=[bn[2]])
              S.op("act", I("activation", out=sg[:], in_=banks[2][:], func=AF.Silu), reads=[bn[2]], writes=["sg"])
              if ti == 16:
                for h in range(4):
                    for s2 in range(2):
                        S.dma("sp", I("dma_start", out=S0all[s2 * 64:(s2 + 1) * 64, h, :, :],
                                      in_=st_ret[l, s2::2, 4 * hh + h, :, :].rearrange("pr d e -> d pr e")), key="S0_%d" % h,
                              reads=([] if (h == 0 and s2 == 0) else ["rw"]), writes=(["S0_%d" % h, "rw"] if (h == 0 and s2 == 0) else ["S0_%d" % h]))
              for h in range(4):
                S.op("pe", I("matmul", banks[5][:, h * 128:(h + 1) * 128], lhsT=kT[i2][:, h, :], rhs=qT[i2][:, h, :], start=True, stop=True),
                     reads=[kTn, qTn], writes=[bn[5]])
              S.op("dve", I("tensor_tensor", out=AT[:].rearrange("p h t -> p (h t)"), in0=banks[5][:],
                          in1=dm[:].rearrange("p h t -> p (h t)"), op=ALU.mult), reads=[bn[5], "dm"], writes=["AT"])
            for h in (range(4) if part == 2 else ()):
                H = 4 * hh + h
                hb = slice(h * 128, (h + 1) * 128)
                if pi == 0:
                    S.op("pe", I("matmul", banks[6][:, hb], lhsT=AT[:, h, :], rhs=vb[i2][:, hb], start=True, stop=(ti == 0)),
                         reads=["AT", vbn], writes=[bn[6]])
                    if ti > 0:
                        S.op("pe", I("matmul", banks[6][:, hb], lhsT=qdT[i2][:, h, :], rhs=Sb[:, h, :], start=False, stop=True),
                             reads=[qdTn, "Sb"], writes=[bn[6]])
                    S.op("pe", I("matmul", banks[7][0:64, hb], lhsT=qkdt[i2][:, 256 + h * 64:256 + (h + 1) * 64], rhs=vb[i2][:, hb], start=True, stop=True),
                         reads=[kdn, vbn], writes=[bn[7]])
                else:
                    S0 = S0all[:, h, :, :]
                    S0n = "S0_%d" % h
                    S0b, S0bn = S0bs[h % 2], "S0b%d" % (h % 2)
                    Zq, Zqn = Zqs[h % 2], "Zq%d" % (h % 2)
                    Zk = Zq[:].rearrange("p (a b) d -> p a (b d)", b=1).rearrange("p a (s d) -> p (a s) d", d=64)
                    S.op("act", I("copy", out=S0b[:], in_=S0), reads=[S0n, "rw"], writes=[S0bn])
                    for t_ in range(2):
                        S.op("dve", I("tensor_copy", out=qdd[:, t_, :], in_=qkdt[i2][:, h * 64:(h + 1) * 64]), reads=[qdn], writes=["qdd"])
                    S.op("pe", I("transpose", out=B4b[:, 512:640], in_=qdd[:].rearrange("p t d -> p (t d)"), identity=identb[:]),
                         reads=["qdd", "identb"], writes=[bn[4]])
                    S.op("dve", I("tensor_tensor", out=Zq[:], in0=bc(B4b[:, 512:640].unsqueeze(1), [128, 8, 128]), in1=zm[:], op=ALU.mult),
                         reads=[bn[4], "zm"], writes=[Zqn])
                    S.op("pe", I("matmul", banks[6][:, hb], lhsT=AT[:, h, :], rhs=vb[i2][:, hb], start=True, stop=False),
                         reads=["AT", vbn], writes=[bn[6]])
                    for pr in range(8):
                        S.op("pe", I("matmul", banks[6][:, hb], lhsT=Zq[:, pr, :], rhs=S0b[:, pr, :], start=False, stop=(pr == 7)),
                             reads=[Zqn, S0bn], writes=[bn[6]])
                    S.op("dve", I("tensor_tensor", out=Zk, in0=bc(qkdt[i2][:, 256 + h * 64:256 + (h + 1) * 64].unsqueeze(1), [128, 16, 64]),
                                  in1=bc(km[:].unsqueeze(2), [128, 16, 64]), op=ALU.mult), reads=[kdn, "km"], writes=[Zqn])
                    for pr in range(8):
                        S.op("pe", I("matmul", banks[pr // 4][:, (pr % 4) * 128:(pr % 4 + 1) * 128], lhsT=Zq[:, pr, :], rhs=vb[i2][:, hb],
                                     start=True, stop=True), reads=[Zqn, vbn], writes=[bn[pr // 4]])
                    cd8 = float(np.exp(np.float32(8.0) * LG[H]))
                    for q_ in range(2):
                        S.op("dve", I("scalar_tensor_tensor", out=S0[:, 4 * q_:4 * q_ + 4, :].rearrange("p a e -> p (a e)"),
                                      in0=S0[:, 4 * q_:4 * q_ + 4, :].rearrange("p a e -> p (a e)"), scalar=cd8, in1=banks[q_][:],
                                      op0=ALU.mult, op1=ALU.add), reads=[bn[q_], "rw"], writes=[S0n])
                    for s2 in range(2):
                        finals.append(S.dma("sp", I("dma_start", out=rets[l, s2::2, H, :, :].rearrange("pr d e -> d pr e"),
                                                    in_=S0all[s2 * 64:(s2 + 1) * 64, h, :, :]), key="o_rets%d" % h, reads=[S0n, "rw"]))
            if pi == 0 and part == 2:
                for h in range(4):
                    H = 4 * hh + h
                    cd = float(np.exp(np.float32(128.0) * LG[H]))
                    S.op("dve", I("scalar_tensor_tensor", out=Sst[:, h, :], in0=Sst[:, h, :], scalar=cd,
                                  in1=banks[7][0:64, h * 128:(h + 1) * 128], op0=ALU.mult, op1=ALU.add),
                         reads=[bn[7]], writes=["Sst"])
                S.op("act", I("copy", out=Sb[:], in_=Sst[:]), reads=["Sst"], writes=["Sb"])
            if part == 4:
                for h in range(4):
                    S.op("pe", I("transpose", out=B4b[:, 512 + h * 128:512 + (h + 1) * 128], in_=rr[:, h * 128:(h + 1) * 128], identity=identb[:]),
                         reads=[rrn, "identb"], writes=[bn[4]])
                for h in range(4):
                    r = vrow(l, "gn", 4 * hh + h)
                    S.op("act", I("activation", out=mT[:, 4 * hh + h, c0:c0 + 128], in_=B4b[:, 512 + h * 128:512 + (h + 1) * 128], func=AF.Identity,
                                  scale=vecT[:, r:r + 1]), reads=[bn[4], "vecT"], writes=["src:%d" % g])
            if part != 3:
                return
            for h in range(4):
                S.op("dve", I("bn_stats", out=st6[:, h, :], in_=banks[6][:, h * 128:(h + 1) * 128]), reads=[bn[6]], writes=["st6"])
            for h in range(4):
                S.op("dve", I("bn_aggr", out=mv[:, h, :], in_=st6[:, h, :]), reads=["st6"], writes=["mv"])
            S.op("dve", I("tensor_scalar", out=rsd[:], in0=mv[:, :, 1], scalar1=1e-5, scalar2=None, op0=ALU.add), reads=["mv"], writes=["rsd"])
            S.op("pool", I("tensor_tensor", out=rsd[:], in0=rsd[:], in1=mhalf[:, 0:4], op=ALU.pow), reads=["mhalf"], writes=["rsd"])
            S.op("dve", I("scalar_tensor_tensor", out=nmr[:], in0=mv[:, :, 0], scalar=-1.0, in1=rsd[:], op0=ALU.mult, op1=ALU.mult),
                 reads=["mv", "rsd"], writes=["nmr"])
            for h in range(4):
                S.op("act", I("activation", out=AT[:, h, :], in_=banks[6][:, h * 128:(h + 1) * 128], func=AF.Identity,
                              scale=rsd[:, h:h + 1], bias=nmr[:, h:h + 1]), reads=[bn[6], "rsd", "nmr"], writes=["AT"])
            S.op("dve", I("tensor_tensor", out=rr[:], in0=AT[:].rearrange("p h t -> p (h t)"), in1=sg[:], op=ALU.mult),
                 reads=["AT", "sg"], writes=[rrn])

        for hh in range(2):
            for j, (c0, n_) in enumerate(((COL["q"] + hh * 256, 256), (COL["k"] + hh * 256, 256), (COL["v"] + hh * 512, 512),
                                          (COL["g"] + hh * 512, 512))):
                o0 = (0, 256, 512, 1024)[j]
                S.dma("pool", I("dma_start", out=wt[:, :, o0:o0 + n_], in_=wcols(l, c0, n_)), key="rw", writes=["rw"])
            S.dma("sp", I("dma_start", out=dm[:], in_=dmask_d[0, :, 4 * hh:4 * hh + 4, :]), key="c_dm", writes=["dm"])
            S.op("pool", I("memset", Sst[:], 0.0), writes=["Sst"])
            S.op("pool", I("memset", Sb[:], 0.0), writes=["Sb"])
            for part in (1, 2, 3):
                stage_a(hh, 0, part)
            for ti in range(17):
                nxt = ti + 1 < 17
                if nxt:
                    stage_a(hh, ti + 1, 1)
                stage_b(hh, ti, 1)
                if nxt:
                    stage_a(hh, ti + 1, 2)
                stage_b(hh, ti, 2)
                if nxt:
                    stage_a(hh, ti + 1, 3)
                if ti > 0:
                    stage_b(hh, ti - 1, 4)
                stage_b(hh, ti, 3)
            stage_b(hh, 16, 4)
            finals.append(S.dma("sp", I("dma_start", out=retp[l, 4 * hh:4 * hh + 4].rearrange("h d e -> d h e"), in_=Sst[:]), key="o_retp",
                                reads=["Sst"]))
        merge(l, 0, mT, 8, w_ret_out, True, True, PH_BASE)

    def branch_s(l, first):
        S.barrier()
        S.phase = 'brS%d' % l
        AR = Arena(nc, PH_BASE, SB_END, "bs%d" % l)
        pST = AR.alloc("pST", [128, 4, NT], BF16)
        S_BASE = AR.cur
        ring = [AR.alloc("w%d" % i, [128, 8, 384], BF16) for i in range(3)]
        uspad = AR.alloc("uspad", [128, 2 + NP], F32)
        us_s = AR.alloc("us_s", [128, 4, 16, 10], F32)
        sxs = [AR.alloc("sx%d" % i, [128, 512], F32) for i in range(2)]
        acc = [AR.alloc("acc%d" % i, [128, 512], F32) for i in range(2)]
        stst = AR.alloc("stst", [32, 512], F32)
        so_p = AR.alloc("so_p", [2, 512], F32)
        ust = AR.alloc("ust", [128, 4, 32], F32)
        so_s = AR.alloc("so_s", [32, 512], F32)
        S.dma("sp", I("dma_start", out=stst[:], in_=st_sc[l].rearrange("s i c -> (s i) c")), key="stst", writes=["stst"])
        for cc in range(4):
            S.op("pe", I("transpose", out=banks[7][:, cc * 32:(cc + 1) * 32], in_=stst[:, cc * 128:(cc + 1) * 128],
                         identity=ident[0:32, 0:32]), reads=["stst", "ident"], writes=[bn[7]])
        S.op("dve", I("tensor_copy", out=us_s[:, :, :, 0:2], in_=banks[7][:, 0:128].rearrange("p (c s i) -> p c s i", c=4, s=16)),
             reads=[bn[7]], writes=["us_s"])
        S.op("pool", I("memset", uspad[:, 0:2], 0.0), writes=["uspad"])
        cnt = 0
        for cc in range(4):
            slot = wq["n"] % 3
            wq["n"] += 1
            wt = ring[slot]
            wn = "wr%d" % slot
            for j, nm in enumerate(("sb", "scc", "sx")):
                S.dma("pool", I("dma_start", out=wt[:, :, j * 128:(j + 1) * 128], in_=wcols(l, COL[nm] + cc * 128, 128)), key=wn, writes=[wn])
            for g, (t0, n) in enumerate(GROUPS):
                i2 = cnt % 2
                cnt += 1
                for j in range(3):
                    for kc in range(8):
                        S.op("pe", I("matmul", banks[j][:, 0:n], lhsT=wt[:, kc, j * 128:(j + 1) * 128], rhs=hT[:, kc, t0:t0 + n],
                                     start=(kc == 0), stop=(kc == 7)), reads=[wn, "hT:%d" % g], writes=[bn[j]])
                S.op("act", I("copy", out=sxs[i2][:, 0:n], in_=banks[2][:, 0:n]), reads=[bn[2]], writes=["sxs%d" % i2])
                if g < 4:
                    S.op("dve", I("tensor_tensor", out=uspad[:, 2 + t0:2 + t0 + n], in0=banks[1][:, 0:n], in1=sxs[i2][:, 0:n], op=ALU.mult),
                         reads=[bn[1], "sxs%d" % i2], writes=["uspad"])
                    taps = [uspad[:, t0 + k:t0 + k + n] for k in range(3)]
                    accv = acc[i2][:, 0:n]
                    pv = banks[0][:, 0:n]
                    ov = pST[:, cc, t0:t0 + n]
                    srcn = "uspad"
                else:
                    S.op("dve", I("tensor_tensor", out=us_s[:, cc, :, 2:10], in0=banks[1][:, 0:n].rearrange("p (s i) -> p s i", s=16),
                                  in1=sxs[i2][:, 0:n].rearrange("p (s i) -> p s i", s=16), op=ALU.mult),
                         reads=[bn[1], "sxs%d" % i2], writes=["us_s"])
                    taps = [us_s[:, cc, :, k:k + 8] for k in range(3)]
                    accv = acc[i2][:, 0:n].rearrange("p (s i) -> p s i", s=16)
                    pv = banks[0][:, 0:n].rearrange("p (s i) -> p s i", s=16)
                    ov = pST[:, cc, t0:t0 + n].rearrange("p (s i) -> p s i", s=16)
                    srcn = "us_s"
                wr = [vrow(l, "scw", k * 4 + cc) for k in range(3)]
                S.op("dve", I("tensor_scalar", out=accv, in0=taps[2], scalar1=vecT[:, wr[2]:wr[2] + 1], scalar2=None, op0=ALU.mult),
                     reads=[srcn, "vecT"], writes=["acc%d" % i2])
                for k in (1, 0):
                    S.op("dve", I("scalar_tensor_tensor", out=accv, in0=taps[k], scalar=vecT[:, wr[k]:wr[k] + 1], in1=accv,
                                  op0=ALU.mult, op1=ALU.add), reads=[srcn, "vecT"], writes=["acc%d" % i2])
                S.op("dve", I("tensor_tensor", out=ov, in0=pv, in1=accv, op=ALU.mult), reads=[bn[0], "acc%d" % i2], writes=["src:%d" % g])
            S.op("pe", I("transpose", out=banks[6][0:2, cc * 128:(cc + 1) * 128], in_=uspad[:, NP:NP + 2], identity=ident[:]),
                 reads=["uspad", "ident"], writes=[bn[6]])
            S.op("act", I("copy", out=ust[:, cc, :].rearrange("p (s i) -> p s i", s=16), in_=us_s[:, cc, :, 8:10]), reads=["us_s"], writes=["ust"])
            S.op("pe", I("transpose", out=banks[7][0:32, cc * 128:(cc + 1) * 128], in_=ust[:, cc, :], identity=ident[:]),
                 reads=["ust", "ident"], writes=[bn[7]])
        S.op("act", I("copy", out=so_p[:], in_=banks[6][0:2, :]), reads=[bn[6]], writes=["so_p"])
        S.op("act", I("copy", out=so_s[:], in_=banks[7][0:32, :]), reads=[bn[7]], writes=["so_s"])
        finals.append(S.dma("sp", I("dma_start", out=scp[l], in_=so_p[:]), key="o_scp", reads=["so_p"]))
        finals.append(S.dma("sp", I("dma_start", out=scs[l].rearrange("s i c -> (s i) c"), in_=so_s[:]), key="o_scs", reads=["so_s"]))
        merge(l, 2, pST, 4, w_sc_out, first, False, S_BASE)

    def branch_c(l, first):
        S.barrier()
        S.phase = 'brC%d' % l
        AR = Arena(nc, PH_BASE, SB_END, "bc%d" % l)
        cvb = AR.alloc("cvb", [128, 4, NT], BF16)
        LN_BASE = AR.cur
        ring = [AR.alloc("w%d" % i, [128, 8, 256], BF16) for i in range(2)]
        upad = AR.alloc("upad", [128, 30 + NP], BF16)
        us_c = AR.alloc("us_c", [128, 4, 16, 38], BF16)
        usf = AR.alloc("usf", [128, 4, 128], F32)
        u32p = AR.alloc("u32p", [128, 4, 32], F32)
        dg = AR.alloc("dg", [128, 31, 128], BF16)
        sgs = [AR.alloc("sg%d" % i, [128, 512], F32) for i in range(2)]
        uf = [AR.alloc("uf%d" % i, [128, 512], F32) for i in range(2)]
        cst4 = [AR.alloc("cst%d" % i, [120, 512], F32) for i in range(2)]
        co_p = AR.alloc("co_p", [32, 512], F32)
        co_s = AR.alloc("co_s", [128, 512], F32)
        for j in range(4):
            cs = cst4[j % 2]
            S.dma("sp", I("dma_start", out=cs[:], in_=st_conf[l, 4 * j:4 * j + 4].rearrange("s r c -> (s r) c")), key="cst4%d" % (j % 2),
                  writes=["cst4%d" % (j % 2)])
            pb = 6 + (j % 2)
            for cc in range(4):
                S.op("pe", I("transpose", out=banks[pb][:, cc * 120:(cc + 1) * 120], in_=cs[:, cc * 128:(cc + 1) * 128],
                             identity=ident[0:120, 0:120]), reads=["cst4%d" % (j % 2), "ident"], writes=[bn[pb]])
            S.op("dve", I("tensor_copy", out=us_c[:, :, 4 * j:4 * j + 4, 0:30],
                          in_=banks[pb][:, 0:480].rearrange("p (c s r) -> p c s r", c=4, s=4)), reads=[bn[pb]], writes=["us_c"])
        finals.append(S.dma("sp", I("dma_start", out=confs[l, :, 0:22, :], in_=st_conf[l, :, 8:30, :]), key="o_cfs0"))
        S.op("pool", I("memset", upad[:, 0:30], 0.0), writes=["upad"])
        cnt = 0
        for cc in range(4):
            slot = cc % 2
            wt = ring[slot]
            wn = "cw%d" % slot
            for j, nm in enumerate(("ca", "cb")):
                S.dma("pool", I("dma_start", out=wt[:, :, j * 128:(j + 1) * 128], in_=wcols(l, COL[nm] + cc * 128, 128)), key=wn, writes=[wn])
            for k in range(31):
                r = vrow(l, "ccw", k * 4 + cc)
                S.op("dve", I("tensor_scalar", out=dg[:, k, :], in0=identb[:], scalar1=vecT[:, r:r + 1], scalar2=None, op0=ALU.mult),
                     reads=["identb", "vecT"], writes=["dg"])
            for g, (t0, n) in enumerate(GROUPS):
                i2 = cnt % 2
                cnt += 1
                for j in range(2):
                    for kc in range(8):
                        S.op("pe", I("matmul", banks[j][:, 0:n], lhsT=wt[:, kc, j * 128:(j + 1) * 128], rhs=hT[:, kc, t0:t0 + n],
                                     start=(kc == 0), stop=(kc == 7)), reads=[wn, "hT:%d" % g], writes=[bn[j]])
                S.op("act", I("activation", out=sgs[i2][:, 0:n], in_=banks[1][:, 0:n], func=AF.Sigmoid), reads=[bn[1]], writes=["csg%d" % i2])
                S.op("dve", I("tensor_tensor", out=uf[i2][:, 0:n], in0=banks[0][:, 0:n], in1=sgs[i2][:, 0:n], op=ALU.mult),
                     reads=[bn[0], "csg%d" % i2], writes=["uf%d" % i2])
                if g < 4:
                    S.op("act", I("copy", out=upad[:, 30 + t0:30 + t0 + n], in_=uf[i2][:, 0:n]), reads=["uf%d" % i2], writes=["upad"])
                    if g == 3:
                        S.op("act", I("copy", out=u32p[:, cc, :], in_=uf[i2][:, 480:512]), reads=["uf%d" % i2], writes=["u32p"])
                else:
                    S.op("act", I("copy", out=us_c[:, cc, :, 30:38], in_=uf[i2][:, 0:n].rearrange("p (s i) -> p s i", s=16)),
                         reads=["uf%d" % i2], writes=["us_c"])
                    S.op("act", I("copy", out=usf[:, cc, :], in_=uf[i2][:, 0:n]), reads=["uf%d" % i2], writes=["usf"])
            r = vrow(l, "ccb", cc)
            for g, (t0, n) in enumerate(GROUPS[:4]):
                pb = 2 + (g % 2)
                for k in range(31):
                    S.op("pe", I("matmul", banks[pb][:, 0:n], lhsT=dg[:, k, :], rhs=upad[:, t0 + k:t0 + k + n], start=(k == 0), stop=(k == 30)),
                         reads=["dg", "upad"], writes=[bn[pb]])
                S.op("act", I("activation", out=cvb[:, cc, t0:t0 + n], in_=banks[pb][:, 0:n], func=AF.Identity, bias=vecT[:, r:r + 1]),
                     reads=[bn[pb], "vecT"], writes=["src:%d" % g])
            flat = us_c[:, cc, :, :].rearrange("p s r -> p (s r)")
            for hf in range(2):
                pb = 2 + hf
                for k in range(31):
                    S.op("pe", I("matmul", banks[pb][:, 0:274], lhsT=dg[:, k, :], rhs=flat[:, hf * 304 + k:hf * 304 + k + 274],
                                 start=(k == 0), stop=(k == 30)), reads=["dg", "us_c"], writes=[bn[pb]])
                S.op("act", I("activation", out=cvb[:, cc, NP + hf * 64:NP + (hf + 1) * 64].rearrange("p (s i) -> p s i", s=8),
                              in_=banks[pb][:, 0:304].rearrange("p (s r) -> p s r", s=8)[:, :, 0:8], func=AF.Identity, bias=vecT[:, r:r + 1]),
                     reads=[bn[pb], "vecT"], writes=["src:4"])
        for cc in range(4):
            S.op("pe", I("transpose", out=banks[6][0:32, cc * 128:(cc + 1) * 128], in_=u32p[:, cc, :], identity=ident[:]),
                 reads=["u32p", "ident"], writes=[bn[6]])
            S.op("pe", I("transpose", out=banks[7][:, cc * 128:(cc + 1) * 128], in_=usf[:, cc, :], identity=ident[:]),
                 reads=["usf", "ident"], writes=[bn[7]])
        S.op("act", I("copy", out=co_p[:], in_=banks[6][0:32, :]), reads=[bn[6]], writes=["co_p"])
        S.op("act", I("copy", out=co_s[:], in_=banks[7][:]), reads=[bn[7]], writes=["co_s"])
        finals.append(S.dma("sp", I("dma_start", out=confp[l], in_=co_p[2:32, :]), key="o_cfp", reads=["co_p"]))
        for s_ in range(16):
            finals.append(S.dma("sp", I("dma_start", out=confs[l, s_, 22:30, :], in_=co_s[s_ * 8:(s_ + 1) * 8, :]), key="o_cfs", reads=["co_s"]))
        S.barrier()
        S.phase = 'brCln%d' % l
        AR = Arena(nc, LN_BASE, SB_END, "bcl%d" % l)
        sqb = [AR.alloc("sqb%d" % i, [128, 512], BF16) for i in range(2)]
        mean = AR.alloc("mean", [128, 512], F32)
        var = AR.alloc("var", [128, 512], F32)
        rs = AR.alloc("rs", [128, 512], F32)
        tt = [AR.alloc("tt%d" % i, [128, 512], F32) for i in range(2)]
        for g, (t0, n) in enumerate(GROUPS):
            for cc in range(4):
                S.op("act", I("activation", out=sqb[cc % 2][:, 0:n], in_=cvb[:, cc, t0:t0 + n], func=AF.Square),
                     reads=["src:%d" % g], writes=["sqb%d" % (cc % 2)])
                S.op("pe", I("matmul", banks[4][:, 0:n], lhsT=onesb[:], rhs=cvb[:, cc, t0:t0 + n], start=(cc == 0), stop=(cc == 3)),
                     reads=["src:%d" % g, "onesb"], writes=[bn[4]])
                S.op("pe", I("matmul", banks[5][:, 0:n], lhsT=onesb[:], rhs=sqb[cc % 2][:, 0:n], start=(cc == 0), stop=(cc == 3)),
                     reads=["sqb%d" % (cc % 2), "onesb"], writes=[bn[5]])
            S.op("dve", I("tensor_scalar", out=mean[:, 0:n], in0=banks[4][:, 0:n], scalar1=1.0 / 512, scalar2=None, op0=ALU.mult),
                 reads=[bn[4]], writes=["mean"])
            S.op("dve", I("tensor_tensor", out=var[:, 0:n], in0=mean[:, 0:n], in1=mean[:, 0:n], op=ALU.mult), reads=["mean"], writes=["var"])
            S.op("dve", I("scalar_tensor_tensor", out=var[:, 0:n], in0=banks[5][:, 0:n], scalar=1.0 / 512, in1=var[:, 0:n],
                          op0=ALU.mult, op1=ALU.subtract), reads=[bn[5]], writes=["var"])
            S.op("dve", I("tensor_scalar", out=var[:, 0:n], in0=var[:, 0:n], scalar1=1e-5, scalar2=None, op0=ALU.add), writes=["var"])
            S.op("act", I("activation", out=rs[:, 0:n], in_=var[:, 0:n], func=AF.Sqrt), reads=["var"], writes=["rs"])
            S.op("dve", I("reciprocal", out=rs[:, 0:n], in_=rs[:, 0:n]), writes=["rs"])
            for cc in range(4):
                t_ = tt[cc % 2]
                S.op("dve", I("tensor_tensor", out=t_[:, 0:n], in0=cvb[:, cc, t0:t0 + n], in1=mean[:, 0:n], op=ALU.subtract),
                     reads=["src:%d" % g, "mean"], writes=["ctt%d" % (cc % 2)])
                S.op("dve", I("tensor_tensor", out=t_[:, 0:n], in0=t_[:, 0:n], in1=rs[:, 0:n], op=ALU.mult), reads=["rs"], writes=["ctt%d" % (cc % 2)])
                rg, rb = vrow(l, "lng", cc), vrow(l, "lnb", cc)
                S.op("act", I("activation", out=cvb[:, cc, t0:t0 + n], in_=t_[:, 0:n], func=AF.Silu, scale=vecT[:, rg:rg + 1],
                              bias=vecT[:, rb:rb + 1]), reads=["ctt%d" % (cc % 2), "vecT"], writes=["src:%d" % g])
        merge(l, 1, cvb, 4, w_conf_out, first, False, LN_BASE)

    def wo_resid(l):
        S.barrier()
        S.phase = 'wo%d' % l
        AR = Arena(nc, PH_BASE, SB_END, "wo%d" % l)
        wt = AR.alloc("wo", [128, 8, D], BF16)
        for j in range(4):
            S.dma("pool", I("dma_start", out=wt[:, :, j * 256:(j + 1) * 256],
                            in_=w_o[l, :, j * 256:(j + 1) * 256].rearrange("(kc p) n -> p kc n", p=128)), key="mwo%d" % j, writes=["mwo%d" % j])
        c = 0
        for g, (t0, n) in enumerate(GROUPS):
            for d in range(8):
                pb = 4 + (c % 4)
                c += 1
                for kc in range(8):
                    S.op("pe", I("matmul", banks[pb][:, 0:n], lhsT=wt[:, kc, d * 128:(d + 1) * 128], rhs=mT[:, kc, t0:t0 + n],
                                 start=(kc == 0), stop=(kc == 7)), reads=["mwo%d" % (d // 2), "m:%d" % g], writes=[bn[pb]])
                resid_add(pb, d, g, 5)

    def mixer(l):
        S.barrier()
        S.phase = 'mixnorm%d' % l
        AR = Arena(nc, PH_BASE, SB_END, "mx%d" % l)
        norm_mod(AR, 3, 4, "m")
        first = True
        if "R" in BRS:
            branch_r(l)
            first = False
        if "S" in BRS:
            branch_s(l, first)
            first = False
        if "C" in BRS:
            branch_c(l, first)
            first = False
        wo_resid(l)

    for l in range(n_layers):
        ada_layer(l)
        ffn(l, 0)
        if mixer_on:
            mixer(l)
        ffn(l, 1)

    S.barrier()
    S.phase = 'final'
    AR = Arena(nc, PH_BASE, SB_END, "fin")
    sq = [AR.alloc("sq%d" % i, [128, 512], BF16) for i in range(2)]
    v = AR.alloc("v", [128, 512], F32)
    rstd = AR.alloc("rstd", [128, 512], F32)
    yT = AR.alloc("yT", [128, 8, 512], F32)
    ost = [AR.alloc("ost%d" % i, [128, D], F32) for i in range(2)]
    gfin = 2 * VR_PER_LAYER
    oc = 0
    for g, (t0, n) in enumerate(GROUPS):
        pb = 6 + (g % 2)
        for kc in range(8):
            s_ = sq[kc % 2]
            S.op("act", I("activation", out=s_[:, 0:n], in_=xT[:, kc, t0:t0 + n], func=AF.Square),
                 reads=[xg(g)], writes=["sq%d" % (kc % 2)])
            S.op("pe", I("matmul", banks[pb][:, 0:n], lhsT=onesb[:], rhs=s_[:, 0:n],
                                                             start=(kc == 0), stop=(kc == 7)),
                 reads=["sq%d" % (kc % 2), "onesb"], writes=[bn[pb]])
        S.op("dve", I("tensor_scalar", out=v[:, 0:n], in0=banks[pb][:, 0:n], scalar1=1.0 / D, scalar2=1e-6,
                                                      op0=ALU.mult, op1=ALU.add), reads=[bn[pb]], writes=["v"])
        S.op("act", I("activation", out=rstd[:, 0:n], in_=v[:, 0:n], func=AF.Sqrt), reads=["v"], writes=["rstd"])
        S.op("dve", I("reciprocal", out=rstd[:, 0:n], in_=rstd[:, 0:n]), writes=["rstd"])
        for kc in range(8):
            S.op("dve", I("scalar_tensor_tensor", out=yT[:, kc, 0:n], in0=xT[:, kc, t0:t0 + n],
                                                               scalar=vecT[:, gfin + kc:gfin + kc + 1], in1=rstd[:, 0:n],
                                                               op0=ALU.mult, op1=ALU.mult),
                 reads=[xg(g), "rstd", "vecT"], writes=["yT"])
        for ti in range(n // 128):
            o_ = ost[oc % 2]
            on = "ost%d" % (oc % 2)
            for hb in range(2):
                pbk = (oc % 2) * 2 + hb
                for c4 in range(4):
                    c = hb * 4 + c4
                    S.op("pe", I("transpose", out=banks[pbk][:, c4 * 128:(c4 + 1) * 128], in_=yT[:, c, ti * 128:(ti + 1) * 128], identity=ident[:]),
                        reads=["yT", "ident"], writes=[bn[pbk]])
                if hb == 0:
                    S.op("act", I("copy", out=o_[:, 0:512], in_=banks[pbk][:]), reads=[bn[pbk]], writes=[on])
                else:
                    S.op("dve", I("tensor_copy", out=o_[:, 512:1024], in_=banks[pbk][:]),
                         reads=[bn[pbk]], writes=[on])
            tg = t0 // 128 + ti
            dst = yp[tg * 128:(tg + 1) * 128, :] if tg < 16 else ys
            finals.append(S.dma("sp", I("dma_start", out=dst, in_=o_[:]), key="o%d" % (oc % 2), reads=[on]))
            oc += 1

    last = {}
    for t in finals:
        last[(t[0], t[1])] = t
    S.emit(final_waits=list(last.values()))
    return nc


def pack_vecs(inp):
    rows = []
    for l in range(L):
        rows.append(inp["b_ada"][l].reshape(72, 128))
        rows.append(inp["g_ffn1"][l].reshape(8, 128))
        rows.append(inp["g_mix"][l].reshape(8, 128))
        rows.append(inp["g_ffn2"][l].reshape(8, 128))
        rows.append(inp["b_gate"][l].reshape(24, 128))
        rows.append(inp["ret_gn_g"][l].reshape(8, 128))
        rows.append(inp["conf_conv_w"][l].reshape(124, 128))
        rows.append(inp["conf_conv_b"][l].reshape(4, 128))
        rows.append(inp["conf_ln_g"][l].reshape(4, 128))
        rows.append(inp["conf_ln_b"][l].reshape(4, 128))
        rows.append(inp["sc_conv_w"][l].reshape(12, 128))
    rows.append(inp["g_final"].reshape(8, 128))
    v = np.ascontiguousarray(np.concatenate(rows, axis=0), dtype=np.float32)
    assert v.shape == (VR_TOTAL, 128)
    return v


_NC_CACHE = {}


def make_consts():
    f32 = np.float32
    lg = np.log(f32(1.0) - np.exp2(-5.0 - np.arange(8, dtype=f32))).astype(f32)
    freqs = (f32(10000.0) ** (-np.arange(32, dtype=f32) / f32(32))).astype(f32)
    p = np.arange(128)
    rot = np.zeros((17, 128, 3, 32), f32)
    for ti in range(17):
        pos = (ti * 128 + p).astype(f32) if ti < 16 else (16384 + (p % 8)).astype(f32)
        ang = (pos[:, None] * freqs[None, :]).astype(f32)
        rot[ti, :, 0] = np.cos(ang)
        rot[ti, :, 1] = np.sin(ang)
        rot[ti, :, 2] = -np.sin(ang)
    dmask = np.zeros((2, 128, 8, 128), f32)
    jj, ii = np.meshgrid(p, p, indexing="ij")
    diff = (ii - jj).astype(f32)
    for h in range(8):
        dec = np.where(diff >= 0, np.exp(np.maximum(diff, 0) * lg[h]), 0.0).astype(f32)
        dmask[0, :, h, :] = dec * f32(0.125)
        d8 = ((ii % 8) - (jj % 8)).astype(f32)
        same = (ii // 8) == (jj // 8)
        dmask[1, :, h, :] = np.where(same & (d8 >= 0), np.exp(np.maximum(d8, 0) * lg[h]), 0.0).astype(f32) * f32(0.125)
    qkdec = np.zeros((128, 2, 2, 8), f32)
    for h in range(8):
        qkdec[:, 0, 0, h] = np.exp((p + 1).astype(f32) * lg[h]) * f32(0.125)
        qkdec[:, 0, 1, h] = np.exp((127 - p).astype(f32) * lg[h])
        qkdec[:, 1, 0, h] = np.exp(((p % 8) + 1).astype(f32) * lg[h]) * f32(0.125)
        qkdec[:, 1, 1, h] = np.exp((7 - (p % 8)).astype(f32) * lg[h])
    zmask = np.zeros((128, 8, 128), f32)
    for pr in range(8):
        for s2 in range(2):
            zmask[s2 * 64:(s2 + 1) * 64, pr, (2 * pr + s2) * 8:(2 * pr + s2 + 1) * 8] = 1.0
    kmask = np.zeros((128, 16), f32)
    kmask[p, p // 8] = 1.0
    return dict(rot=rot, dmask=dmask, qkdec=qkdec, zmask=zmask, kmask=kmask)


def make_in_maps(inp):
    vecs = pack_vecs(inp)
    ident = np.eye(128, dtype=np.float32)
    consts = make_consts()
    maps = []
    for b in range(8):
        m = dict(
            xp=np.ascontiguousarray(inp["x_prompt"][b]),
            xs=np.ascontiguousarray(inp["x_sample"][16 * b:16 * b + 16].reshape(128, D)),
            cc=np.ascontiguousarray(np.concatenate([inp["c_prompt"][b:b + 1], inp["c_sample"][16 * b:16 * b + 16]], axis=0)),
            vecs=vecs, ident=ident,
            w_ada=inp["w_ada"], w1_a=inp["w1_a"], w3_a=inp["w3_a"], w2_a=inp["w2_a"],
            w1_b=inp["w1_b"], w3_b=inp["w3_b"], w2_b=inp["w2_b"],
            w_in=inp["w_in"], w_ret_out=inp["w_ret_out"], w_conf_out=inp["w_conf_out"], w_sc_out=inp["w_sc_out"], w_o=inp["w_o"],
            st_ret=np.ascontiguousarray(inp["state_ret"][:, 16 * b:16 * b + 16]),
            st_conf=np.ascontiguousarray(inp["state_conf"][:, 16 * b:16 * b + 16]),
            st_sc=np.ascontiguousarray(inp["state_sconv"][:, 16 * b:16 * b + 16]),
            **consts,
        )
        maps.append(m)
    return maps


def kernel(**inputs):
    inp = {k: np.asarray(v) for k, v in inputs.items()}
    if "nc" not in _NC_CACHE:
        _NC_CACHE["nc"] = build_program()
    nc = _NC_CACHE["nc"]
    maps = make_in_maps(inp)
    res = run_bass_kernel_spmd(nc, maps, core_ids=list(range(8)))
    r = res.results
    y_prompt = np.stack([r[b]["yp"] for b in range(8)], axis=0)
    y_sample = np.concatenate([r[b]["ys"].reshape(16, 8, D) for b in range(8)], axis=0)
    ret_p = np.stack([r[b]["retp"] for b in range(8)], axis=1)
    ret_s = np.concatenate([r[b]["rets"] for b in range(8)], axis=1)
    conf_p = np.stack([r[b]["confp"] for b in range(8)], axis=1)
    conf_s = np.concatenate([r[b]["confs"] for b in range(8)], axis=1)
    sc_p = np.stack([r[b]["scp"] for b in range(8)], axis=1)
    sc_s = np.concatenate([r[b]["scs"] for b in range(8)], axis=1)
    return (y_prompt, y_sample, ret_p, ret_s, conf_p, conf_s, sc_p, sc_s)
```

```python
import os
import numpy as np
import concourse.bass as bass
import concourse.mybir as mybir
from concourse.bass_utils import run_bass_kernel_spmd

F32 = mybir.dt.float32
BF16 = mybir.dt.bfloat16
AF = mybir.ActivationFunctionType
ALU = mybir.AluOpType

D = 1024
DFF = 2816
NT = 2176
NP = 2048
NS = 17
L = 2
INC = 8704
GROUPS = [(0, 512), (512, 512), (1024, 512), (1536, 512), (2048, 128)]
SB_BASE = 16640
SB_END = 229376
ANNOTATE = bool(int(os.environ.get('ANNOTATE', '0')))

VR_PER_LAYER = 276
VR_TOTAL = 2 * VR_PER_LAYER + 8
VO = dict(b_ada=0, g_ffn1=72, g_mix=80, g_ffn2=88, b_gate=96, gn=120, ccw=128, ccb=252, lng=256, lnb=260, scw=264)


def vrow(l, name, i=0):
    return l * VR_PER_LAYER + VO[name] + i


class Sched:
    ENGS = ("pe", "act", "dve", "pool", "sp")

    def __init__(self, nc):
        self.nc = nc
        self.ops = {e: [] for e in self.ENGS}
        self.cnt = {e: 0 for e in self.ENGS}
        self.last_w = {}
        self.readers = {}
        self.dma_sems = {}
        self.eng_sem = {}
        self.bar = []
        self.phase = 'setup'

    def _deps(self, reads, writes):
        deps = list(self.bar)
        for r in reads:
            t = self.last_w.get(r)
            if t is not None:
                deps.append(t)
        for w in writes:
            t = self.last_w.get(w)
            if t is not None:
                deps.append(t)
            deps.extend(self.readers.get(w, ()))
        return deps

    def _commit(self, tok, reads, writes):
        for r in reads:
            self.readers.setdefault(r, []).append(tok)
        for w in writes:
            self.last_w[w] = tok
            self.readers[w] = []

    @staticmethod
    def _split(reads, writes):
        r2 = [r for r in reads if not r.startswith("ps")]
        w2 = list(writes) + [r for r in reads if r.startswith("ps")]
        return r2, w2

    def op(self, eng, fn, reads=(), writes=()):
        reads, writes = self._split(reads, writes)
        deps = self._deps(reads, writes)
        self.cnt[eng] += 1
        tok = ("c", eng, self.cnt[eng])
        self.ops[eng].append(dict(fn=fn, deps=deps, tok=tok, ph=self.phase))
        self._commit(tok, reads, writes)
        return tok

    def dma(self, eng, fn, key, reads=(), writes=()):
        deps = self._deps(reads, writes)
        if key not in self.dma_sems:
            self.dma_sems[key] = [self.nc.alloc_semaphore("d_" + key), 0]
        ent = self.dma_sems[key]
        ent[1] += 16
        tok = ("d", key, ent[1])
        self.ops[eng].append(dict(fn=fn, deps=deps, tok=tok, ph=self.phase))
        self._commit(tok, reads, writes)
        return tok

    def barrier(self):
        toks = []
        for e in self.ENGS:
            if self.cnt[e]:
                toks.append(("c", e, self.cnt[e]))
        for k, (s, v) in self.dma_sems.items():
            if v:
                toks.append(("d", k, v))
        self.bar = toks

    def emit(self, final_waits=()):
        nc = self.nc
        for e in self.ENGS:
            self.eng_sem[e] = nc.alloc_semaphore("e_" + e)

        def run(ename, eng):
            waited = {}
            for o in self.ops[ename]:
                need = {}
                for t in o["deps"]:
                    if t[0] == "c":
                        if t[1] == ename and ename == "pe":
                            continue
                        sem = self.eng_sem[t[1]]
                    else:
                        sem = self.dma_sems[t[1]][0]
                    k = sem.num
                    if t[2] > need.get(k, (None, 0))[1]:
                        need[k] = (sem, t[2])
                for k, (sem, v) in need.items():
                    if waited.get(k, 0) >= v:
                        continue
                    eng.wait_ge(sem, v)
                    waited[k] = v
                ins = o["fn"](eng)
                if ANNOTATE:
                    ins.annotate(o["ph"])
                t = o["tok"]
                if t[0] == "c":
                    ins.then_inc(self.eng_sem[ename], 1)
                else:
                    ins.then_inc(self.dma_sems[t[1]][0], 16)
            if ename == "sp":
                for t in final_waits:
                    if t[0] == "c":
                        eng.wait_ge(self.eng_sem[t[1]], t[2])
                    else:
                        eng.wait_ge(self.dma_sems[t[1]][0], t[2])

        with nc.Block() as block:
            @block.tensor
            def _(e):
                run("pe", e)

            @block.scalar
            def _(e):
                run("act", e)

            @block.vector
            def _(e):
                run("dve", e)

            @block.gpsimd
            def _(e):
                run("pool", e)

            @block.sync
            def _(e):
                run("sp", e)


class Arena:
    def __init__(self, nc, base, end, tag):
        self.nc, self.base, self.end, self.tag, self.cur, self.n = nc, base, end, tag, base, 0

    def alloc(self, name, shape, dtype):
        esz = 4 if dtype == F32 else 2
        nbytes = int(np.prod(shape[1:])) * esz
        nbytes = (nbytes + 63) // 64 * 64
        assert self.cur + nbytes <= self.end, (self.tag, name, self.cur + nbytes - self.end)
        t = self.nc.alloc_sbuf_tensor_at("%s_%s_%d" % (self.tag, name, self.n), list(shape), dtype, offset=self.cur)
        self.n += 1
        self.cur += nbytes
        return t


def I(name, *a, **kw):
    return lambda e: getattr(e, name)(*a, **kw)


def bc(ap, shape):
    return ap.to_broadcast(list(shape))


def build_program(mixer_on=True, n_layers=L):
    nc = bass.Bass("TRN2", target_bir_lowering=False)
    S = Sched(nc)

    def din(name, shape, dt=F32):
        return nc.dram_tensor(name, list(shape), dt, kind="ExternalInput").ap()

    def dout(name, shape):
        return nc.dram_tensor(name, list(shape), F32, kind="ExternalOutput").ap()

    xp = din("xp", [NP, D])
    xs = din("xs", [128, D])
    cc = din("cc", [NS, D])
    vecs = din("vecs", [VR_TOTAL, 128])
    ident_d = din("ident", [128, 128])
    w_ada = din("w_ada", [L, D, 9 * D])
    w1 = [din("w1_a", [L, D, DFF]), din("w1_b", [L, D, DFF])]
    w3 = [din("w3_a", [L, D, DFF]), din("w3_b", [L, D, DFF])]
    w2 = [din("w2_a", [L, DFF, D]), din("w2_b", [L, DFF, D])]
    w_in = din("w_in", [L, D, INC])
    w_ret_out = din("w_ret_out", [L, D, D])
    w_conf_out = din("w_conf_out", [L, 512, D])
    w_sc_out = din("w_sc_out", [L, 512, D])
    w_o = din("w_o", [L, D, D])
    st_ret = din("st_ret", [L, 16, 8, 64, 128])
    st_conf = din("st_conf", [L, 16, 30, 512])
    st_sc = din("st_sc", [L, 16, 2, 512])
    rot_d = din("rot", [17, 128, 3, 32])
    dmask_d = din("dmask", [2, 128, 8, 128])
    qk_dec_d = din("qkdec", [128, 2, 2, 8])
    zmask_d = din("zmask", [128, 8, 128])
    kmask_d = din("kmask", [128, 16])
    yp = dout("yp", [NP, D])
    ys = dout("ys", [128, D])
    retp = dout("retp", [L, 8, 64, 128])
    rets = dout("rets", [L, 16, 8, 64, 128])
    confp = dout("confp", [L, 30, 512])
    confs = dout("confs", [L, 16, 30, 512])
    scp = dout("scp", [L, 2, 512])
    scs = dout("scs", [L, 16, 2, 512])
    finals = []

    P = Arena(nc, SB_BASE, SB_END, "P")
    xT = P.alloc("xT", [128, 8, NT], F32)
    hT = P.alloc("hT", [128, 8, NT], BF16)
    mod = P.alloc("mod", [128, 9, 8, NS], F32)
    vecT = P.alloc("vecT", [128, VR_TOTAL], F32)
    ident = P.alloc("ident", [128, 128], F32)
    identb = P.alloc("identb", [128, 128], BF16)
    onesb = P.alloc("onesb", [128, 128], BF16)
    scT = P.alloc("scT", [128, 8, NS], BF16)
    mhalf = P.alloc("mhalf", [128, 512], F32)
    MT_BASE = P.cur
    mT = P.alloc("mT", [128, 8, NT], BF16)
    PH_BASE = P.cur

    banks = [nc.alloc_psum_tensor("bank%d" % i, [128, 512], F32) for i in range(8)]
    bn = ["ps%d" % i for i in range(8)]

    S.dma("sp", I("dma_start", out=ident[:], in_=ident_d), key="c_id", writes=["ident"])
    S.op("act", I("copy", out=identb[:], in_=ident[:]), reads=["ident"], writes=["identb"])
    S.op("pool", I("memset", onesb[:], 1.0), writes=["onesb"])
    S.op("pool", I("memset", mhalf[:], -0.5), writes=["mhalf"])

    A0 = Arena(nc, PH_BASE, SB_END, "A0")
    xstage = [A0.alloc("xst%d" % i, [128, D], F32) for i in range(2)]
    vst = A0.alloc("vst", [112, 5, 128], F32)
    cst = A0.alloc("cst", [NS, D], F32)
    cT = A0.alloc("cT", [128, 8, NS], F32)

    for ti in range(17):
        st = xstage[ti % 2]
        src = xp[ti * 128:(ti + 1) * 128, :] if ti < 16 else xs
        S.dma("sp", I("dma_start", out=st[:], in_=src), key="xst%d" % (ti % 2),
              writes=["xst%d" % (ti % 2)])
        for hb in range(2):
            b = banks[(ti % 2) * 2 + hb]
            for c4 in range(4):
                c = hb * 4 + c4
                S.op("pe", I("transpose", out=b[:, c4 * 128:(c4 + 1) * 128], in_=st[:, c * 128:(c + 1) * 128], identity=ident[:]),
                    reads=["xst%d" % (ti % 2), "ident"], writes=[bn[(ti % 2) * 2 + hb]])
            eng = "act" if hb == 0 else "dve"
            dst = xT[:, hb * 4:hb * 4 + 4, ti * 128:(ti + 1) * 128]
            srcp = b[:].rearrange("p (a b) -> p a b", a=4)
            if eng == "act":
                S.op("act", I("copy", out=dst, in_=srcp),
                     reads=[bn[(ti % 2) * 2 + hb]], writes=["xT:%d" % (ti // 4)])
            else:
                S.op("dve", I("tensor_copy", out=dst, in_=srcp),
                     reads=[bn[(ti % 2) * 2 + hb]], writes=["xT:%d" % (ti // 4)])

    S.dma("sp", I("dma_start", out=vst[:], in_=vecs.rearrange("(a r) f -> r a f", r=112)), key="vst", writes=["vst"])
    for a in range(5):
        b = banks[4 + (a % 2)]
        S.op("pe", I("transpose", out=b[:, 0:112], in_=vst[:, a, :], identity=ident[0:112, 0:112]),
             reads=["vst", "ident"], writes=[bn[4 + (a % 2)]])
        S.op("dve", I("tensor_copy", out=vecT[:, a * 112:(a + 1) * 112], in_=b[:, 0:112]),
             reads=[bn[4 + (a % 2)]], writes=["vecT"])
    S.dma("sp", I("dma_start", out=cst[:], in_=cc), key="cst", writes=["cst"])
    for c in range(8):
        S.op("pe", I("transpose", out=banks[6][:, c * NS:(c + 1) * NS], in_=cst[:, c * 128:(c + 1) * 128],
                                              identity=ident[0:NS, 0:NS]), reads=["cst", "ident"], writes=[bn[6]])
    S.op("act", I("activation", out=scT[:].rearrange("p a b -> p (a b)"), in_=banks[6][:, 0:8 * NS], func=AF.Silu),
         reads=[bn[6]], writes=["scT"])

    wq = {"n": 0}

    def xg(g):
        return "xT:%d" % g

    def norm_mod(AR, ia, ib, tag, lazy=False):
        sq = [AR.alloc("sq%d" % i, [128, 512], BF16) for i in range(2)]
        vs = [AR.alloc("v%d" % i, [128, 512], F32) for i in range(2)]
        rstds = [AR.alloc("rstd%d" % i, [128, 512], F32) for i in range(2)]
        tt = [AR.alloc("tt%d" % i, [128, 512], F32) for i in range(2)]

        def n_sq(g):
            t0, n = GROUPS[g]
            pb = 6 + (g % 2)
            for kc in range(8):
                s_ = sq[kc % 2]
                S.op("act", I("activation", out=s_[:, 0:n], in_=xT[:, kc, t0:t0 + n], func=AF.Square),
                     reads=[xg(g)], writes=["sq%d" % (kc % 2)])
                S.op("pe", I("matmul", banks[pb][:, 0:n], lhsT=onesb[:], rhs=s_[:, 0:n], start=(kc == 0), stop=(kc == 7)),
                     reads=["sq%d" % (kc % 2), "onesb"], writes=[bn[pb]])

        def n_rstd(g):
            t0, n = GROUPS[g]
            pb = 6 + (g % 2)
            v, rstd = vs[g % 2], rstds[g % 2]
            S.op("dve", I("tensor_scalar", out=v[:, 0:n], in0=banks[pb][:, 0:n], scalar1=1.0 / D, scalar2=1e-6,
                          op0=ALU.mult, op1=ALU.add), reads=[bn[pb]], writes=["v%d" % (g % 2)])
            S.op("act", I("activation", out=rstd[:, 0:n], in_=v[:, 0:n], func=AF.Sqrt), reads=["v%d" % (g % 2)], writes=["rstd%d" % (g % 2)])
            S.op("dve", I("reciprocal", out=rstd[:, 0:n], in_=rstd[:, 0:n]), writes=["rstd%d" % (g % 2)])

        def n_apply(g):
            t0, n = GROUPS[g]
            rstd = rstds[g % 2]
            rsn = "rstd%d" % (g % 2)
            for kc in range(8):
                t_ = tt[kc % 2]
                S.op("dve", I("tensor_tensor", out=t_[:, 0:n], in0=xT[:, kc, t0:t0 + n], in1=rstd[:, 0:n], op=ALU.mult),
                     reads=[xg(g), rsn], writes=["tt%d" % (kc % 2)])
                if g < 4:
                    S.op("act", I("activation", out=hT[:, kc, t0:t0 + n], in_=t_[:, 0:n], func=AF.Identity,
                                  scale=mod[:, ia, kc, 0:1], bias=mod[:, ib, kc, 0:1]),
                         reads=["tt%d" % (kc % 2), "mod"], writes=["hT:%d" % g])
                else:
                    tv = t_[:, 0:128].rearrange("p (s i) -> p s i", s=16)
                    S.op("dve", I("tensor_tensor", out=tv, in0=tv, in1=bc(mod[:, ia, kc, 1:17].unsqueeze(2), [128, 16, 8]), op=ALU.mult),
                         reads=["mod"], writes=["tt%d" % (kc % 2)])
                    S.op("dve", I("tensor_tensor", out=hT[:, kc, t0:t0 + n].rearrange("p (s i) -> p s i", s=16), in0=tv,
                                  in1=bc(mod[:, ib, kc, 1:17].unsqueeze(2), [128, 16, 8]), op=ALU.add),
                         reads=["mod", "tt%d" % (kc % 2)], writes=["hT:%d" % g])

        def norm_group(g):
            n_sq(g)
            n_rstd(g)
            n_apply(g)

        if lazy:
            return norm_group
        ng = len(GROUPS)
        n_sq(0)
        for g in range(ng):
            if g + 1 < ng:
                n_sq(g + 1)
            n_rstd(g)
            n_apply(g)

    def resid_add(pbank, d, g, ig):
        t0, n = GROUPS[g]
        if g < 4:
            S.op("dve", I("scalar_tensor_tensor", out=xT[:, d, t0:t0 + n], in0=banks[pbank][:, 0:n],
                                                          scalar=mod[:, ig, d, 0:1], in1=xT[:, d, t0:t0 + n],
                                                          op0=ALU.mult, op1=ALU.add),
                 reads=[bn[pbank], "mod", xg(g)], writes=[xg(g)])
        else:
            xv = xT[:, d, t0:t0 + n].rearrange("p (s i) -> p s i", s=16)
            pv = banks[pbank][:, 0:n].rearrange("p (s i) -> p s i", s=16)
            S.op("dve", I("tensor_tensor", out=pv, in0=pv, in1=bc(mod[:, ig, d, 1:17].unsqueeze(2), [128, 16, 8]),
                                                   op=ALU.mult), reads=[bn[pbank], "mod"], writes=[bn[pbank]])
            S.op("dve", I("tensor_tensor", out=xv, in0=xv, in1=pv, op=ALU.add),
                 reads=[bn[pbank], xg(g)], writes=[xg(g)])

    def ada_layer(l):
        S.barrier()
        S.phase = 'ada%d' % l
        AR = Arena(nc, PH_BASE, SB_END, "ada%d" % l)
        ring = [AR.alloc("w%d" % i, [128, 8, 768], BF16) for i in range(3)]
        adaT = AR.alloc("adaT", [128, 72, NS], F32)
        for blk in range(12):
            slot = wq["n"] % 3
            wq["n"] += 1
            wt = ring[slot]
            src = w_ada[l, :, blk * 768:(blk + 1) * 768].rearrange("(kc p) n -> p kc n", p=128)
            S.dma("pool", I("dma_start", out=wt[:], in_=src), key="wr%d" % slot, writes=["wr%d" % slot])
            pb = 4 + (blk % 2)
            for j6 in range(6):
                for kc in range(8):
                    S.op("pe", I("matmul", banks[pb][:, j6 * NS:(j6 + 1) * NS], lhsT=wt[:, kc, j6 * 128:(j6 + 1) * 128], rhs=scT[:, kc, :],
                        start=(kc == 0), stop=(kc == 7)), reads=["wr%d" % slot, "scT"], writes=[bn[pb]])
            j0 = blk * 6
            S.op("dve", I("tensor_tensor", out=adaT[:, j0:j0 + 6, :], in0=banks[pb][:, 0:6 * NS].rearrange("p (a b) -> p a b", a=6),
                in1=bc(vecT[:, vrow(l, "b_ada", j0):vrow(l, "b_ada", j0) + 6].unsqueeze(2), [128, 6, NS]), op=ALU.add),
                reads=[bn[pb], "vecT"], writes=["adaT"])
        for k, (gname, half) in enumerate((("g_ffn1", 0.5), ("g_mix", 1.0), ("g_ffn2", 0.5))):
            sh = adaT[:, (3 * k) * 8:(3 * k) * 8 + 8, :]
            sc = adaT[:, (3 * k + 1) * 8:(3 * k + 1) * 8 + 8, :]
            gt = adaT[:, (3 * k + 2) * 8:(3 * k + 2) * 8 + 8, :]
            gv = bc(vecT[:, vrow(l, gname):vrow(l, gname) + 8].unsqueeze(2), [128, 8, NS])
            S.op("dve", I("scalar_tensor_tensor", out=mod[:, 3 * k, :, :], in0=sc, scalar=1.0, in1=gv,
                                                                          op0=ALU.add, op1=ALU.mult),
                 reads=["adaT", "vecT"], writes=["mod"])
            S.op("dve", I("tensor_copy", out=mod[:, 3 * k + 1, :, :], in_=sh), reads=["adaT"], writes=["mod"])
            S.op("dve", I("tensor_scalar", out=mod[:, 3 * k + 2, :, :], in0=gt, scalar1=half,
                                                                       scalar2=None, op0=ALU.mult),
                 reads=["adaT"], writes=["mod"])

    def ffn(l, which):
        S.barrier()
        S.phase = 'ffn%d%d' % (l, which)
        AR = Arena(nc, MT_BASE, SB_END, "ffn%d%d" % (l, which))
        FS = 4
        ring = [(AR.alloc("w1_%d" % i, [128, 8, FS * 128], BF16), AR.alloc("w3_%d" % i, [128, 8, FS * 128], BF16),
                 AR.alloc("w2_%d" % i, [128, FS, D], BF16)) for i in range(2)]
        gT = AR.alloc("gT", [128, FS, NT], BF16)
        sl = [AR.alloc("s%d" % i, [128, 512], F32) for i in range(2)]
        k3 = 0 if which == 0 else 2
        ia, ib, ig = 3 * k3, 3 * k3 + 1, 3 * k3 + 2
        W1, W3, W2 = w1[which], w3[which], w2[which]
        stages = [(f0, min(FS, 22 - f0)) for f0 in range(0, 22, FS)]

        def load(si):
            f0, nf = stages[si]
            slot = si % 2
            a, b_, c_ = ring[slot]
            s1 = W1[l, :, f0 * 128:(f0 + nf) * 128].rearrange("(kc p) n -> p kc n", p=128)
            s3 = W3[l, :, f0 * 128:(f0 + nf) * 128].rearrange("(kc p) n -> p kc n", p=128)
            s2 = W2[l, f0 * 128:(f0 + nf) * 128, :].rearrange("(f p) n -> p f n", p=128)
            for dst, src in ((a[:, :, 0:nf * 128], s1), (b_[:, :, 0:nf * 128], s3), (c_[:, 0:nf, :], s2)):
                S.dma("pool", I("dma_start", out=dst, in_=src), key="fw%d" % slot, writes=["fw%d" % slot])

        load(0)
        norm_group = norm_mod(AR, ia, ib, "f", lazy=True)
        norm_group(0)
        load(1)
        cnt = 0
        yc = 0
        for si, (f0, nf) in enumerate(stages):
            slot = si % 2
            a, b_, c_ = ring[slot]
            wn = "fw%d" % slot
            for g, (t0, n) in enumerate(GROUPS):
                if si == 0 and g + 1 < len(GROUPS):
                    norm_group(g + 1)
                for f in range(nf):
                    pu1, pu3 = (cnt % 2) * 2, (cnt % 2) * 2 + 1
                    s_ = sl[cnt % 2]
                    sn = "sl%d" % (cnt % 2)
                    cnt += 1
                    for kc in range(8):
                        S.op("pe", I("matmul", banks[pu1][:, 0:n], lhsT=a[:, kc, f * 128:(f + 1) * 128], rhs=hT[:, kc, t0:t0 + n],
                                     start=(kc == 0), stop=(kc == 7)), reads=[wn, "hT:%d" % g], writes=[bn[pu1]])
                    for kc in range(8):
                        S.op("pe", I("matmul", banks[pu3][:, 0:n], lhsT=b_[:, kc, f * 128:(f + 1) * 128], rhs=hT[:, kc, t0:t0 + n],
                                     start=(kc == 0), stop=(kc == 7)), reads=[wn, "hT:%d" % g], writes=[bn[pu3]])
                    S.op("act", I("activation", out=s_[:, 0:n], in_=banks[pu1][:, 0:n], func=AF.Silu), reads=[bn[pu1]], writes=[sn])
                    S.op("dve", I("tensor_tensor", out=gT[:, f, t0:t0 + n], in0=banks[pu3][:, 0:n], in1=s_[:, 0:n], op=ALU.mult),
                         reads=[bn[pu3], sn], writes=["gT:%d:%d" % (f, g)])
                for d in range(8):
                    pb = 4 + (yc % 4)
                    yc += 1
                    for f in range(nf):
                        S.op("pe", I("matmul", banks[pb][:, 0:n], lhsT=c_[:, f, d * 128:(d + 1) * 128], rhs=gT[:, f, t0:t0 + n],
                                     start=(f == 0), stop=(f == nf - 1)), reads=[wn, "gT:%d:%d" % (f, g)], writes=[bn[pb]])
                    resid_add(pb, d, g, ig)
            if si + 2 < len(stages):
                load(si + 2)

    COL = dict(q=0, k=512, v=1024, g=2048, ca=3072, cb=3584, sb=4096, scc=4608, sx=5120, gl=5632)
    BRS = os.environ.get("BR", "RSC")

    def wcols(l, c0, n):
        return w_in[l, :, c0:c0 + n].rearrange("(kc p) n -> p kc n", p=128)

    def merge(l, b, srcT, nk, Wout, first, inplace, base):
        S.barrier()
        S.phase = 'merge%d%d' % (l, b)
        AR = Arena(nc, base, SB_END, "mg%d%d" % (l, b))
        wo_t = AR.alloc("wo", [128, nk, D], BF16)
        wg_t = AR.alloc("wg", [128, 8, D], BF16)
        sg = [AR.alloc("sg%d" % i, [128, 512], F32) for i in range(2)]
        tmp = AR.alloc("tmp", [128, 512], F32)
        mtmp = AR.alloc("mtmp", [128, 8, 512], BF16) if inplace else None
        for j in range(4):
            S.dma("pool", I("dma_start", out=wo_t[:, :, j * 256:(j + 1) * 256],
                            in_=Wout[l, :, j * 256:(j + 1) * 256].rearrange("(kc p) n -> p kc n", p=128)), key="mwo%d" % j, writes=["mwo%d" % j])
            S.dma("pool", I("dma_start", out=wg_t[:, :, j * 256:(j + 1) * 256], in_=wcols(l, COL["gl"] + b * D + j * 256, 256)),
                  key="mwg%d" % j, writes=["mwg%d" % j])
        cnt = 0
        for g, (t0, n) in enumerate(GROUPS):
            for d in range(8):
                pa, pb = (cnt % 2) * 2, (cnt % 2) * 2 + 1
                sg_ = sg[cnt % 2]
                sgn = "msg%d" % (cnt % 2)
                cnt += 1
                for kc in range(nk):
                    S.op("pe", I("matmul", banks[pa][:, 0:n], lhsT=wo_t[:, kc, d * 128:(d + 1) * 128], rhs=srcT[:, kc, t0:t0 + n],
                                 start=(kc == 0), stop=(kc == nk - 1)), reads=["mwo%d" % (d // 2), "src:%d" % g], writes=[bn[pa]])
                for kc in range(8):
                    S.op("pe", I("matmul", banks[pb][:, 0:n], lhsT=wg_t[:, kc, d * 128:(d + 1) * 128], rhs=hT[:, kc, t0:t0 + n],
                                 start=(kc == 0), stop=(kc == 7)), reads=["mwg%d" % (d // 2), "hT:%d" % g], writes=[bn[pb]])
                br = vrow(l, "b_gate", b * 8 + d)
                S.op("act", I("activation", out=sg_[:, 0:n], in_=banks[pb][:, 0:n], func=AF.Sigmoid, bias=vecT[:, br:br + 1]),
                     reads=[bn[pb], "vecT"], writes=[sgn])
                if inplace:
                    S.op("dve", I("tensor_tensor", out=mtmp[:, d, 0:n], in0=banks[pa][:, 0:n], in1=sg_[:, 0:n], op=ALU.mult),
                         reads=[bn[pa], sgn], writes=["mtmp"])
                elif first:
                    S.op("dve", I("tensor_tensor", out=mT[:, d, t0:t0 + n], in0=banks[pa][:, 0:n], in1=sg_[:, 0:n], op=ALU.mult),
                         reads=[bn[pa], sgn], writes=["m:%d" % g])
                else:
                    S.op("dve", I("tensor_tensor", out=tmp[:, 0:n], in0=banks[pa][:, 0:n], in1=sg_[:, 0:n], op=ALU.mult),
                         reads=[bn[pa], sgn], writes=["mtmpf"])
                    S.op("dve", I("tensor_tensor", out=mT[:, d, t0:t0 + n], in0=mT[:, d, t0:t0 + n], in1=tmp[:, 0:n], op=ALU.add),
                         reads=["mtmpf", "m:%d" % g], writes=["m:%d" % g])
            if inplace:
                S.op("act", I("copy", out=mT[:, :, t0:t0 + n], in_=mtmp[:, :, 0:n]), reads=["mtmp", "src:%d" % g],
                     writes=["m:%d" % g, "src:%d" % g])


    LG = np.log(np.float32(1.0) - np.exp2(-5.0 - np.arange(8, dtype=np.float32))).astype(np.float32)

    def branch_r(l):
        S.barrier()
        S.phase = 'brR%d' % l
        AR = Arena(nc, PH_BASE, SB_END, "br%d" % l)
        WT_OFF = AR.cur
        wt = AR.alloc("wqkvg", [128, 8, 1536], BF16)
        S0all = nc.alloc_sbuf_tensor_at("br%d_S0all" % l, [128, 4, 8, 128], F32, offset=WT_OFF)
        dm = AR.alloc("dm", [128, 4, 128], F32)
        zm = AR.alloc("zm", [128, 8, 128], BF16)
        km = AR.alloc("km", [128, 16], BF16)
        qkd = AR.alloc("qkd", [128, 2, 2, 8], F32)
        rot = [AR.alloc("rot%d" % i, [128, 3, 32], F32) for i in range(2)]
        ta = AR.alloc("ta", [128, 512], F32)
        tb = AR.alloc("tb", [128, 512], F32)
        qkrot = AR.alloc("qkrot", [128, 512], BF16)
        qrot, krot = qkrot[:, 0:256], qkrot[:, 256:512]
        qkdt = [AR.alloc("qkdt%d" % i, [128, 512], BF16) for i in range(2)]
        qd = [t[:, 0:256] for t in qkdt]
        kd = [t[:, 256:512] for t in qkdt]
        nmr = AR.alloc("nmr", [128, 4], F32)
        qT = [AR.alloc("qT%d" % i, [64, 4, 128], BF16) for i in range(2)]
        qdT = [AR.alloc("qdT%d" % i, [64, 4, 128], BF16) for i in range(2)]
        kT = [AR.alloc("kT%d" % i, [64, 4, 128], BF16) for i in range(2)]
        vb = [AR.alloc("vb%d" % i, [128, 512], BF16) for i in range(2)]
        sg = AR.alloc("sg", [128, 512], F32)
        AT = AR.alloc("AT", [128, 4, 128], BF16)
        st6 = AR.alloc("st6", [128, 4, 6], F32)
        mv = AR.alloc("mv", [128, 4, 2], F32)
        rsd = AR.alloc("rsd", [128, 4], F32)
        rrs = [AR.alloc("rr%d" % i, [128, 512], BF16) for i in range(2)]
        Sst = AR.alloc("Sst", [64, 4, 128], F32)
        Sb = AR.alloc("Sb", [64, 4, 128], BF16)
        S0bs = [AR.alloc("S0b%d" % i, [128, 8, 128], BF16) for i in range(2)]
        Zqs = [AR.alloc("Zq%d" % i, [128, 8, 128], BF16) for i in range(2)]
        qdd = AR.alloc("qdd", [128, 2, 64], BF16)
        B3b = banks[3][:].bitcast(BF16)
        B4b = banks[4][:].bitcast(BF16)
        B5b = banks[5][:].bitcast(BF16)
        S.dma("pool", I("dma_start", out=zm[:], in_=zmask_d), key="c_zm", writes=["zm"])
        S.dma("pool", I("dma_start", out=km[:], in_=kmask_d), key="c_km", writes=["km"])
        S.dma("sp", I("dma_start", out=qkd[:], in_=qk_dec_d), key="c_qkd", writes=["qkd"])

        def stage_a(hh, ti, part):
            S.phase = 'brR%d:%d:%02d' % (l, hh, ti)
            g = ti // 4
            c0 = ti * 128
            pi = 0 if ti < 16 else 1
            i2 = ti % 2
            rt = rot[i2]
            rn = "rot%d" % i2
            if part == 1:
              S.dma("sp", I("dma_start", out=rt[:], in_=rot_d[ti]), key=rn, writes=[rn])
              for (pb, o0, n_, c_) in ((0, 0, 256, 0), (0, 256, 256, 256), (1, 512, 512, 0)):
                for kc in range(8):
                    S.op("pe", I("matmul", banks[pb][:, c_:c_ + n_], lhsT=hT[:, kc, c0:c0 + 128], rhs=wt[:, kc, o0:o0 + n_],
                                 start=(kc == 0), stop=(kc == 7)), reads=["rw", "hT:%d" % g], writes=[bn[pb]])
              S.op("act", I("copy", out=vb[i2][:], in_=banks[1][:]), reads=[bn[1]], writes=["vb%d" % i2])
            if part == 2:
                X = banks[0][:]
                X16 = X.rearrange("p (a r) -> p a r", r=32)
                X8 = X.rearrange("p (h t r) -> p h t r", h=8, t=2)
                ta16 = ta[:].rearrange("p (a r) -> p a r", r=32)
                tb8 = tb[:].rearrange("p (h t r) -> p h t r", h=8, t=2)
                S.op("dve", I("tensor_tensor", out=ta16, in0=X16, in1=bc(rt[:, 0, :].unsqueeze(1), [128, 16, 32]), op=ALU.mult),
                     reads=[bn[0], rn], writes=["ta"])
                S.op("dve", I("tensor_tensor", out=tb8[:, :, 0, :], in0=X8[:, :, 1, :], in1=bc(rt[:, 2, :].unsqueeze(1), [128, 8, 32]),
                              op=ALU.mult), reads=[bn[0], rn], writes=["tb"])
                S.op("dve", I("tensor_tensor", out=tb8[:, :, 1, :], in0=X8[:, :, 0, :], in1=bc(rt[:, 1, :].unsqueeze(1), [128, 8, 32]),
                              op=ALU.mult), reads=[bn[0], rn], writes=["tb"])
                S.op("dve", I("tensor_tensor", out=ta[:], in0=ta[:], in1=tb[:], op=ALU.add), reads=["tb"], writes=["ta"])
                S.op("act", I("copy", out=qkrot[:], in_=ta[:]), reads=["ta"], writes=["qkrot"])
                S.op("dve", I("tensor_tensor", out=qkdt[i2][:].rearrange("p (w h d) -> p w h d", w=2, h=4),
                              in0=ta[:].rearrange("p (w h d) -> p w h d", w=2, h=4),
                              in1=bc(qkd[:, pi, :, 4 * hh:4 * hh + 4].unsqueeze(3), [128, 2, 4, 64]), op=ALU.mult),
                     reads=["ta", "qkd"], writes=["qkdt%d" % i2])
            if part != 3:
                return
            for h in range(4):
                S.op("pe", I("transpose", out=B3b[0:64, h * 128:(h + 1) * 128], in_=qkrot[:, h * 64:(h + 1) * 64], identity=identb[:]),
                     reads=["qkrot", "identb"], writes=[bn[3]])
                S.op("pe", I("transpose", out=B3b[0:64, 512 + h * 128:512 + (h + 1) * 128], in_=qkdt[i2][:, h * 64:(h + 1) * 64],
                             identity=identb[:]), reads=["qkdt%d" % i2, "identb"], writes=[bn[3]])
                S.op("pe", I("transpose", out=B4b[0:64, h * 128:(h + 1) * 128], in_=qkrot[:, 256 + h * 64:256 + (h + 1) * 64], identity=identb[:]),
                     reads=["qkrot", "identb"], writes=[bn[4]])
            S.op("act", I("copy", out=qT[i2][:].rearrange("p h t -> p (h t)"), in_=B3b[0:64, 0:512]), reads=[bn[3]], writes=["qT%d" % i2])
            S.op("act", I("copy", out=qdT[i2][:].rearrange("p h t -> p (h t)"), in_=B3b[0:64, 512:1024]), reads=[bn[3]], writes=["qdT%d" % i2])
            S.op("act", I("copy", out=kT[i2][:].rearrange("p h t -> p (h t)"), in_=B4b[0:64, 0:512]), reads=[bn[4]], writes=["kT%d" % i2])

        def stage_b(hh, ti, part):
            S.phase = 'brR%d:%d:%02d' % (l, hh, ti)
            g = ti // 4
            c0 = ti * 128
            pi = 0 if ti < 16 else 1
            i2 = ti % 2
            vbn, qTn, qdTn, kTn, qdn, kdn = ("vb%d" % i2, "qT%d" % i2, "qdT%d" % i2, "kT%d" % i2, "qkdt%d" % i2, "qkdt%d" % i2)
            rr = rrs[i2]
            rrn = "rr%d" % i2
            if part == 1:
              if ti == 16:
                S.dma("sp", I("dma_start", out=dm[:], in_=dmask_d[1, :, 4 * hh:4 * hh + 4, :]), key="c_dm", writes=["dm"])
              for kc in range(8):
                S.op("pe", I("matmul", banks[2][:], lhsT=hT[:, kc, c0:c0 + 128], rhs=wt[:, kc, 1024:1536], start=(kc == 0), stop=(kc == 7)),
                     reads=["rw", "hT:%d" % g], writes=[bn[2]])
              S.op("act", I("activation", out=sg[:], in_=banks[2][:], func=AF.Silu), reads=[bn[2]], writes=["sg"])
              if ti == 16:
                for h in range(4):
                    for s2 in range(2):
                        S.dma("sp", I("dma_start", out=S0all[s2 * 64:(s2 + 1) * 64, h, :, :],
                                      in_=st_ret[l, s2::2, 4 * hh + h, :, :].rearrange("pr d e -> d pr e")), key="S0_%d" % h,
                              reads=([] if (h == 0 and s2 == 0) else ["rw"]), writes=(["S0_%d" % h, "rw"] if (h == 0 and s2 == 0) else ["S0_%d" % h]))
              for h in range(4):
                S.op("pe", I("matmul", banks[5][:, h * 128:(h + 1) * 128], lhsT=kT[i2][:, h, :], rhs=qT[i2][:, h, :], start=True, stop=True),
                     reads=[kTn, qTn], writes=[bn[5]])
              S.op("dve", I("tensor_tensor", out=AT[:].rearrange("p h t -> p (h t)"), in0=banks[5][:],
                          in1=dm[:].rearrange("p h t -> p (h t)"), op=ALU.mult), reads=[bn[5], "dm"], writes=["AT"])
            for h in (range(4) if part == 2 else ()):
                H = 4 * hh + h
                hb = slice(h * 128, (h + 1) * 128)
                if pi == 0:
                    S.op("pe", I("matmul", banks[6][:, hb], lhsT=AT[:, h, :], rhs=vb[i2][:, hb], start=True, stop=(ti == 0)),
                         reads=["AT", vbn], writes=[bn[6]])
                    if ti > 0:
                        S.op("pe", I("matmul", banks[6][:, hb], lhsT=qdT[i2][:, h, :], rhs=Sb[:, h, :], start=False, stop=True),
                             reads=[qdTn, "Sb"], writes=[bn[6]])
                    S.op("pe", I("matmul", banks[7][0:64, hb], lhsT=qkdt[i2][:, 256 + h * 64:256 + (h + 1) * 64], rhs=vb[i2][:, hb], start=True, stop=True),
                         reads=[kdn, vbn], writes=[bn[7]])
                else:
                    S0 = S0all[:, h, :, :]
                    S0n = "S0_%d" % h
                    S0b, S0bn = S0bs[h % 2], "S0b%d" % (h % 2)
                    Zq, Zqn = Zqs[h % 2], "Zq%d" % (h % 2)
                    Zk = Zq[:].rearrange("p (a b) d -> p a (b d)", b=1).rearrange("p a (s d) -> p (a s) d", d=64)
                    S.op("act", I("copy", out=S0b[:], in_=S0), reads=[S0n, "rw"], writes=[S0bn])
                    for t_ in range(2):
                        S.op("dve", I("tensor_copy", out=qdd[:, t_, :], in_=qkdt[i2][:, h * 64:(h + 1) * 64]), reads=[qdn], writes=["qdd"])
                    S.op("pe", I("transpose", out=B4b[:, 512:640], in_=qdd[:].rearrange("p t d -> p (t d)"), identity=identb[:]),
                         reads=["qdd", "identb"], writes=[bn[4]])
                    S.op("dve", I("tensor_tensor", out=Zq[:], in0=bc(B4b[:, 512:640].unsqueeze(1), [128, 8, 128]), in1=zm[:], op=ALU.mult),
                         reads=[bn[4], "zm"], writes=[Zqn])
                    S.op("pe", I("matmul", banks[6][:, hb], lhsT=AT[:, h, :], rhs=vb[i2][:, hb], start=True, stop=False),
                         reads=["AT", vbn], writes=[bn[6]])
                    for pr in range(8):
                        S.op("pe", I("matmul", banks[6][:, hb], lhsT=Zq[:, pr, :], rhs=S0b[:, pr, :], start=False, stop=(pr == 7)),
                             reads=[Zqn, S0bn], writes=[bn[6]])
                    S.op("dve", I("tensor_tensor", out=Zk, in0=bc(qkdt[i2][:, 256 + h * 64:256 + (h + 1) * 64].unsqueeze(1), [128, 16, 64]),
                                  in1=bc(km[:].unsqueeze(2), [128, 16, 64]), op=ALU.mult), reads=[kdn, "km"], writes=[Zqn])
                    for pr in range(8):
                        S.op("pe", I("matmul", banks[pr // 4][:, (pr % 4) * 128:(pr % 4 + 1) * 128], lhsT=Zq[:, pr, :], rhs=vb[i2][:, hb],
                                     start=True, stop=True), reads=[Zqn, vbn], writes=[bn[pr // 4]])
                    cd8 = float(np.exp(np.float32(8.0) * LG[H]))
                    for q_ in range(2):
                        S.op("dve", I("scalar_tensor_tensor", out=S0[:, 4 * q_:4 * q_ + 4, :].rearrange("p a e -> p (a e)"),
                                      in0=S0[:, 4 * q_:4 * q_ + 4, :].rearrange("p a e -> p (a e)"), scalar=cd8, in1=banks[q_][:],
                                      op0=ALU.mult, op1=ALU.add), reads=[bn[q_], "rw"], writes=[S0n])
                    for s2 in range(2):
                        finals.append(S.dma("sp", I("dma_start", out=rets[l, s2::2, H, :, :].rearrange("pr d e -> d pr e"),
                                                    in_=S0all[s2 * 64:(s2 + 1) * 64, h, :, :]), key="o_rets%d" % h, reads=[S0n, "rw"]))
            if pi == 0 and part == 2:
                for h in range(4):
                    H = 4 * hh + h
                    cd = float(np.exp(np.float32(128.0) * LG[H]))
                    S.op("dve", I("scalar_tensor_tensor", out=Sst[:, h, :], in0=Sst[:, h, :], scalar=cd,
                                  in1=banks[7][0:64, h * 128:(h + 1) * 128], op0=ALU.mult, op1=ALU.add),
                         reads=[bn[7]], writes=["Sst"])
                S.op("act", I("copy", out=Sb[:], in_=Sst[:]), reads=["Sst"], writes=["Sb"])
            if part == 4:
                for h in range(4):
                    S.op("pe", I("transpose", out=B4b[:, 512 + h * 128:512 + (h + 1) * 128], in_=rr[:, h * 128:(h + 1) * 128], identity=identb[:]),
                         reads=[rrn, "identb"], writes=[bn[4]])
                for h in range(4):
                    r = vrow(l, "gn", 4 * hh + h)
                    S.op("act", I("activation", out=mT[:, 4 * hh + h, c0:c0 + 128], in_=B4b[:, 512 + h * 128:512 + (h + 1) * 128], func=AF.Identity,
                                  scale=vecT[:, r:r + 1]), reads=[bn[4], "vecT"], writes=["src:%d" % g])
            if part != 3:
                return
            for h in range(4):
                S.op("dve", I("bn_stats", out=st6[:, h, :], in_=banks[6][:, h * 128:(h + 1) * 128]), reads=[bn[6]], writes=["st6"])
            for h in range(4):
                S.op("dve", I("bn_aggr", out=mv[:, h, :], in_=st6[:, h, :]), reads=["st6"], writes=["mv"])
            S.op("dve", I("tensor_scalar", out=rsd[:], in0=mv[:, :, 1], scalar1=1e-5, scalar2=None, op0=ALU.add), reads=["mv"], writes=["rsd"])
            S.op("pool", I("tensor_tensor", out=rsd[:], in0=rsd[:], in1=mhalf[:, 0:4], op=ALU.pow), reads=["mhalf"], writes=["rsd"])
            S.op("dve", I("scalar_tensor_tensor", out=nmr[:], in0=mv[:, :, 0], scalar=-1.0, in1=rsd[:], op0=ALU.mult, op1=ALU.mult),
                 reads=["mv", "rsd"], writes=["nmr"])
            for h in range(4):
                S.op("act", I("activation", out=AT[:, h, :], in_=banks[6][:, h * 128:(h + 1) * 128], func=AF.Identity,
                              scale=rsd[:, h:h + 1], bias=nmr[:, h:h + 1]), reads=[bn[6], "rsd", "nmr"], writes=["AT"])
            S.op("dve", I("tensor_tensor", out=rr[:], in0=AT[:].rearrange("p h t -> p (h t)"), in1=sg[:], op=ALU.mult),
                 reads=["AT", "sg"], writes=[rrn])

        for hh in range(2):
            for j, (c0, n_) in enumerate(((COL["q"] + hh * 256, 256), (COL["k"] + hh * 256, 256), (COL["v"] + hh * 512, 512),
                                          (COL["g"] + hh * 512, 512))):
                o0 = (0, 256, 512, 1024)[j]
                S.dma("pool", I("dma_start", out=wt[:, :, o0:o0 + n_], in_=wcols(l, c0, n_)), key="rw", writes=["rw"])
            S.dma("sp", I("dma_start", out=dm[:], in_=dmask_d[0, :, 4 * hh:4 * hh + 4, :]), key="c_dm", writes=["dm"])
            S.op("pool", I("memset", Sst[:], 0.0), writes=["Sst"])
            S.op("pool", I("memset", Sb[:], 0.0), writes=["Sb"])
            for part in (1, 2, 3):
                stage_a(hh, 0, part)
            for ti in range(17):
                nxt = ti + 1 < 17
                if nxt:
                    stage_a(hh, ti + 1, 1)
                stage_b(hh, ti, 1)
                if nxt:
                    stage_a(hh, ti + 1, 2)
                stage_b(hh, ti, 2)
                if nxt:
                    stage_a(hh, ti + 1, 3)
                if ti > 0:
                    stage_b(hh, ti - 1, 4)
                stage_b(hh, ti, 3)
            stage_b(hh, 16, 4)
            finals.append(S.dma("sp", I("dma_start", out=retp[l, 4 * hh:4 * hh + 4].rearrange("h d e -> d h e"), in_=Sst[:]), key="o_retp",
                                reads=["Sst"]))
        merge(l, 0, mT, 8, w_ret_out, True, True, PH_BASE)

    def branch_s(l, first):
        S.barrier()
        S.phase = 'brS%d' % l
        AR = Arena(nc, PH_BASE, SB_END, "bs%d" % l)
        pST = AR.alloc("pST", [128, 4, NT], BF16)
        S_BASE = AR.cur
        ring = [AR.alloc("w%d" % i, [128, 8, 384], BF16) for i in range(3)]
        uspad = AR.alloc("uspad", [128, 2 + NP], F32)
        us_s = AR.alloc("us_s", [128, 4, 16, 10], F32)
        sxs = [AR.alloc("sx%d" % i, [128, 512], F32) for i in range(2)]
        acc = [AR.alloc("acc%d" % i, [128, 512], F32) for i in range(2)]
        stst = AR.alloc("stst", [32, 512], F32)
        so_p = AR.alloc("so_p", [2, 512], F32)
        ust = AR.alloc("ust", [128, 4, 32], F32)
        so_s = AR.alloc("so_s", [32, 512], F32)
        S.dma("sp", I("dma_start", out=stst[:], in_=st_sc[l].rearrange("s i c -> (s i) c")), key="stst", writes=["stst"])
        for cc in range(4):
            S.op("pe", I("transpose", out=banks[7][:, cc * 32:(cc + 1) * 32], in_=stst[:, cc * 128:(cc + 1) * 128],
                         identity=ident[0:32, 0:32]), reads=["stst", "ident"], writes=[bn[7]])
        S.op("dve", I("tensor_copy", out=us_s[:, :, :, 0:2], in_=banks[7][:, 0:128].rearrange("p (c s i) -> p c s i", c=4, s=16)),
             reads=[bn[7]], writes=["us_s"])
        S.op("pool", I("memset", uspad[:, 0:2], 0.0), writes=["uspad"])
        cnt = 0
        for cc in range(4):
            slot = wq["n"] % 3
            wq["n"] += 1
            wt = ring[slot]
            wn = "wr%d" % slot
            for j, nm in enumerate(("sb", "scc", "sx")):
                S.dma("pool", I("dma_start", out=wt[:, :, j * 128:(j + 1) * 128], in_=wcols(l, COL[nm] + cc * 128, 128)), key=wn, writes=[wn])
            for g, (t0, n) in enumerate(GROUPS):
                i2 = cnt % 2
                cnt += 1
                B0, B1, B2 = 3 * i2, 3 * i2 + 1, 3 * i2 + 2
                for j in range(3):
                    for kc in range(8):
                        S.op("pe", I("matmul", banks[3 * i2 + j][:, 0:n], lhsT=wt[:, kc, j * 128:(j + 1) * 128], rhs=hT[:, kc, t0:t0 + n],
                                     start=(kc == 0), stop=(kc == 7)), reads=[wn, "hT:%d" % g], writes=[bn[3 * i2 + j]])
                S.op("act", I("copy", out=sxs[i2][:, 0:n], in_=banks[B2][:, 0:n]), reads=[bn[B2]], writes=["sxs%d" % i2])
                if g < 4:
                    S.op("dve", I("tensor_tensor", out=uspad[:, 2 + t0:2 + t0 + n], in0=banks[B1][:, 0:n], in1=sxs[i2][:, 0:n], op=ALU.mult),
                         reads=[bn[B1], "sxs%d" % i2], writes=["uspad"])
                    taps = [uspad[:, t0 + k:t0 + k + n] for k in range(3)]
                    accv = acc[i2][:, 0:n]
                    pv = banks[B0][:, 0:n]
                    ov = pST[:, cc, t0:t0 + n]
                    srcn = "uspad"
                else:
                    S.op("dve", I("tensor_tensor", out=us_s[:, cc, :, 2:10], in0=banks[B1][:, 0:n].rearrange("p (s i) -> p s i", s=16),
                                  in1=sxs[i2][:, 0:n].rearrange("p (s i) -> p s i", s=16), op=ALU.mult),
                         reads=[bn[B1], "sxs%d" % i2], writes=["us_s"])
                    taps = [us_s[:, cc, :, k:k + 8] for k in range(3)]
                    accv = acc[i2][:, 0:n].rearrange("p (s i) -> p s i", s=16)
                    pv = banks[B0][:, 0:n].rearrange("p (s i) -> p s i", s=16)
                    ov = pST[:, cc, t0:t0 + n].rearrange("p (s i) -> p s i", s=16)
                    srcn = "us_s"
                wr = [vrow(l, "scw", k * 4 + cc) for k in range(3)]
                S.op("dve", I("tensor_scalar", out=accv, in0=taps[2], scalar1=vecT[:, wr[2]:wr[2] + 1], scalar2=None, op0=ALU.mult),
                     reads=[srcn, "vecT"], writes=["acc%d" % i2])
                for k in (1, 0):
                    S.op("dve", I("scalar_tensor_tensor", out=accv, in0=taps[k], scalar=vecT[:, wr[k]:wr[k] + 1], in1=accv,
                                  op0=ALU.mult, op1=ALU.add), reads=[srcn, "vecT"], writes=["acc%d" % i2])
                S.op("dve", I("tensor_tensor", out=ov, in0=pv, in1=accv, op=ALU.mult), reads=[bn[B0], "acc%d" % i2], writes=["src:%d" % g])
            S.op("pe", I("transpose", out=banks[6][0:2, cc * 128:(cc + 1) * 128], in_=uspad[:, NP:NP + 2], identity=ident[:]),
                 reads=["uspad", "ident"], writes=[bn[6]])
            S.op("act", I("copy", out=ust[:, cc, :].rearrange("p (s i) -> p s i", s=16), in_=us_s[:, cc, :, 8:10]), reads=["us_s"], writes=["ust"])
            S.op("pe", I("transpose", out=banks[7][0:32, cc * 128:(cc + 1) * 128], in_=ust[:, cc, :], identity=ident[:]),
                 reads=["ust", "ident"], writes=[bn[7]])
        S.op("act", I("copy", out=so_p[:], in_=banks[6][0:2, :]), reads=[bn[6]], writes=["so_p"])
        S.op("act", I("copy", out=so_s[:], in_=banks[7][0:32, :]), reads=[bn[7]], writes=["so_s"])
        finals.append(S.dma("sp", I("dma_start", out=scp[l], in_=so_p[:]), key="o_scp", reads=["so_p"]))
        finals.append(S.dma("sp", I("dma_start", out=scs[l].rearrange("s i c -> (s i) c"), in_=so_s[:]), key="o_scs", reads=["so_s"]))
        merge(l, 2, pST, 4, w_sc_out, first, False, S_BASE)

    def branch_c(l, first):
        S.barrier()
        S.phase = 'brC%d' % l
        AR = Arena(nc, PH_BASE, SB_END, "bc%d" % l)
        cvb = AR.alloc("cvb", [128, 4, NT], BF16)
        LN_BASE = AR.cur
        ring = [AR.alloc("w%d" % i, [128, 8, 256], BF16) for i in range(2)]
        upad = AR.alloc("upad", [128, 30 + NP], BF16)
        us_c = AR.alloc("us_c", [128, 4, 16, 38], BF16)
        usf = AR.alloc("usf", [128, 4, 128], F32)
        u32p = AR.alloc("u32p", [128, 4, 32], F32)
        dg = AR.alloc("dg", [128, 31, 128], BF16)
        sgs = [AR.alloc("sg%d" % i, [128, 512], F32) for i in range(2)]
        uf = [AR.alloc("uf%d" % i, [128, 512], F32) for i in range(2)]
        cst4 = [AR.alloc("cst%d" % i, [120, 512], F32) for i in range(2)]
        co_p = AR.alloc("co_p", [32, 512], F32)
        co_s = AR.alloc("co_s", [128, 512], F32)
        for j in range(4):
            cs = cst4[j % 2]
            S.dma("sp", I("dma_start", out=cs[:], in_=st_conf[l, 4 * j:4 * j + 4].rearrange("s r c -> (s r) c")), key="cst4%d" % (j % 2),
                  writes=["cst4%d" % (j % 2)])
            pb = 6 + (j % 2)
            for cc in range(4):
                S.op("pe", I("transpose", out=banks[pb][:, cc * 120:(cc + 1) * 120], in_=cs[:, cc * 128:(cc + 1) * 128],
                             identity=ident[0:120, 0:120]), reads=["cst4%d" % (j % 2), "ident"], writes=[bn[pb]])
            S.op("dve", I("tensor_copy", out=us_c[:, :, 4 * j:4 * j + 4, 0:30],
                          in_=banks[pb][:, 0:480].rearrange("p (c s r) -> p c s r", c=4, s=4)), reads=[bn[pb]], writes=["us_c"])
        finals.append(S.dma("sp", I("dma_start", out=confs[l, :, 0:22, :], in_=st_conf[l, :, 8:30, :]), key="o_cfs0"))
        S.op("pool", I("memset", upad[:, 0:30], 0.0), writes=["upad"])
        cnt = 0
        for cc in range(4):
            slot = cc % 2
            wt = ring[slot]
            wn = "cw%d" % slot
            for j, nm in enumerate(("ca", "cb")):
                S.dma("pool", I("dma_start", out=wt[:, :, j * 128:(j + 1) * 128], in_=wcols(l, COL[nm] + cc * 128, 128)), key=wn, writes=[wn])
            for k in range(31):
                r = vrow(l, "ccw", k * 4 + cc)
                S.op("dve", I("tensor_scalar", out=dg[:, k, :], in0=identb[:], scalar1=vecT[:, r:r + 1], scalar2=None, op0=ALU.mult),
                     reads=["identb", "vecT"], writes=["dg"])
            for g, (t0, n) in enumerate(GROUPS):
                i2 = cnt % 2
                cnt += 1
                Ba, Bb = 4 * i2, 4 * i2 + 1
                for j in range(2):
                    for kc in range(8):
                        S.op("pe", I("matmul", banks[4 * i2 + j][:, 0:n], lhsT=wt[:, kc, j * 128:(j + 1) * 128], rhs=hT[:, kc, t0:t0 + n],
                                     start=(kc == 0), stop=(kc == 7)), reads=[wn, "hT:%d" % g], writes=[bn[4 * i2 + j]])
                S.op("act", I("activation", out=sgs[i2][:, 0:n], in_=banks[Bb][:, 0:n], func=AF.Sigmoid), reads=[bn[Bb]], writes=["csg%d" % i2])
                S.op("dve", I("tensor_tensor", out=uf[i2][:, 0:n], in0=banks[Ba][:, 0:n], in1=sgs[i2][:, 0:n], op=ALU.mult),
                     reads=[bn[Ba], "csg%d" % i2], writes=["uf%d" % i2])
                if g < 4:
                    S.op("act", I("copy", out=upad[:, 30 + t0:30 + t0 + n], in_=uf[i2][:, 0:n]), reads=["uf%d" % i2], writes=["upad"])
                    if g == 3:
                        S.op("act", I("copy", out=u32p[:, cc, :], in_=uf[i2][:, 480:512]), reads=["uf%d" % i2], writes=["u32p"])
                else:
                    S.op("act", I("copy", out=us_c[:, cc, :, 30:38], in_=uf[i2][:, 0:n].rearrange("p (s i) -> p s i", s=16)),
                         reads=["uf%d" % i2], writes=["us_c"])
                    S.op("act", I("copy", out=usf[:, cc, :], in_=uf[i2][:, 0:n]), reads=["uf%d" % i2], writes=["usf"])
            r = vrow(l, "ccb", cc)
            for g, (t0, n) in enumerate(GROUPS[:4]):
                pb = 2 + (g % 2)
                for k in range(31):
                    S.op("pe", I("matmul", banks[pb][:, 0:n], lhsT=dg[:, k, :], rhs=upad[:, t0 + k:t0 + k + n], start=(k == 0), stop=(k == 30)),
                         reads=["dg", "upad"], writes=[bn[pb]])
                S.op("act", I("activation", out=cvb[:, cc, t0:t0 + n], in_=banks[pb][:, 0:n], func=AF.Identity, bias=vecT[:, r:r + 1]),
                     reads=[bn[pb], "vecT"], writes=["src:%d" % g])
            flat = us_c[:, cc, :, :].rearrange("p s r -> p (s r)")
            for hf in range(2):
                pb = 2 + hf
                for k in range(31):
                    S.op("pe", I("matmul", banks[pb][:, 0:274], lhsT=dg[:, k, :], rhs=flat[:, hf * 304 + k:hf * 304 + k + 274],
                                 start=(k == 0), stop=(k == 30)), reads=["dg", "us_c"], writes=[bn[pb]])
                S.op("act", I("activation", out=cvb[:, cc, NP + hf * 64:NP + (hf + 1) * 64].rearrange("p (s i) -> p s i", s=8),
                              in_=banks[pb][:, 0:304].rearrange("p (s r) -> p s r", s=8)[:, :, 0:8], func=AF.Identity, bias=vecT[:, r:r + 1]),
                     reads=[bn[pb], "vecT"], writes=["src:4"])
        for cc in range(4):
            S.op("pe", I("transpose", out=banks[6][0:32, cc * 128:(cc + 1) * 128], in_=u32p[:, cc, :], identity=ident[:]),
                 reads=["u32p", "ident"], writes=[bn[6]])
            S.op("pe", I("transpose", out=banks[7][:, cc * 128:(cc + 1) * 128], in_=usf[:, cc, :], identity=ident[:]),
                 reads=["usf", "ident"], writes=[bn[7]])
        S.op("act", I("copy", out=co_p[:], in_=banks[6][0:32, :]), reads=[bn[6]], writes=["co_p"])
        S.op("act", I("copy", out=co_s[:], in_=banks[7][:]), reads=[bn[7]], writes=["co_s"])
        finals.append(S.dma("sp", I("dma_start", out=confp[l], in_=co_p[2:32, :]), key="o_cfp", reads=["co_p"]))
        for s_ in range(16):
            finals.append(S.dma("sp", I("dma_start", out=confs[l, s_, 22:30, :], in_=co_s[s_ * 8:(s_ + 1) * 8, :]), key="o_cfs", reads=["co_s"]))
        S.barrier()
        S.phase = 'brCln%d' % l
        AR = Arena(nc, LN_BASE, SB_END, "bcl%d" % l)
        sqb = [AR.alloc("sqb%d" % i, [128, 512], BF16) for i in range(2)]
        mean = AR.alloc("mean", [128, 512], F32)
        var = AR.alloc("var", [128, 512], F32)
        rs = AR.alloc("rs", [128, 512], F32)
        tt = [AR.alloc("tt%d" % i, [128, 512], F32) for i in range(2)]
        for g, (t0, n) in enumerate(GROUPS):
            for cc in range(4):
                S.op("act", I("activation", out=sqb[cc % 2][:, 0:n], in_=cvb[:, cc, t0:t0 + n], func=AF.Square),
                     reads=["src:%d" % g], writes=["sqb%d" % (cc % 2)])
                S.op("pe", I("matmul", banks[4][:, 0:n], lhsT=onesb[:], rhs=cvb[:, cc, t0:t0 + n], start=(cc == 0), stop=(cc == 3)),
                     reads=["src:%d" % g, "onesb"], writes=[bn[4]])
                S.op("pe", I("matmul", banks[5][:, 0:n], lhsT=onesb[:], rhs=sqb[cc % 2][:, 0:n], start=(cc == 0), stop=(cc == 3)),
                     reads=["sqb%d" % (cc % 2), "onesb"], writes=[bn[5]])
            S.op("dve", I("tensor_scalar", out=mean[:, 0:n], in0=banks[4][:, 0:n], scalar1=1.0 / 512, scalar2=None, op0=ALU.mult),
                 reads=[bn[4]], writes=["mean"])
            S.op("dve", I("tensor_tensor", out=var[:, 0:n], in0=mean[:, 0:n], in1=mean[:, 0:n], op=ALU.mult), reads=["mean"], writes=["var"])
            S.op("dve", I("scalar_tensor_tensor", out=var[:, 0:n], in0=banks[5][:, 0:n], scalar=1.0 / 512, in1=var[:, 0:n],
                          op0=ALU.mult, op1=ALU.subtract), reads=[bn[5]], writes=["var"])
            S.op("dve", I("tensor_scalar", out=var[:, 0:n], in0=var[:, 0:n], scalar1=1e-5, scalar2=None, op0=ALU.add), writes=["var"])
            S.op("act", I("activation", out=rs[:, 0:n], in_=var[:, 0:n], func=AF.Sqrt), reads=["var"], writes=["rs"])
            S.op("dve", I("reciprocal", out=rs[:, 0:n], in_=rs[:, 0:n]), writes=["rs"])
            for cc in range(4):
                t_ = tt[cc % 2]
                S.op("dve", I("tensor_tensor", out=t_[:, 0:n], in0=cvb[:, cc, t0:t0 + n], in1=mean[:, 0:n], op=ALU.subtract),
                     reads=["src:%d" % g, "mean"], writes=["ctt%d" % (cc % 2)])
                S.op("dve", I("tensor_tensor", out=t_[:, 0:n], in0=t_[:, 0:n], in1=rs[:, 0:n], op=ALU.mult), reads=["rs"], writes=["ctt%d" % (cc % 2)])
                rg, rb = vrow(l, "lng", cc), vrow(l, "lnb", cc)
                S.op("act", I("activation", out=cvb[:, cc, t0:t0 + n], in_=t_[:, 0:n], func=AF.Silu, scale=vecT[:, rg:rg + 1],
                              bias=vecT[:, rb:rb + 1]), reads=["ctt%d" % (cc % 2), "vecT"], writes=["src:%d" % g])
        merge(l, 1, cvb, 4, w_conf_out, first, False, LN_BASE)

    def wo_resid(l):
        S.barrier()
        S.phase = 'wo%d' % l
        AR = Arena(nc, PH_BASE, SB_END, "wo%d" % l)
        wt = AR.alloc("wo", [128, 8, D], BF16)
        for j in range(4):
            S.dma("pool", I("dma_start", out=wt[:, :, j * 256:(j + 1) * 256],
                            in_=w_o[l, :, j * 256:(j + 1) * 256].rearrange("(kc p) n -> p kc n", p=128)), key="mwo%d" % j, writes=["mwo%d" % j])
        c = 0
        for g, (t0, n) in enumerate(GROUPS):
            for d in range(8):
                pb = 4 + (c % 4)
                c += 1
                for kc in range(8):
                    S.op("pe", I("matmul", banks[pb][:, 0:n], lhsT=wt[:, kc, d * 128:(d + 1) * 128], rhs=mT[:, kc, t0:t0 + n],
                                 start=(kc == 0), stop=(kc == 7)), reads=["mwo%d" % (d // 2), "m:%d" % g], writes=[bn[pb]])
                resid_add(pb, d, g, 5)

    def mixer(l):
        S.barrier()
        S.phase = 'mixnorm%d' % l
        AR = Arena(nc, PH_BASE, SB_END, "mx%d" % l)
        norm_mod(AR, 3, 4, "m")
        first = True
        if "R" in BRS:
            branch_r(l)
            first = False
        if "S" in BRS:
            branch_s(l, first)
            first = False
        if "C" in BRS:
            branch_c(l, first)
            first = False
        wo_resid(l)

    for l in range(n_layers):
        ada_layer(l)
        ffn(l, 0)
        if mixer_on:
            mixer(l)
        ffn(l, 1)

    S.barrier()
    S.phase = 'final'
    AR = Arena(nc, PH_BASE, SB_END, "fin")
    sq = [AR.alloc("sq%d" % i, [128, 512], BF16) for i in range(2)]
    v = AR.alloc("v", [128, 512], F32)
    rstd = AR.alloc("rstd", [128, 512], F32)
    yT = AR.alloc("yT", [128, 8, 512], F32)
    ost = [AR.alloc("ost%d" % i, [128, D], F32) for i in range(2)]
    gfin = 2 * VR_PER_LAYER
    oc = 0
    for g, (t0, n) in enumerate(GROUPS):
        pb = 6 + (g % 2)
        for kc in range(8):
            s_ = sq[kc % 2]
            S.op("act", I("activation", out=s_[:, 0:n], in_=xT[:, kc, t0:t0 + n], func=AF.Square),
                 reads=[xg(g)], writes=["sq%d" % (kc % 2)])
            S.op("pe", I("matmul", banks[pb][:, 0:n], lhsT=onesb[:], rhs=s_[:, 0:n],
                                                             start=(kc == 0), stop=(kc == 7)),
                 reads=["sq%d" % (kc % 2), "onesb"], writes=[bn[pb]])
        S.op("dve", I("tensor_scalar", out=v[:, 0:n], in0=banks[pb][:, 0:n], scalar1=1.0 / D, scalar2=1e-6,
                                                      op0=ALU.mult, op1=ALU.add), reads=[bn[pb]], writes=["v"])
        S.op("act", I("activation", out=rstd[:, 0:n], in_=v[:, 0:n], func=AF.Sqrt), reads=["v"], writes=["rstd"])
        S.op("dve", I("reciprocal", out=rstd[:, 0:n], in_=rstd[:, 0:n]), writes=["rstd"])
        for kc in range(8):
            S.op("dve", I("scalar_tensor_tensor", out=yT[:, kc, 0:n], in0=xT[:, kc, t0:t0 + n],
                                                               scalar=vecT[:, gfin + kc:gfin + kc + 1], in1=rstd[:, 0:n],
                                                               op0=ALU.mult, op1=ALU.mult),
                 reads=[xg(g), "rstd", "vecT"], writes=["yT"])
        for ti in range(n // 128):
            o_ = ost[oc % 2]
            on = "ost%d" % (oc % 2)
            for hb in range(2):
                pbk = (oc % 2) * 2 + hb
                for c4 in range(4):
                    c = hb * 4 + c4
                    S.op("pe", I("transpose", out=banks[pbk][:, c4 * 128:(c4 + 1) * 128], in_=yT[:, c, ti * 128:(ti + 1) * 128], identity=ident[:]),
                        reads=["yT", "ident"], writes=[bn[pbk]])
                if hb == 0:
                    S.op("act", I("copy", out=o_[:, 0:512], in_=banks[pbk][:]), reads=[bn[pbk]], writes=[on])
                else:
                    S.op("dve", I("tensor_copy", out=o_[:, 512:1024], in_=banks[pbk][:]),
                         reads=[bn[pbk]], writes=[on])
            tg = t0 // 128 + ti
            dst = yp[tg * 128:(tg + 1) * 128, :] if tg < 16 else ys
            finals.append(S.dma("sp", I("dma_start", out=dst, in_=o_[:]), key="o%d" % (oc % 2), reads=[on]))
            oc += 1

    last = {}
    for t in finals:
        last[(t[0], t[1])] = t
    S.emit(final_waits=list(last.values()))
    return nc


def pack_vecs(inp):
    rows = []
    for l in range(L):
        rows.append(inp["b_ada"][l].reshape(72, 128))
        rows.append(inp["g_ffn1"][l].reshape(8, 128))
        rows.append(inp["g_mix"][l].reshape(8, 128))
        rows.append(inp["g_ffn2"][l].reshape(8, 128))
        rows.append(inp["b_gate"][l].reshape(24, 128))
        rows.append(inp["ret_gn_g"][l].reshape(8, 128))
        rows.append(inp["conf_conv_w"][l].reshape(124, 128))
        rows.append(inp["conf_conv_b"][l].reshape(4, 128))
        rows.append(inp["conf_ln_g"][l].reshape(4, 128))
        rows.append(inp["conf_ln_b"][l].reshape(4, 128))
        rows.append(inp["sc_conv_w"][l].reshape(12, 128))
    rows.append(inp["g_final"].reshape(8, 128))
    v = np.ascontiguousarray(np.concatenate(rows, axis=0), dtype=np.float32)
    assert v.shape == (VR_TOTAL, 128)
    return v


_NC_CACHE = {}


def make_consts():
    f32 = np.float32
    lg = np.log(f32(1.0) - np.exp2(-5.0 - np.arange(8, dtype=f32))).astype(f32)
    freqs = (f32(10000.0) ** (-np.arange(32, dtype=f32) / f32(32))).astype(f32)
    p = np.arange(128)
    rot = np.zeros((17, 128, 3, 32), f32)
    for ti in range(17):
        pos = (ti * 128 + p).astype(f32) if ti < 16 else (16384 + (p % 8)).astype(f32)
        ang = (pos[:, None] * freqs[None, :]).astype(f32)
        rot[ti, :, 0] = np.cos(ang)
        rot[ti, :, 1] = np.sin(ang)
        rot[ti, :, 2] = -np.sin(ang)
    dmask = np.zeros((2, 128, 8, 128), f32)
    jj, ii = np.meshgrid(p, p, indexing="ij")
    diff = (ii - jj).astype(f32)
    for h in range(8):
        dec = np.where(diff >= 0, np.exp(np.maximum(diff, 0) * lg[h]), 0.0).astype(f32)
        dmask[0, :, h, :] = dec * f32(0.125)
        d8 = ((ii % 8) - (jj % 8)).astype(f32)
        same = (ii // 8) == (jj // 8)
        dmask[1, :, h, :] = np.where(same & (d8 >= 0), np.exp(np.maximum(d8, 0) * lg[h]), 0.0).astype(f32) * f32(0.125)
    qkdec = np.zeros((128, 2, 2, 8), f32)
    for h in range(8):
        qkdec[:, 0, 0, h] = np.exp((p + 1).astype(f32) * lg[h]) * f32(0.125)
        qkdec[:, 0, 1, h] = np.exp((127 - p).astype(f32) * lg[h])
        qkdec[:, 1, 0, h] = np.exp(((p % 8) + 1).astype(f32) * lg[h]) * f32(0.125)
        qkdec[:, 1, 1, h] = np.exp((7 - (p % 8)).astype(f32) * lg[h])
    zmask = np.zeros((128, 8, 128), f32)
    for pr in range(8):
        for s2 in range(2):
            zmask[s2 * 64:(s2 + 1) * 64, pr, (2 * pr + s2) * 8:(2 * pr + s2 + 1) * 8] = 1.0
    kmask = np.zeros((128, 16), f32)
    kmask[p, p // 8] = 1.0
    return dict(rot=rot, dmask=dmask, qkdec=qkdec, zmask=zmask, kmask=kmask)


def make_in_maps(inp):
    vecs = pack_vecs(inp)
    ident = np.eye(128, dtype=np.float32)
    consts = make_consts()
    maps = []
    for b in range(8):
        m = dict(
            xp=np.ascontiguousarray(inp["x_prompt"][b]),
            xs=np.ascontiguousarray(inp["x_sample"][16 * b:16 * b + 16].reshape(128, D)),
            cc=np.ascontiguousarray(np.concatenate([inp["c_prompt"][b:b + 1], inp["c_sample"][16 * b:16 * b + 16]], axis=0)),
            vecs=vecs, ident=ident,
            w_ada=inp["w_ada"], w1_a=inp["w1_a"], w3_a=inp["w3_a"], w2_a=inp["w2_a"],
            w1_b=inp["w1_b"], w3_b=inp["w3_b"], w2_b=inp["w2_b"],
            w_in=inp["w_in"], w_ret_out=inp["w_ret_out"], w_conf_out=inp["w_conf_out"], w_sc_out=inp["w_sc_out"], w_o=inp["w_o"],
            st_ret=np.ascontiguousarray(inp["state_ret"][:, 16 * b:16 * b + 16]),
            st_conf=np.ascontiguousarray(inp["state_conf"][:, 16 * b:16 * b + 16]),
            st_sc=np.ascontiguousarray(inp["state_sconv"][:, 16 * b:16 * b + 16]),
            **consts,
        )
        maps.append(m)
    return maps


def kernel(**inputs):
    inp = {k: np.asarray(v) for k, v in inputs.items()}
    if "nc" not in _NC_CACHE:
        _NC_CACHE["nc"] = build_program()
    nc = _NC_CACHE["nc"]
    maps = make_in_maps(inp)
    res = run_bass_kernel_spmd(nc, maps, core_ids=list(range(8)))
    r = res.results
    y_prompt = np.stack([r[b]["yp"] for b in range(8)], axis=0)
    y_sample = np.concatenate([r[b]["ys"].reshape(16, 8, D) for b in range(8)], axis=0)
    ret_p = np.stack([r[b]["retp"] for b in range(8)], axis=1)
    ret_s = np.concatenate([r[b]["rets"] for b in range(8)], axis=1)
    conf_p = np.stack([r[b]["confp"] for b in range(8)], axis=1)
    conf_s = np.concatenate([r[b]["confs"] for b in range(8)], axis=1)
    sc_p = np.stack([r[b]["scp"] for b in range(8)], axis=1)
    sc_s = np.concatenate([r[b]["scs"] for b in range(8)], axis=1)
    return (y_prompt, y_sample, ret_p, ret_s, conf_p, conf_s, sc_p, sc_s)
```

```python
import os
import numpy as np
import concourse.bass as bass
import concourse.mybir as mybir
from concourse.bass_utils import run_bass_kernel_spmd

F32 = mybir.dt.float32
BF16 = mybir.dt.bfloat16
AF = mybir.ActivationFunctionType
ALU = mybir.AluOpType

D = 1024
DFF = 2816
NT = 2176
NP = 2048
NS = 17
L = 2
INC = 8704
GROUPS = [(0, 512), (512, 512), (1024, 512), (1536, 512), (2048, 128)]
SB_BASE = 16640
SB_END = 229376
ANNOTATE = bool(int(os.environ.get('ANNOTATE', '0')))

VR_PER_LAYER = 276
VR_TOTAL = 2 * VR_PER_LAYER + 8
VO = dict(b_ada=0, g_ffn1=72, g_mix=80, g_ffn2=88, b_gate=96, gn=120, ccw=128, ccb=252, lng=256, lnb=260, scw=264)


def vrow(l, name, i=0):
    return l * VR_PER_LAYER + VO[name] + i


class Sched:
    ENGS = ("pe", "act", "dve", "pool", "sp")

    def __init__(self, nc):
        self.nc = nc
        self.ops = {e: [] for e in self.ENGS}
        self.cnt = {e: 0 for e in self.ENGS}
        self.last_w = {}
        self.readers = {}
        self.dma_sems = {}
        self.eng_sem = {}
        self.bar = []
        self.phase = 'setup'

    def _deps(self, reads, writes):
        deps = list(self.bar)
        for r in reads:
            t = self.last_w.get(r)
            if t is not None:
                deps.append(t)
        for w in writes:
            t = self.last_w.get(w)
            if t is not None:
                deps.append(t)
            deps.extend(self.readers.get(w, ()))
        return deps

    def _commit(self, tok, reads, writes):
        for r in reads:
            self.readers.setdefault(r, []).append(tok)
        for w in writes:
            self.last_w[w] = tok
            self.readers[w] = []

    @staticmethod
    def _split(reads, writes):
        r2 = [r for r in reads if not r.startswith("ps")]
        w2 = list(writes) + [r for r in reads if r.startswith("ps")]
        return r2, w2

    def op(self, eng, fn, reads=(), writes=()):
        reads, writes = self._split(reads, writes)
        deps = self._deps(reads, writes)
        self.cnt[eng] += 1
        tok = ("c", eng, self.cnt[eng])
        self.ops[eng].append(dict(fn=fn, deps=deps, tok=tok, ph=self.phase))
        self._commit(tok, reads, writes)
        return tok

    def dma(self, eng, fn, key, reads=(), writes=()):
        deps = self._deps(reads, writes)
        if key not in self.dma_sems:
            self.dma_sems[key] = [self.nc.alloc_semaphore("d_" + key), 0]
        ent = self.dma_sems[key]
        ent[1] += 16
        tok = ("d", key, ent[1])
        self.ops[eng].append(dict(fn=fn, deps=deps, tok=tok, ph=self.phase))
        self._commit(tok, reads, writes)
        return tok

    def barrier(self):
        toks = []
        for e in self.ENGS:
            if self.cnt[e]:
                toks.append(("c", e, self.cnt[e]))
        for k, (s, v) in self.dma_sems.items():
            if v:
                toks.append(("d", k, v))
        self.bar = toks

    def emit(self, final_waits=()):
        nc = self.nc
        for e in self.ENGS:
            self.eng_sem[e] = nc.alloc_semaphore("e_" + e)

        def run(ename, eng):
            waited = {}
            for o in self.ops[ename]:
                need = {}
                for t in o["deps"]:
                    if t[0] == "c":
                        if t[1] == ename and ename == "pe":
                            continue
                        sem = self.eng_sem[t[1]]
                    else:
                        sem = self.dma_sems[t[1]][0]
                    k = sem.num
                    if t[2] > need.get(k, (None, 0))[1]:
                        need[k] = (sem, t[2])
                for k, (sem, v) in need.items():
                    if waited.get(k, 0) >= v:
                        continue
                    eng.wait_ge(sem, v)
                    waited[k] = v
                ins = o["fn"](eng)
                if ANNOTATE:
                    ins.annotate(o["ph"])
                t = o["tok"]
                if t[0] == "c":
                    ins.then_inc(self.eng_sem[ename], 1)
                else:
                    ins.then_inc(self.dma_sems[t[1]][0], 16)
            if ename == "sp":
                for t in final_waits:
                    if t[0] == "c":
                        eng.wait_ge(self.eng_sem[t[1]], t[2])
                    else:
                        eng.wait_ge(self.dma_sems[t[1]][0], t[2])

        with nc.Block() as block:
            @block.tensor
            def _(e):
                run("pe", e)

            @block.scalar
            def _(e):
                run("act", e)

            @block.vector
            def _(e):
                run("dve", e)

            @block.gpsimd
            def _(e):
                run("pool", e)

            @block.sync
            def _(e):
                run("sp", e)


class Arena:
    def __init__(self, nc, base, end, tag):
        self.nc, self.base, self.end, self.tag, self.cur, self.n = nc, base, end, tag, base, 0

    def alloc(self, name, shape, dtype):
        esz = 4 if dtype == F32 else 2
        nbytes = int(np.prod(shape[1:])) * esz
        nbytes = (nbytes + 63) // 64 * 64
        assert self.cur + nbytes <= self.end, (self.tag, name, self.cur + nbytes - self.end)
        t = self.nc.alloc_sbuf_tensor_at("%s_%s_%d" % (self.tag, name, self.n), list(shape), dtype, offset=self.cur)
        self.n += 1
        self.cur += nbytes
        return t


def I(name, *a, **kw):
    return lambda e: getattr(e, name)(*a, **kw)


def bc(ap, shape):
    return ap.to_broadcast(list(shape))


def build_program(mixer_on=True, n_layers=L):
    nc = bass.Bass("TRN2", target_bir_lowering=False)
    S = Sched(nc)

    def din(name, shape, dt=F32):
        return nc.dram_tensor(name, list(shape), dt, kind="ExternalInput").ap()

    def dout(name, shape):
        return nc.dram_tensor(name, list(shape), F32, kind="ExternalOutput").ap()

    xp = din("xp", [NP, D])
    xs = din("xs", [128, D])
    cc = din("cc", [NS, D])
    vecs = din("vecs", [VR_TOTAL, 128])
    ident_d = din("ident", [128, 128])
    w_ada = din("w_ada", [L, D, 9 * D])
    w1 = [din("w1_a", [L, D, DFF]), din("w1_b", [L, D, DFF])]
    w3 = [din("w3_a", [L, D, DFF]), din("w3_b", [L, D, DFF])]
    w2 = [din("w2_a", [L, DFF, D]), din("w2_b", [L, DFF, D])]
    w_in = din("w_in", [L, D, INC])
    w_ret_out = din("w_ret_out", [L, D, D])
    w_conf_out = din("w_conf_out", [L, 512, D])
    w_sc_out = din("w_sc_out", [L, 512, D])
    w_o = din("w_o", [L, D, D])
    st_ret = din("st_ret", [L, 16, 8, 64, 128])
    st_conf = din("st_conf", [L, 16, 30, 512])
    st_sc = din("st_sc", [L, 16, 2, 512])
    rot_d = din("rot", [17, 128, 3, 32])
    dmask_d = din("dmask", [2, 128, 8, 128])
    qk_dec_d = din("qkdec", [128, 2, 2, 8])
    zmask_d = din("zmask", [128, 8, 128])
    kmask_d = din("kmask", [128, 16])
    yp = dout("yp", [NP, D])
    ys = dout("ys", [128, D])
    retp = dout("retp", [L, 8, 64, 128])
    rets = dout("rets", [L, 16, 8, 64, 128])
    confp = dout("confp", [L, 30, 512])
    confs = dout("confs", [L, 16, 30, 512])
    scp = dout("scp", [L, 2, 512])
    scs = dout("scs", [L, 16, 2, 512])
    finals = []

    P = Arena(nc, SB_BASE, SB_END, "P")
    xT = P.alloc("xT", [128, 8, NT], F32)
    hT = P.alloc("hT", [128, 8, NT], BF16)
    mod = P.alloc("mod", [128, 9, 8, NS], F32)
    vecT = P.alloc("vecT", [128, VR_TOTAL], F32)
    ident = P.alloc("ident", [128, 128], F32)
    identb = P.alloc("identb", [128, 128], BF16)
    onesb = P.alloc("onesb", [128, 128], BF16)
    scT = P.alloc("scT", [128, 8, NS], BF16)
    mhalf = P.alloc("mhalf", [128, 512], F32)
    MT_BASE = P.cur
    mT = P.alloc("mT", [128, 8, NT], BF16)
    PH_BASE = P.cur

    banks = [nc.alloc_psum_tensor("bank%d" % i, [128, 512], F32) for i in range(8)]
    bn = ["ps%d" % i for i in range(8)]

    S.dma("sp", I("dma_start", out=ident[:], in_=ident_d), key="c_id", writes=["ident"])
    S.op("act", I("copy", out=identb[:], in_=ident[:]), reads=["ident"], writes=["identb"])
    S.op("pool", I("memset", onesb[:], 1.0), writes=["onesb"])
    S.op("pool", I("memset", mhalf[:], -0.5), writes=["mhalf"])

    A0 = Arena(nc, PH_BASE, SB_END, "A0")
    xstage = [A0.alloc("xst%d" % i, [128, D], F32) for i in range(2)]
    vst = A0.alloc("vst", [112, 5, 128], F32)
    cst = A0.alloc("cst", [NS, D], F32)
    cT = A0.alloc("cT", [128, 8, NS], F32)

    for ti in range(17):
        st = xstage[ti % 2]
        src = xp[ti * 128:(ti + 1) * 128, :] if ti < 16 else xs
        S.dma("sp", I("dma_start", out=st[:], in_=src), key="xst%d" % (ti % 2),
              writes=["xst%d" % (ti % 2)])
        for hb in range(2):
            b = banks[(ti % 2) * 2 + hb]
            for c4 in range(4):
                c = hb * 4 + c4
                S.op("pe", I("transpose", out=b[:, c4 * 128:(c4 + 1) * 128], in_=st[:, c * 128:(c + 1) * 128], identity=ident[:]),
                    reads=["xst%d" % (ti % 2), "ident"], writes=[bn[(ti % 2) * 2 + hb]])
            eng = "act" if hb == 0 else "dve"
            dst = xT[:, hb * 4:hb * 4 + 4, ti * 128:(ti + 1) * 128]
            srcp = b[:].rearrange("p (a b) -> p a b", a=4)
            if eng == "act":
                S.op("act", I("copy", out=dst, in_=srcp),
                     reads=[bn[(ti % 2) * 2 + hb]], writes=["xT:%d" % (ti // 4)])
            else:
                S.op("dve", I("tensor_copy", out=dst, in_=srcp),
                     reads=[bn[(ti % 2) * 2 + hb]], writes=["xT:%d" % (ti // 4)])

    S.dma("sp", I("dma_start", out=vst[:], in_=vecs.rearrange("(a r) f -> r a f", r=112)), key="vst", writes=["vst"])
    for a in range(5):
        b = banks[4 + (a % 2)]
        S.op("pe", I("transpose", out=b[:, 0:112], in_=vst[:, a, :], identity=ident[0:112, 0:112]),
             reads=["vst", "ident"], writes=[bn[4 + (a % 2)]])
        S.op("dve", I("tensor_copy", out=vecT[:, a * 112:(a + 1) * 112], in_=b[:, 0:112]),
             reads=[bn[4 + (a % 2)]], writes=["vecT"])
    S.dma("sp", I("dma_start", out=cst[:], in_=cc), key="cst", writes=["cst"])
    for c in range(8):
        S.op("pe", I("transpose", out=banks[6][:, c * NS:(c + 1) * NS], in_=cst[:, c * 128:(c + 1) * 128],
                                              identity=ident[0:NS, 0:NS]), reads=["cst", "ident"], writes=[bn[6]])
    S.op("act", I("activation", out=scT[:].rearrange("p a b -> p (a b)"), in_=banks[6][:, 0:8 * NS], func=AF.Silu),
         reads=[bn[6]], writes=["scT"])

    wq = {"n": 0}

    def xg(g):
        return "xT:%d" % g

    def norm_mod(AR, ia, ib, tag, lazy=False):
        sq = [AR.alloc("sq%d" % i, [128, 512], BF16) for i in range(2)]
        vs = [AR.alloc("v%d" % i, [128, 512], F32) for i in range(2)]
        rstds = [AR.alloc("rstd%d" % i, [128, 512], F32) for i in range(2)]
        tt = [AR.alloc("tt%d" % i, [128, 512], F32) for i in range(2)]

        def n_sq(g):
            t0, n = GROUPS[g]
            pb = 6 + (g % 2)
            for kc in range(8):
                s_ = sq[kc % 2]
                S.op("act", I("activation", out=s_[:, 0:n], in_=xT[:, kc, t0:t0 + n], func=AF.Square),
                     reads=[xg(g)], writes=["sq%d" % (kc % 2)])
                S.op("pe", I("matmul", banks[pb][:, 0:n], lhsT=onesb[:], rhs=s_[:, 0:n], start=(kc == 0), stop=(kc == 7)),
                     reads=["sq%d" % (kc % 2), "onesb"], writes=[bn[pb]])

        def n_rstd(g):
            t0, n = GROUPS[g]
            pb = 6 + (g % 2)
            v, rstd = vs[g % 2], rstds[g % 2]
            S.op("dve", I("tensor_scalar", out=v[:, 0:n], in0=banks[pb][:, 0:n], scalar1=1.0 / D, scalar2=1e-6,
                          op0=ALU.mult, op1=ALU.add), reads=[bn[pb]], writes=["v%d" % (g % 2)])
            S.op("act", I("activation", out=rstd[:, 0:n], in_=v[:, 0:n], func=AF.Sqrt), reads=["v%d" % (g % 2)], writes=["rstd%d" % (g % 2)])
            S.op("dve", I("reciprocal", out=rstd[:, 0:n], in_=rstd[:, 0:n]), writes=["rstd%d" % (g % 2)])

        def n_apply(g):
            t0, n = GROUPS[g]
            rstd = rstds[g % 2]
            rsn = "rstd%d" % (g % 2)
            for kc in range(8):
                t_ = tt[kc % 2]
                S.op("dve", I("tensor_tensor", out=t_[:, 0:n], in0=xT[:, kc, t0:t0 + n], in1=rstd[:, 0:n], op=ALU.mult),
                     reads=[xg(g), rsn], writes=["tt%d" % (kc % 2)])
                if g < 4:
                    S.op("act", I("activation", out=hT[:, kc, t0:t0 + n], in_=t_[:, 0:n], func=AF.Identity,
                                  scale=mod[:, ia, kc, 0:1], bias=mod[:, ib, kc, 0:1]),
                         reads=["tt%d" % (kc % 2), "mod"], writes=["hT:%d" % g])
                else:
                    tv = t_[:, 0:128].rearrange("p (s i) -> p s i", s=16)
                    S.op("dve", I("tensor_tensor", out=tv, in0=tv, in1=bc(mod[:, ia, kc, 1:17].unsqueeze(2), [128, 16, 8]), op=ALU.mult),
                         reads=["mod"], writes=["tt%d" % (kc % 2)])
                    S.op("dve", I("tensor_tensor", out=hT[:, kc, t0:t0 + n].rearrange("p (s i) -> p s i", s=16), in0=tv,
                                  in1=bc(mod[:, ib, kc, 1:17].unsqueeze(2), [128, 16, 8]), op=ALU.add),
                         reads=["mod", "tt%d" % (kc % 2)], writes=["hT:%d" % g])

        def norm_group(g):
            n_sq(g)
            n_rstd(g)
            n_apply(g)

        if lazy:
            return norm_group
        ng = len(GROUPS)
        n_sq(0)
        for g in range(ng):
            n_rstd(g)
            if g + 1 < ng:
                n_sq(g + 1)
            n_apply(g)

    def resid_add(pbank, d, g, ig):
        t0, n = GROUPS[g]
        if g < 4:
            S.op("dve", I("scalar_tensor_tensor", out=xT[:, d, t0:t0 + n], in0=banks[pbank][:, 0:n],
                                                          scalar=mod[:, ig, d, 0:1], in1=xT[:, d, t0:t0 + n],
                                                          op0=ALU.mult, op1=ALU.add),
                 reads=[bn[pbank], "mod", xg(g)], writes=[xg(g)])
        else:
            xv = xT[:, d, t0:t0 + n].rearrange("p (s i) -> p s i", s=16)
            pv = banks[pbank][:, 0:n].rearrange("p (s i) -> p s i", s=16)
            S.op("dve", I("tensor_tensor", out=pv, in0=pv, in1=bc(mod[:, ig, d, 1:17].unsqueeze(2), [128, 16, 8]),
                                                   op=ALU.mult), reads=[bn[pbank], "mod"], writes=[bn[pbank]])
            S.op("dve", I("tensor_tensor", out=xv, in0=xv, in1=pv, op=ALU.add),
                 reads=[bn[pbank], xg(g)], writes=[xg(g)])

    def ada_layer(l):
        S.barrier()
        S.phase = 'ada%d' % l
        AR = Arena(nc, PH_BASE, SB_END, "ada%d" % l)
        ring = [AR.alloc("w%d" % i, [128, 8, 768], BF16) for i in range(3)]
        adaT = AR.alloc("adaT", [128, 72, NS], F32)
        for blk in range(12):
            slot = wq["n"] % 3
            wq["n"] += 1
            wt = ring[slot]
            src = w_ada[l, :, blk * 768:(blk + 1) * 768].rearrange("(kc p) n -> p kc n", p=128)
            S.dma("pool", I("dma_start", out=wt[:], in_=src), key="wr%d" % slot, writes=["wr%d" % slot])
            pb = 4 + (blk % 2)
            for j6 in range(6):
                for kc in range(8):
                    S.op("pe", I("matmul", banks[pb][:, j6 * NS:(j6 + 1) * NS], lhsT=wt[:, kc, j6 * 128:(j6 + 1) * 128], rhs=scT[:, kc, :],
                        start=(kc == 0), stop=(kc == 7)), reads=["wr%d" % slot, "scT"], writes=[bn[pb]])
            j0 = blk * 6
            S.op("dve", I("tensor_tensor", out=adaT[:, j0:j0 + 6, :], in0=banks[pb][:, 0:6 * NS].rearrange("p (a b) -> p a b", a=6),
                in1=bc(vecT[:, vrow(l, "b_ada", j0):vrow(l, "b_ada", j0) + 6].unsqueeze(2), [128, 6, NS]), op=ALU.add),
                reads=[bn[pb], "vecT"], writes=["adaT"])
        for k, (gname, half) in enumerate((("g_ffn1", 0.5), ("g_mix", 1.0), ("g_ffn2", 0.5))):
            sh = adaT[:, (3 * k) * 8:(3 * k) * 8 + 8, :]
            sc = adaT[:, (3 * k + 1) * 8:(3 * k + 1) * 8 + 8, :]
            gt = adaT[:, (3 * k + 2) * 8:(3 * k + 2) * 8 + 8, :]
            gv = bc(vecT[:, vrow(l, gname):vrow(l, gname) + 8].unsqueeze(2), [128, 8, NS])
            S.op("dve", I("scalar_tensor_tensor", out=mod[:, 3 * k, :, :], in0=sc, scalar=1.0, in1=gv,
                                                                          op0=ALU.add, op1=ALU.mult),
                 reads=["adaT", "vecT"], writes=["mod"])
            S.op("dve", I("tensor_copy", out=mod[:, 3 * k + 1, :, :], in_=sh), reads=["adaT"], writes=["mod"])
            S.op("dve", I("tensor_scalar", out=mod[:, 3 * k + 2, :, :], in0=gt, scalar1=half,
                                                                       scalar2=None, op0=ALU.mult),
                 reads=["adaT"], writes=["mod"])

    def ffn(l, which):
        S.barrier()
        S.phase = 'ffn%d%d' % (l, which)
        AR = Arena(nc, MT_BASE, SB_END, "ffn%d%d" % (l, which))
        FS = 4
        ring = [(AR.alloc("w1_%d" % i, [128, 8, FS * 128], BF16), AR.alloc("w3_%d" % i, [128, 8, FS * 128], BF16),
                 AR.alloc("w2_%d" % i, [128, FS, D], BF16)) for i in range(2)]
        gT = AR.alloc("gT", [128, FS, NT], BF16)
        sl = [AR.alloc("s%d" % i, [128, 512], F32) for i in range(2)]
        k3 = 0 if which == 0 else 2
        ia, ib, ig = 3 * k3, 3 * k3 + 1, 3 * k3 + 2
        W1, W3, W2 = w1[which], w3[which], w2[which]
        stages = [(f0, min(FS, 22 - f0)) for f0 in range(0, 22, FS)]

        def load(si):
            f0, nf = stages[si]
            slot = si % 2
            a, b_, c_ = ring[slot]
            s1 = W1[l, :, f0 * 128:(f0 + nf) * 128].rearrange("(kc p) n -> p kc n", p=128)
            s3 = W3[l, :, f0 * 128:(f0 + nf) * 128].rearrange("(kc p) n -> p kc n", p=128)
            s2 = W2[l, f0 * 128:(f0 + nf) * 128, :].rearrange("(f p) n -> p f n", p=128)
            for dst, src in ((a[:, :, 0:nf * 128], s1), (b_[:, :, 0:nf * 128], s3), (c_[:, 0:nf, :], s2)):
                S.dma("pool", I("dma_start", out=dst, in_=src), key="fw%d" % slot, writes=["fw%d" % slot])

        load(0)
        norm_group = norm_mod(AR, ia, ib, "f", lazy=True)
        norm_group(0)
        norm_group(1)
        load(1)
        cnt = 0
        yc = 0
        for si, (f0, nf) in enumerate(stages):
            slot = si % 2
            a, b_, c_ = ring[slot]
            wn = "fw%d" % slot
            for g, (t0, n) in enumerate(GROUPS):
                if si == 0 and g + 2 < len(GROUPS):
                    norm_group(g + 2)
                for f in range(nf):
                    pu1, pu3 = (cnt % 2) * 2, (cnt % 2) * 2 + 1
                    s_ = sl[cnt % 2]
                    sn = "sl%d" % (cnt % 2)
                    cnt += 1
                    for kc in range(8):
                        S.op("pe", I("matmul", banks[pu1][:, 0:n], lhsT=a[:, kc, f * 128:(f + 1) * 128], rhs=hT[:, kc, t0:t0 + n],
                                     start=(kc == 0), stop=(kc == 7)), reads=[wn, "hT:%d" % g], writes=[bn[pu1]])
                    for kc in range(8):
                        S.op("pe", I("matmul", banks[pu3][:, 0:n], lhsT=b_[:, kc, f * 128:(f + 1) * 128], rhs=hT[:, kc, t0:t0 + n],
                                     start=(kc == 0), stop=(kc == 7)), reads=[wn, "hT:%d" % g], writes=[bn[pu3]])
                    S.op("act", I("activation", out=s_[:, 0:n], in_=banks[pu1][:, 0:n], func=AF.Silu), reads=[bn[pu1]], writes=[sn])
                    S.op("dve", I("tensor_tensor", out=gT[:, f, t0:t0 + n], in0=banks[pu3][:, 0:n], in1=s_[:, 0:n], op=ALU.mult),
                         reads=[bn[pu3], sn], writes=["gT:%d:%d" % (f, g)])
                for d in range(8):
                    pb = 4 + (yc % 4)
                    yc += 1
                    for f in range(nf):
                        S.op("pe", I("matmul", banks[pb][:, 0:n], lhsT=c_[:, f, d * 128:(d + 1) * 128], rhs=gT[:, f, t0:t0 + n],
                                     start=(f == 0), stop=(f == nf - 1)), reads=[wn, "gT:%d:%d" % (f, g)], writes=[bn[pb]])
                    resid_add(pb, d, g, ig)
            if si + 2 < len(stages):
                load(si + 2)

    COL = dict(q=0, k=512, v=1024, g=2048, ca=3072, cb=3584, sb=4096, scc=4608, sx=5120, gl=5632)
    BRS = os.environ.get("BR", "RSC")

    def wcols(l, c0, n):
        return w_in[l, :, c0:c0 + n].rearrange("(kc p) n -> p kc n", p=128)

    def merge(l, b, srcT, nk, Wout, first, inplace, base):
        S.barrier()
        S.phase = 'merge%d%d' % (l, b)
        AR = Arena(nc, base, SB_END, "mg%d%d" % (l, b))
        wo_t = AR.alloc("wo", [128, nk, D], BF16)
        wg_t = AR.alloc("wg", [128, 8, D], BF16)
        sg = [AR.alloc("sg%d" % i, [128, 512], F32) for i in range(2)]
        tmp = AR.alloc("tmp", [128, 512], F32)
        mtmp = AR.alloc("mtmp", [128, 8, 512], BF16) if inplace else None
        for j in range(4):
            S.dma("pool", I("dma_start", out=wo_t[:, :, j * 256:(j + 1) * 256],
                            in_=Wout[l, :, j * 256:(j + 1) * 256].rearrange("(kc p) n -> p kc n", p=128)), key="mwo%d" % j, writes=["mwo%d" % j])
            S.dma("pool", I("dma_start", out=wg_t[:, :, j * 256:(j + 1) * 256], in_=wcols(l, COL["gl"] + b * D + j * 256, 256)),
                  key="mwg%d" % j, writes=["mwg%d" % j])
        cnt = 0
        for g, (t0, n) in enumerate(GROUPS):
            for d in range(8):
                pa, pb = (cnt % 2) * 2, (cnt % 2) * 2 + 1
                sg_ = sg[cnt % 2]
                sgn = "msg%d" % (cnt % 2)
                cnt += 1
                for kc in range(nk):
                    S.op("pe", I("matmul", banks[pa][:, 0:n], lhsT=wo_t[:, kc, d * 128:(d + 1) * 128], rhs=srcT[:, kc, t0:t0 + n],
                                 start=(kc == 0), stop=(kc == nk - 1)), reads=["mwo%d" % (d // 2), "src:%d" % g], writes=[bn[pa]])
                for kc in range(8):
                    S.op("pe", I("matmul", banks[pb][:, 0:n], lhsT=wg_t[:, kc, d * 128:(d + 1) * 128], rhs=hT[:, kc, t0:t0 + n],
                                 start=(kc == 0), stop=(kc == 7)), reads=["mwg%d" % (d // 2), "hT:%d" % g], writes=[bn[pb]])
                br = vrow(l, "b_gate", b * 8 + d)
                S.op("act", I("activation", out=sg_[:, 0:n], in_=banks[pb][:, 0:n], func=AF.Sigmoid, bias=vecT[:, br:br + 1]),
                     reads=[bn[pb], "vecT"], writes=[sgn])
                if inplace:
                    S.op("dve", I("tensor_tensor", out=mtmp[:, d, 0:n], in0=banks[pa][:, 0:n], in1=sg_[:, 0:n], op=ALU.mult),
                         reads=[bn[pa], sgn], writes=["mtmp"])
                elif first:
                    S.op("dve", I("tensor_tensor", out=mT[:, d, t0:t0 + n], in0=banks[pa][:, 0:n], in1=sg_[:, 0:n], op=ALU.mult),
                         reads=[bn[pa], sgn], writes=["m:%d" % g])
                else:
                    S.op("dve", I("tensor_tensor", out=tmp[:, 0:n], in0=banks[pa][:, 0:n], in1=sg_[:, 0:n], op=ALU.mult),
                         reads=[bn[pa], sgn], writes=["mtmpf"])
                    S.op("dve", I("tensor_tensor", out=mT[:, d, t0:t0 + n], in0=mT[:, d, t0:t0 + n], in1=tmp[:, 0:n], op=ALU.add),
                         reads=["mtmpf", "m:%d" % g], writes=["m:%d" % g])
            if inplace:
                S.op("act", I("copy", out=mT[:, :, t0:t0 + n], in_=mtmp[:, :, 0:n]), reads=["mtmp", "src:%d" % g],
                     writes=["m:%d" % g, "src:%d" % g])


    LG = np.log(np.float32(1.0) - np.exp2(-5.0 - np.arange(8, dtype=np.float32))).astype(np.float32)

    def branch_r(l):
        S.barrier()
        S.phase = 'brR%d' % l
        AR = Arena(nc, PH_BASE, SB_END, "br%d" % l)
        WT_OFF = AR.cur
        wt = AR.alloc("wqkvg", [128, 8, 1536], BF16)
        S0all = nc.alloc_sbuf_tensor_at("br%d_S0all" % l, [128, 4, 8, 128], F32, offset=WT_OFF)
        dm = AR.alloc("dm", [128, 4, 128], F32)
        zm = AR.alloc("zm", [128, 8, 128], BF16)
        km = AR.alloc("km", [128, 16], BF16)
        qkd = AR.alloc("qkd", [128, 2, 2, 8], F32)
        rot = [AR.alloc("rot%d" % i, [128, 3, 32], F32) for i in range(2)]
        ta = AR.alloc("ta", [128, 512], F32)
        tb = AR.alloc("tb", [128, 512], F32)
        qkrot = AR.alloc("qkrot", [128, 512], BF16)
        qrot, krot = qkrot[:, 0:256], qkrot[:, 256:512]
        qkdt = [AR.alloc("qkdt%d" % i, [128, 512], BF16) for i in range(2)]
        qd = [t[:, 0:256] for t in qkdt]
        kd = [t[:, 256:512] for t in qkdt]
        nmr = AR.alloc("nmr", [128, 4], F32)
        qT = [AR.alloc("qT%d" % i, [64, 4, 128], BF16) for i in range(2)]
        qdT = [AR.alloc("qdT%d" % i, [64, 4, 128], BF16) for i in range(2)]
        kT = [AR.alloc("kT%d" % i, [64, 4, 128], BF16) for i in range(2)]
        vb = [AR.alloc("vb%d" % i, [128, 512], BF16) for i in range(2)]
        sg = AR.alloc("sg", [128, 512], F32)
        AT = AR.alloc("AT", [128, 4, 128], BF16)
        st6 = AR.alloc("st6", [128, 4, 6], F32)
        mv = AR.alloc("mv", [128, 4, 2], F32)
        rsd = AR.alloc("rsd", [128, 4], F32)
        rrs = [AR.alloc("rr%d" % i, [128, 512], BF16) for i in range(2)]
        Sst = AR.alloc("Sst", [64, 4, 128], F32)
        Sb = AR.alloc("Sb", [64, 4, 128], BF16)
        S0bs = [AR.alloc("S0b%d" % i, [128, 8, 128], BF16) for i in range(2)]
        Zqs = [AR.alloc("Zq%d" % i, [128, 8, 128], BF16) for i in range(2)]
        qdd = AR.alloc("qdd", [128, 2, 64], BF16)
        B3b = banks[3][:].bitcast(BF16)
        B4b = banks[4][:].bitcast(BF16)
        B5b = banks[5][:].bitcast(BF16)
        S.dma("pool", I("dma_start", out=zm[:], in_=zmask_d), key="c_zm", writes=["zm"])
        S.dma("pool", I("dma_start", out=km[:], in_=kmask_d), key="c_km", writes=["km"])
        S.dma("sp", I("dma_start", out=qkd[:], in_=qk_dec_d), key="c_qkd", writes=["qkd"])

        def stage_a(hh, ti, part):
            S.phase = 'brR%d:%d:%02d' % (l, hh, ti)
            g = ti // 4
            c0 = ti * 128
            pi = 0 if ti < 16 else 1
            i2 = ti % 2
            rt = rot[i2]
            rn = "rot%d" % i2
            if part == 1:
              S.dma("sp", I("dma_start", out=rt[:], in_=rot_d[ti]), key=rn, writes=[rn])
              for (pb, o0, n_, c_) in ((0, 0, 256, 0), (0, 256, 256, 256), (1, 512, 512, 0)):
                for kc in range(8):
                    S.op("pe", I("matmul", banks[pb][:, c_:c_ + n_], lhsT=hT[:, kc, c0:c0 + 128], rhs=wt[:, kc, o0:o0 + n_],
                                 start=(kc == 0), stop=(kc == 7)), reads=["rw", "hT:%d" % g], writes=[bn[pb]])
              S.op("act", I("copy", out=vb[i2][:], in_=banks[1][:]), reads=[bn[1]], writes=["vb%d" % i2])
            if part == 2:
                X = banks[0][:]
                X16 = X.rearrange("p (a r) -> p a r", r=32)
                X8 = X.rearrange("p (h t r) -> p h t r", h=8, t=2)
                ta16 = ta[:].rearrange("p (a r) -> p a r", r=32)
                tb8 = tb[:].rearrange("p (h t r) -> p h t r", h=8, t=2)
                S.op("dve", I("tensor_tensor", out=ta16, in0=X16, in1=bc(rt[:, 0, :].unsqueeze(1), [128, 16, 32]), op=ALU.mult),
                     reads=[bn[0], rn], writes=["ta"])
                S.op("dve", I("tensor_tensor", out=tb8[:, :, 0, :], in0=X8[:, :, 1, :], in1=bc(rt[:, 2, :].unsqueeze(1), [128, 8, 32]),
                              op=ALU.mult), reads=[bn[0], rn], writes=["tb"])
                S.op("dve", I("tensor_tensor", out=tb8[:, :, 1, :], in0=X8[:, :, 0, :], in1=bc(rt[:, 1, :].unsqueeze(1), [128, 8, 32]),
                              op=ALU.mult), reads=[bn[0], rn], writes=["tb"])
                S.op("dve", I("tensor_tensor", out=ta[:], in0=ta[:], in1=tb[:], op=ALU.add), reads=["tb"], writes=["ta"])
                S.op("act", I("copy", out=qkrot[:], in_=ta[:]), reads=["ta"], writes=["qkrot"])
                S.op("dve", I("tensor_tensor", out=qkdt[i2][:].rearrange("p (w h d) -> p w h d", w=2, h=4),
                              in0=ta[:].rearrange("p (w h d) -> p w h d", w=2, h=4),
                              in1=bc(qkd[:, pi, :, 4 * hh:4 * hh + 4].unsqueeze(3), [128, 2, 4, 64]), op=ALU.mult),
                     reads=["ta", "qkd"], writes=["qkdt%d" % i2])
            if part != 3:
                return
            for h in range(4):
                S.op("pe", I("transpose", out=B3b[0:64, h * 128:(h + 1) * 128], in_=qkrot[:, h * 64:(h + 1) * 64], identity=identb[:]),
                     reads=["qkrot", "identb"], writes=[bn[3]])
                S.op("pe", I("transpose", out=B3b[0:64, 512 + h * 128:512 + (h + 1) * 128], in_=qkdt[i2][:, h * 64:(h + 1) * 64],
                             identity=identb[:]), reads=["qkdt%d" % i2, "identb"], writes=[bn[3]])
                S.op("pe", I("transpose", out=B4b[0:64, h * 128:(h + 1) * 128], in_=qkrot[:, 256 + h * 64:256 + (h + 1) * 64], identity=identb[:]),
                     reads=["qkrot", "identb"], writes=[bn[4]])
            S.op("act", I("copy", out=qT[i2][:].rearrange("p h t -> p (h t)"), in_=B3b[0:64, 0:512]), reads=[bn[3]], writes=["qT%d" % i2])
            S.op("act", I("copy", out=qdT[i2][:].rearrange("p h t -> p (h t)"), in_=B3b[0:64, 512:1024]), reads=[bn[3]], writes=["qdT%d" % i2])
            S.op("act", I("copy", out=kT[i2][:].rearrange("p h t -> p (h t)"), in_=B4b[0:64, 0:512]), reads=[bn[4]], writes=["kT%d" % i2])

        def stage_b(hh, ti, part):
            S.phase = 'brR%d:%d:%02d' % (l, hh, ti)
            g = ti // 4
            c0 = ti * 128
            pi = 0 if ti < 16 else 1
            i2 = ti % 2
            vbn, qTn, qdTn, kTn, qdn, kdn = ("vb%d" % i2, "qT%d" % i2, "qdT%d" % i2, "kT%d" % i2, "qkdt%d" % i2, "qkdt%d" % i2)
            rr = rrs[i2]
            rrn = "rr%d" % i2
            if part == 1:
              if ti == 16:
                S.dma("sp", I("dma_start", out=dm[:], in_=dmask_d[1, :, 4 * hh:4 * hh + 4, :]), key="c_dm", writes=["dm"])
              for kc in range(8):
                S.op("pe", I("matmul", banks[2][:], lhsT=hT[:, kc, c0:c0 + 128], rhs=wt[:, kc, 1024:1536], start=(kc == 0), stop=(kc == 7)),
                     reads=["rw", "hT:%d" % g], writes=[bn[2]])
              S.op("act", I("activation", out=sg[:], in_=banks[2][:], func=AF.Silu), reads=[bn[2]], writes=["sg"])
              if ti == 16:
                for h in range(4):
                    for s2 in range(2):
                        S.dma("sp", I("dma_start", out=S0all[s2 * 64:(s2 + 1) * 64, h, :, :],
                                      in_=st_ret[l, s2::2, 4 * hh + h, :, :].rearrange("pr d e -> d pr e")), key="S0_%d" % h,
                              reads=([] if (h == 0 and s2 == 0) else ["rw"]), writes=(["S0_%d" % h, "rw"] if (h == 0 and s2 == 0) else ["S0_%d" % h]))
              for h in range(4):
                S.op("pe", I("matmul", banks[5][:, h * 128:(h + 1) * 128], lhsT=kT[i2][:, h, :], rhs=qT[i2][:, h, :], start=True, stop=True),
                     reads=[kTn, qTn], writes=[bn[5]])
              S.op("dve", I("tensor_tensor", out=AT[:].rearrange("p h t -> p (h t)"), in0=banks[5][:],
                          in1=dm[:].rearrange("p h t -> p (h t)"), op=ALU.mult), reads=[bn[5], "dm"], writes=["AT"])
            for h in (range(4) if part == 2 else ()):
                H = 4 * hh + h
                hb = slice(h * 128, (h + 1) * 128)
                if pi == 0:
                    S.op("pe", I("matmul", banks[6][:, hb], lhsT=AT[:, h, :], rhs=vb[i2][:, hb], start=True, stop=(ti == 0)),
                         reads=["AT", vbn], writes=[bn[6]])
                    if ti > 0:
                        S.op("pe", I("matmul", banks[6][:, hb], lhsT=qdT[i2][:, h, :], rhs=Sb[:, h, :], start=False, stop=True),
                             reads=[qdTn, "Sb"], writes=[bn[6]])
                    S.op("pe", I("matmul", banks[7][0:64, hb], lhsT=qkdt[i2][:, 256 + h * 64:256 + (h + 1) * 64], rhs=vb[i2][:, hb], start=True, stop=True),
                         reads=[kdn, vbn], writes=[bn[7]])
                else:
                    S0 = S0all[:, h, :, :]
                    S0n = "S0_%d" % h
                    S0b, S0bn = S0bs[h % 2], "S0b%d" % (h % 2)
                    Zq, Zqn = Zqs[h % 2], "Zq%d" % (h % 2)
                    Zk = Zq[:].rearrange("p (a b) d -> p a (b d)", b=1).rearrange("p a (s d) -> p (a s) d", d=64)
                    S.op("act", I("copy", out=S0b[:], in_=S0), reads=[S0n, "rw"], writes=[S0bn])
                    for t_ in range(2):
                        S.op("dve", I("tensor_copy", out=qdd[:, t_, :], in_=qkdt[i2][:, h * 64:(h + 1) * 64]), reads=[qdn], writes=["qdd"])
                    S.op("pe", I("transpose", out=B4b[:, 512:640], in_=qdd[:].rearrange("p t d -> p (t d)"), identity=identb[:]),
                         reads=["qdd", "identb"], writes=[bn[4]])
                    S.op("dve", I("tensor_tensor", out=Zq[:], in0=bc(B4b[:, 512:640].unsqueeze(1), [128, 8, 128]), in1=zm[:], op=ALU.mult),
                         reads=[bn[4], "zm"], writes=[Zqn])
                    S.op("pe", I("matmul", banks[6][:, hb], lhsT=AT[:, h, :], rhs=vb[i2][:, hb], start=True, stop=False),
                         reads=["AT", vbn], writes=[bn[6]])
                    for pr in range(8):
                        S.op("pe", I("matmul", banks[6][:, hb], lhsT=Zq[:, pr, :], rhs=S0b[:, pr, :], start=False, stop=(pr == 7)),
                             reads=[Zqn, S0bn], writes=[bn[6]])
                    S.op("dve", I("tensor_tensor", out=Zk, in0=bc(qkdt[i2][:, 256 + h * 64:256 + (h + 1) * 64].unsqueeze(1), [128, 16, 64]),
                                  in1=bc(km[:].unsqueeze(2), [128, 16, 64]), op=ALU.mult), reads=[kdn, "km"], writes=[Zqn])
                    for pr in range(8):
                        S.op("pe", I("matmul", banks[pr // 4][:, (pr % 4) * 128:(pr % 4 + 1) * 128], lhsT=Zq[:, pr, :], rhs=vb[i2][:, hb],
                                     start=True, stop=True), reads=[Zqn, vbn], writes=[bn[pr // 4]])
                    cd8 = float(np.exp(np.float32(8.0) * LG[H]))
                    for q_ in range(2):
                        S.op("dve", I("scalar_tensor_tensor", out=S0[:, 4 * q_:4 * q_ + 4, :].rearrange("p a e -> p (a e)"),
                                      in0=S0[:, 4 * q_:4 * q_ + 4, :].rearrange("p a e -> p (a e)"), scalar=cd8, in1=banks[q_][:],
                                      op0=ALU.mult, op1=ALU.add), reads=[bn[q_], "rw"], writes=[S0n])
                    for s2 in range(2):
                        finals.append(S.dma("sp", I("dma_start", out=rets[l, s2::2, H, :, :].rearrange("pr d e -> d pr e"),
                                                    in_=S0all[s2 * 64:(s2 + 1) * 64, h, :, :]), key="o_rets%d" % h, reads=[S0n, "rw"]))
            if pi == 0 and part == 2:
                for h in range(4):
                    H = 4 * hh + h
                    cd = float(np.exp(np.float32(128.0) * LG[H]))
                    S.op("dve", I("scalar_tensor_tensor", out=Sst[:, h, :], in0=Sst[:, h, :], scalar=cd,
                                  in1=banks[7][0:64, h * 128:(h + 1) * 128], op0=ALU.mult, op1=ALU.add),
                         reads=[bn[7]], writes=["Sst"])
                S.op("act", I("copy", out=Sb[:], in_=Sst[:]), reads=["Sst"], writes=["Sb"])
            if part == 4:
                for h in range(4):
                    S.op("pe", I("transpose", out=B4b[:, 512 + h * 128:512 + (h + 1) * 128], in_=rr[:, h * 128:(h + 1) * 128], identity=identb[:]),
                         reads=[rrn, "identb"], writes=[bn[4]])
                for h in range(4):
                    r = vrow(l, "gn", 4 * hh + h)
                    S.op("act", I("activation", out=mT[:, 4 * hh + h, c0:c0 + 128], in_=B4b[:, 512 + h * 128:512 + (h + 1) * 128], func=AF.Identity,
                                  scale=vecT[:, r:r + 1]), reads=[bn[4], "vecT"], writes=["src:%d" % g])
            if part != 3:
                return
            for h in range(4):
                S.op("dve", I("bn_stats", out=st6[:, h, :], in_=banks[6][:, h * 128:(h + 1) * 128]), reads=[bn[6]], writes=["st6"])
            for h in range(4):
                S.op("dve", I("bn_aggr", out=mv[:, h, :], in_=st6[:, h, :]), reads=["st6"], writes=["mv"])
            S.op("dve", I("tensor_scalar", out=rsd[:], in0=mv[:, :, 1], scalar1=1e-5, scalar2=None, op0=ALU.add), reads=["mv"], writes=["rsd"])
            S.op("pool", I("tensor_tensor", out=rsd[:], in0=rsd[:], in1=mhalf[:, 0:4], op=ALU.pow), reads=["mhalf"], writes=["rsd"])
            S.op("dve", I("scalar_tensor_tensor", out=nmr[:], in0=mv[:, :, 0], scalar=-1.0, in1=rsd[:], op0=ALU.mult, op1=ALU.mult),
                 reads=["mv", "rsd"], writes=["nmr"])
            for h in range(4):
                S.op("act", I("activation", out=AT[:, h, :], in_=banks[6][:, h * 128:(h + 1) * 128], func=AF.Identity,
                              scale=rsd[:, h:h + 1], bias=nmr[:, h:h + 1]), reads=[bn[6], "rsd", "nmr"], writes=["AT"])
            S.op("dve", I("tensor_tensor", out=rr[:], in0=AT[:].rearrange("p h t -> p (h t)"), in1=sg[:], op=ALU.mult),
                 reads=["AT", "sg"], writes=[rrn])

        for hh in range(2):
            for j, (c0, n_) in enumerate(((COL["q"] + hh * 256, 256), (COL["k"] + hh * 256, 256), (COL["v"] + hh * 512, 512),
                                          (COL["g"] + hh * 512, 512))):
                o0 = (0, 256, 512, 1024)[j]
                S.dma("pool", I("dma_start", out=wt[:, :, o0:o0 + n_], in_=wcols(l, c0, n_)), key="rw", writes=["rw"])
            S.dma("sp", I("dma_start", out=dm[:], in_=dmask_d[0, :, 4 * hh:4 * hh + 4, :]), key="c_dm", writes=["dm"])
            S.op("pool", I("memset", Sst[:], 0.0), writes=["Sst"])
            S.op("pool", I("memset", Sb[:], 0.0), writes=["Sb"])
            for part in (1, 2, 3):
                stage_a(hh, 0, part)
            for ti in range(17):
                nxt = ti + 1 < 17
                if nxt:
                    stage_a(hh, ti + 1, 1)
                stage_b(hh, ti, 1)
                if nxt:
                    stage_a(hh, ti + 1, 2)
                stage_b(hh, ti, 2)
                if nxt:
                    stage_a(hh, ti + 1, 3)
                if ti > 0:
                    stage_b(hh, ti - 1, 4)
                stage_b(hh, ti, 3)
            stage_b(hh, 16, 4)
            finals.append(S.dma("sp", I("dma_start", out=retp[l, 4 * hh:4 * hh + 4].rearrange("h d e -> d h e"), in_=Sst[:]), key="o_retp",
                                reads=["Sst"]))
        merge(l, 0, mT, 8, w_ret_out, True, True, PH_BASE)

    def branch_s(l, first):
        S.barrier()
        S.phase = 'brS%d' % l
        AR = Arena(nc, PH_BASE, SB_END, "bs%d" % l)
        pST = AR.alloc("pST", [128, 4, NT], BF16)
        S_BASE = AR.cur
        ring = [AR.alloc("w%d" % i, [128, 8, 384], BF16) for i in range(3)]
        uspad = AR.alloc("uspad", [128, 2 + NP], F32)
        us_s = AR.alloc("us_s", [128, 4, 16, 10], F32)
        sxs = [AR.alloc("sx%d" % i, [128, 512], F32) for i in range(2)]
        acc = [AR.alloc("acc%d" % i, [128, 512], F32) for i in range(2)]
        stst = AR.alloc("stst", [32, 512], F32)
        so_p = AR.alloc("so_p", [2, 512], F32)
        ust = AR.alloc("ust", [128, 4, 32], F32)
        so_s = AR.alloc("so_s", [32, 512], F32)
        S.dma("sp", I("dma_start", out=stst[:], in_=st_sc[l].rearrange("s i c -> (s i) c")), key="stst", writes=["stst"])
        for cc in range(4):
            S.op("pe", I("transpose", out=banks[7][:, cc * 32:(cc + 1) * 32], in_=stst[:, cc * 128:(cc + 1) * 128],
                         identity=ident[0:32, 0:32]), reads=["stst", "ident"], writes=[bn[7]])
        S.op("dve", I("tensor_copy", out=us_s[:, :, :, 0:2], in_=banks[7][:, 0:128].rearrange("p (c s i) -> p c s i", c=4, s=16)),
             reads=[bn[7]], writes=["us_s"])
        S.op("pool", I("memset", uspad[:, 0:2], 0.0), writes=["uspad"])
        cnt = 0
        for cc in range(4):
            slot = wq["n"] % 3
            wq["n"] += 1
            wt = ring[slot]
            wn = "wr%d" % slot
            for j, nm in enumerate(("sb", "scc", "sx")):
                S.dma("pool", I("dma_start", out=wt[:, :, j * 128:(j + 1) * 128], in_=wcols(l, COL[nm] + cc * 128, 128)), key=wn, writes=[wn])
            for g, (t0, n) in enumerate(GROUPS):
                i2 = cnt % 2
                cnt += 1
                B0, B1, B2 = 3 * i2, 3 * i2 + 1, 3 * i2 + 2
                for j in range(3):
                    for kc in range(8):
                        S.op("pe", I("matmul", banks[3 * i2 + j][:, 0:n], lhsT=wt[:, kc, j * 128:(j + 1) * 128], rhs=hT[:, kc, t0:t0 + n],
                                     start=(kc == 0), stop=(kc == 7)), reads=[wn, "hT:%d" % g], writes=[bn[3 * i2 + j]])
                S.op("act", I("copy", out=sxs[i2][:, 0:n], in_=banks[B2][:, 0:n]), reads=[bn[B2]], writes=["sxs%d" % i2])
                if g < 4:
                    S.op("dve", I("tensor_tensor", out=uspad[:, 2 + t0:2 + t0 + n], in0=banks[B1][:, 0:n], in1=sxs[i2][:, 0:n], op=ALU.mult),
                         reads=[bn[B1], "sxs%d" % i2], writes=["uspad"])
                    taps = [uspad[:, t0 + k:t0 + k + n] for k in range(3)]
                    accv = acc[i2][:, 0:n]
                    pv = banks[B0][:, 0:n]
                    ov = pST[:, cc, t0:t0 + n]
                    srcn = "uspad"
                else:
                    S.op("dve", I("tensor_tensor", out=us_s[:, cc, :, 2:10], in0=banks[B1][:, 0:n].rearrange("p (s i) -> p s i", s=16),
                                  in1=sxs[i2][:, 0:n].rearrange("p (s i) -> p s i", s=16), op=ALU.mult),
                         reads=[bn[B1], "sxs%d" % i2], writes=["us_s"])
                    taps = [us_s[:, cc, :, k:k + 8] for k in range(3)]
                    accv = acc[i2][:, 0:n].rearrange("p (s i) -> p s i", s=16)
                    pv = banks[B0][:, 0:n].rearrange("p (s i) -> p s i", s=16)
                    ov = pST[:, cc, t0:t0 + n].rearrange("p (s i) -> p s i", s=16)
                    srcn = "us_s"
                wr = [vrow(l, "scw", k * 4 + cc) for k in range(3)]
                S.op("dve", I("tensor_scalar", out=accv, in0=taps[2], scalar1=vecT[:, wr[2]:wr[2] + 1], scalar2=None, op0=ALU.mult),
                     reads=[srcn, "vecT"], writes=["acc%d" % i2])
                for k in (1, 0):
                    S.op("dve", I("scalar_tensor_tensor", out=accv, in0=taps[k], scalar=vecT[:, wr[k]:wr[k] + 1], in1=accv,
                                  op0=ALU.mult, op1=ALU.add), reads=[srcn, "vecT"], writes=["acc%d" % i2])
                S.op("dve", I("tensor_tensor", out=ov, in0=pv, in1=accv, op=ALU.mult), reads=[bn[B0], "acc%d" % i2], writes=["src:%d" % g])
            S.op("pe", I("transpose", out=banks[6][0:2, cc * 128:(cc + 1) * 128], in_=uspad[:, NP:NP + 2], identity=ident[:]),
                 reads=["uspad", "ident"], writes=[bn[6]])
            S.op("act", I("copy", out=ust[:, cc, :].rearrange("p (s i) -> p s i", s=16), in_=us_s[:, cc, :, 8:10]), reads=["us_s"], writes=["ust"])
            S.op("pe", I("transpose", out=banks[7][0:32, cc * 128:(cc + 1) * 128], in_=ust[:, cc, :], identity=ident[:]),
                 reads=["ust", "ident"], writes=[bn[7]])
        S.op("act", I("copy", out=so_p[:], in_=banks[6][0:2, :]), reads=[bn[6]], writes=["so_p"])
        S.op("act", I("copy", out=so_s[:], in_=banks[7][0:32, :]), reads=[bn[7]], writes=["so_s"])
        finals.append(S.dma("sp", I("dma_start", out=scp[l], in_=so_p[:]), key="o_scp", reads=["so_p"]))
        finals.append(S.dma("sp", I("dma_start", out=scs[l].rearrange("s i c -> (s i) c"), in_=so_s[:]), key="o_scs", reads=["so_s"]))
        merge(l, 2, pST, 4, w_sc_out, first, False, S_BASE)

    def branch_c(l, first):
        S.barrier()
        S.phase = 'brC%d' % l
        AR = Arena(nc, PH_BASE, SB_END, "bc%d" % l)
        cvb = AR.alloc("cvb", [128, 4, NT], BF16)
        LN_BASE = AR.cur
        ring = [AR.alloc("w%d" % i, [128, 8, 256], BF16) for i in range(2)]
        upad = AR.alloc("upad", [128, 30 + NP], BF16)
        us_c = AR.alloc("us_c", [128, 4, 16, 38], BF16)
        usf = AR.alloc("usf", [128, 4, 128], F32)
        u32p = AR.alloc("u32p", [128, 4, 32], F32)
        dg = AR.alloc("dg", [128, 31, 128], BF16)
        sgs = [AR.alloc("sg%d" % i, [128, 512], F32) for i in range(2)]
        uf = [AR.alloc("uf%d" % i, [128, 512], F32) for i in range(2)]
        cst4 = [AR.alloc("cst%d" % i, [120, 512], F32) for i in range(2)]
        co_p = AR.alloc("co_p", [32, 512], F32)
        co_s = AR.alloc("co_s", [128, 512], F32)
        for j in range(4):
            cs = cst4[j % 2]
            S.dma("sp", I("dma_start", out=cs[:], in_=st_conf[l, 4 * j:4 * j + 4].rearrange("s r c -> (s r) c")), key="cst4%d" % (j % 2),
                  writes=["cst4%d" % (j % 2)])
            pb = 6 + (j % 2)
            for cc in range(4):
                S.op("pe", I("transpose", out=banks[pb][:, cc * 120:(cc + 1) * 120], in_=cs[:, cc * 128:(cc + 1) * 128],
                             identity=ident[0:120, 0:120]), reads=["cst4%d" % (j % 2), "ident"], writes=[bn[pb]])
            S.op("dve", I("tensor_copy", out=us_c[:, :, 4 * j:4 * j + 4, 0:30],
                          in_=banks[pb][:, 0:480].rearrange("p (c s r) -> p c s r", c=4, s=4)), reads=[bn[pb]], writes=["us_c"])
        finals.append(S.dma("sp", I("dma_start", out=confs[l, :, 0:22, :], in_=st_conf[l, :, 8:30, :]), key="o_cfs0"))
        S.op("pool", I("memset", upad[:, 0:30], 0.0), writes=["upad"])
        cnt = 0
        for cc in range(4):
            slot = cc % 2
            wt = ring[slot]
            wn = "cw%d" % slot
            for j, nm in enumerate(("ca", "cb")):
                S.dma("pool", I("dma_start", out=wt[:, :, j * 128:(j + 1) * 128], in_=wcols(l, COL[nm] + cc * 128, 128)), key=wn, writes=[wn])
            for k in range(31):
                r = vrow(l, "ccw", k * 4 + cc)
                S.op("dve", I("tensor_scalar", out=dg[:, k, :], in0=identb[:], scalar1=vecT[:, r:r + 1], scalar2=None, op0=ALU.mult),
                     reads=["identb", "vecT"], writes=["dg"])
            for g, (t0, n) in enumerate(GROUPS):
                i2 = cnt % 2
                cnt += 1
                Ba, Bb = 4 * i2, 4 * i2 + 1
                for j in range(2):
                    for kc in range(8):
                        S.op("pe", I("matmul", banks[4 * i2 + j][:, 0:n], lhsT=wt[:, kc, j * 128:(j + 1) * 128], rhs=hT[:, kc, t0:t0 + n],
                                     start=(kc == 0), stop=(kc == 7)), reads=[wn, "hT:%d" % g], writes=[bn[4 * i2 + j]])
                S.op("act", I("activation", out=sgs[i2][:, 0:n], in_=banks[Bb][:, 0:n], func=AF.Sigmoid), reads=[bn[Bb]], writes=["csg%d" % i2])
                S.op("dve", I("tensor_tensor", out=uf[i2][:, 0:n], in0=banks[Ba][:, 0:n], in1=sgs[i2][:, 0:n], op=ALU.mult),
                     reads=[bn[Ba], "csg%d" % i2], writes=["uf%d" % i2])
                if g < 4:
                    S.op("act", I("copy", out=upad[:, 30 + t0:30 + t0 + n], in_=uf[i2][:, 0:n]), reads=["uf%d" % i2], writes=["upad"])
                    if g == 3:
                        S.op("act", I("copy", out=u32p[:, cc, :], in_=uf[i2][:, 480:512]), reads=["uf%d" % i2], writes=["u32p"])
                else:
                    S.op("act", I("copy", out=us_c[:, cc, :, 30:38], in_=uf[i2][:, 0:n].rearrange("p (s i) -> p s i", s=16)),
                         reads=["uf%d" % i2], writes=["us_c"])
                    S.op("act", I("copy", out=usf[:, cc, :], in_=uf[i2][:, 0:n]), reads=["uf%d" % i2], writes=["usf"])
            r = vrow(l, "ccb", cc)
            for g, (t0, n) in enumerate(GROUPS[:4]):
                pb = 2 + (g % 2)
                for k in range(31):
                    S.op("pe", I("matmul", banks[pb][:, 0:n], lhsT=dg[:, k, :], rhs=upad[:, t0 + k:t0 + k + n], start=(k == 0), stop=(k == 30)),
                         reads=["dg", "upad"], writes=[bn[pb]])
                S.op("act", I("activation", out=cvb[:, cc, t0:t0 + n], in_=banks[pb][:, 0:n], func=AF.Identity, bias=vecT[:, r:r + 1]),
                     reads=[bn[pb], "vecT"], writes=["src:%d" % g])
            flat = us_c[:, cc, :, :].rearrange("p s r -> p (s r)")
            for hf in range(2):
                pb = 2 + hf
                for k in range(31):
                    S.op("pe", I("matmul", banks[pb][:, 0:274], lhsT=dg[:, k, :], rhs=flat[:, hf * 304 + k:hf * 304 + k + 274],
                                 start=(k == 0), stop=(k == 30)), reads=["dg", "us_c"], writes=[bn[pb]])
                S.op("act", I("activation", out=cvb[:, cc, NP + hf * 64:NP + (hf + 1) * 64].rearrange("p (s i) -> p s i", s=8),
                              in_=banks[pb][:, 0:304].rearrange("p (s r) -> p s r", s=8)[:, :, 0:8], func=AF.Identity, bias=vecT[:, r:r + 1]),
                     reads=[bn[pb], "vecT"], writes=["src:4"])
        for cc in range(4):
            S.op("pe", I("transpose", out=banks[6][0:32, cc * 128:(cc + 1) * 128], in_=u32p[:, cc, :], identity=ident[:]),
                 reads=["u32p", "ident"], writes=[bn[6]])
            S.op("pe", I("transpose", out=banks[7][:, cc * 128:(cc + 1) * 128], in_=usf[:, cc, :], identity=ident[:]),
                 reads=["usf", "ident"], writes=[bn[7]])
        S.op("act", I("copy", out=co_p[:], in_=banks[6][0:32, :]), reads=[bn[6]], writes=["co_p"])
        S.op("act", I("copy", out=co_s[:], in_=banks[7][:]), reads=[bn[7]], writes=["co_s"])
        finals.append(S.dma("sp", I("dma_start", out=confp[l], in_=co_p[2:32, :]), key="o_cfp", reads=["co_p"]))
        for s_ in range(16):
            finals.append(S.dma("sp", I("dma_start", out=confs[l, s_, 22:30, :], in_=co_s[s_ * 8:(s_ + 1) * 8, :]), key="o_cfs", reads=["co_s"]))
        S.barrier()
        S.phase = 'brCln%d' % l
        AR = Arena(nc, LN_BASE, SB_END, "bcl%d" % l)
        sqb = [AR.alloc("sqb%d" % i, [128, 512], BF16) for i in range(2)]
        means = [AR.alloc("mean%d" % i, [128, 512], F32) for i in range(2)]
        var = AR.alloc("var", [128, 512], F32)
        rss = [AR.alloc("rs%d" % i, [128, 512], F32) for i in range(2)]
        tt = [AR.alloc("tt%d" % i, [128, 512], F32) for i in range(2)]

        def ln_stats(g):
            t0, n = GROUPS[g]
            p2 = g % 2
            mean, rs = means[p2], rss[p2]
            ba, bb = 4 + 2 * p2, 5 + 2 * p2
            for cc in range(4):
                S.op("act", I("activation", out=sqb[cc % 2][:, 0:n], in_=cvb[:, cc, t0:t0 + n], func=AF.Square),
                     reads=["src:%d" % g], writes=["sqb%d" % (cc % 2)])
                S.op("pe", I("matmul", banks[ba][:, 0:n], lhsT=onesb[:], rhs=cvb[:, cc, t0:t0 + n], start=(cc == 0), stop=(cc == 3)),
                     reads=["src:%d" % g, "onesb"], writes=[bn[ba]])
                S.op("pe", I("matmul", banks[bb][:, 0:n], lhsT=onesb[:], rhs=sqb[cc % 2][:, 0:n], start=(cc == 0), stop=(cc == 3)),
                     reads=["sqb%d" % (cc % 2), "onesb"], writes=[bn[bb]])
            S.op("dve", I("tensor_scalar", out=mean[:, 0:n], in0=banks[ba][:, 0:n], scalar1=1.0 / 512, scalar2=None, op0=ALU.mult),
                 reads=[bn[ba]], writes=["mean%d" % p2])
            S.op("dve", I("tensor_tensor", out=var[:, 0:n], in0=mean[:, 0:n], in1=mean[:, 0:n], op=ALU.mult), reads=["mean%d" % p2], writes=["var"])
            S.op("dve", I("scalar_tensor_tensor", out=var[:, 0:n], in0=banks[bb][:, 0:n], scalar=1.0 / 512, in1=var[:, 0:n],
                          op0=ALU.mult, op1=ALU.subtract), reads=[bn[bb]], writes=["var"])
            S.op("dve", I("tensor_scalar", out=var[:, 0:n], in0=var[:, 0:n], scalar1=1e-5, scalar2=None, op0=ALU.add), writes=["var"])
            S.op("act", I("activation", out=rs[:, 0:n], in_=var[:, 0:n], func=AF.Sqrt), reads=["var"], writes=["rs%d" % p2])
            S.op("dve", I("reciprocal", out=rs[:, 0:n], in_=rs[:, 0:n]), writes=["rs%d" % p2])

        def ln_apply(g):
            t0, n = GROUPS[g]
            p2 = g % 2
            mean, rs = means[p2], rss[p2]
            for cc in range(4):
                t_ = tt[cc % 2]
                S.op("dve", I("tensor_tensor", out=t_[:, 0:n], in0=cvb[:, cc, t0:t0 + n], in1=mean[:, 0:n], op=ALU.subtract),
                     reads=["src:%d" % g, "mean%d" % p2], writes=["ctt%d" % (cc % 2)])
                S.op("dve", I("tensor_tensor", out=t_[:, 0:n], in0=t_[:, 0:n], in1=rs[:, 0:n], op=ALU.mult), reads=["rs%d" % p2],
                     writes=["ctt%d" % (cc % 2)])
                rg, rb = vrow(l, "lng", cc), vrow(l, "lnb", cc)
                S.op("act", I("activation", out=cvb[:, cc, t0:t0 + n], in_=t_[:, 0:n], func=AF.Silu, scale=vecT[:, rg:rg + 1],
                              bias=vecT[:, rb:rb + 1]), reads=["ctt%d" % (cc % 2), "vecT"], writes=["src:%d" % g])

        ln_stats(0)
        for g in range(len(GROUPS)):
            if g + 1 < len(GROUPS):
                ln_stats(g + 1)
            ln_apply(g)
        merge(l, 1, cvb, 4, w_conf_out, first, False, LN_BASE)

    def wo_resid(l):
        S.barrier()
        S.phase = 'wo%d' % l
        AR = Arena(nc, PH_BASE, SB_END, "wo%d" % l)
        wt = AR.alloc("wo", [128, 8, D], BF16)
        for j in range(4):
            S.dma("pool", I("dma_start", out=wt[:, :, j * 256:(j + 1) * 256],
                            in_=w_o[l, :, j * 256:(j + 1) * 256].rearrange("(kc p) n -> p kc n", p=128)), key="mwo%d" % j, writes=["mwo%d" % j])
        c = 0
        for g, (t0, n) in enumerate(GROUPS):
            for d in range(8):
                pb = 4 + (c % 4)
                c += 1
                for kc in range(8):
                    S.op("pe", I("matmul", banks[pb][:, 0:n], lhsT=wt[:, kc, d * 128:(d + 1) * 128], rhs=mT[:, kc, t0:t0 + n],
                                 start=(kc == 0), stop=(kc == 7)), reads=["mwo%d" % (d // 2), "m:%d" % g], writes=[bn[pb]])
                resid_add(pb, d, g, 5)

    def mixer(l):
        S.barrier()
        S.phase = 'mixnorm%d' % l
        AR = Arena(nc, PH_BASE, SB_END, "mx%d" % l)
        norm_mod(AR, 3, 4, "m")
        first = True
        if "R" in BRS:
            branch_r(l)
            first = False
        if "S" in BRS:
            branch_s(l, first)
            first = False
        if "C" in BRS:
            branch_c(l, first)
            first = False
        wo_resid(l)

    for l in range(n_layers):
        ada_layer(l)
        ffn(l, 0)
        if mixer_on:
            mixer(l)
        ffn(l, 1)

    S.barrier()
    S.phase = 'final'
    AR = Arena(nc, PH_BASE, SB_END, "fin")
    sq = [AR.alloc("sq%d" % i, [128, 512], BF16) for i in range(2)]
    v = AR.alloc("v", [128, 512], F32)
    rstd = AR.alloc("rstd", [128, 512], F32)
    yT = AR.alloc("yT", [128, 8, 512], F32)
    ost = [AR.alloc("ost%d" % i, [128, D], F32) for i in range(2)]
    gfin = 2 * VR_PER_LAYER
    oc = 0
    for g, (t0, n) in enumerate(GROUPS):
        pb = 6 + (g % 2)
        for kc in range(8):
            s_ = sq[kc % 2]
            S.op("act", I("activation", out=s_[:, 0:n], in_=xT[:, kc, t0:t0 + n], func=AF.Square),
                 reads=[xg(g)], writes=["sq%d" % (kc % 2)])
            S.op("pe", I("matmul", banks[pb][:, 0:n], lhsT=onesb[:], rhs=s_[:, 0:n],
                                                             start=(kc == 0), stop=(kc == 7)),
                 reads=["sq%d" % (kc % 2), "onesb"], writes=[bn[pb]])
        S.op("dve", I("tensor_scalar", out=v[:, 0:n], in0=banks[pb][:, 0:n], scalar1=1.0 / D, scalar2=1e-6,
                                                      op0=ALU.mult, op1=ALU.add), reads=[bn[pb]], writes=["v"])
        S.op("act", I("activation", out=rstd[:, 0:n], in_=v[:, 0:n], func=AF.Sqrt), reads=["v"], writes=["rstd"])
        S.op("dve", I("reciprocal", out=rstd[:, 0:n], in_=rstd[:, 0:n]), writes=["rstd"])
        for kc in range(8):
            S.op("dve", I("scalar_tensor_tensor", out=yT[:, kc, 0:n], in0=xT[:, kc, t0:t0 + n],
                                                               scalar=vecT[:, gfin + kc:gfin + kc + 1], in1=rstd[:, 0:n],
                                                               op0=ALU.mult, op1=ALU.mult),
                 reads=[xg(g), "rstd", "vecT"], writes=["yT"])
        for ti in range(n // 128):
            o_ = ost[oc % 2]
            on = "ost%d" % (oc % 2)
            for hb in range(2):
                pbk = (oc % 2) * 2 + hb
                for c4 in range(4):
                    c = hb * 4 + c4
                    S.op("pe", I("transpose", out=banks[pbk][:, c4 * 128:(c4 + 1) * 128], in_=yT[:, c, ti * 128:(ti + 1) * 128], identity=ident[:]),
                        reads=["yT", "ident"], writes=[bn[pbk]])
                if hb == 0:
                    S.op("act", I("copy", out=o_[:, 0:512], in_=banks[pbk][:]), reads=[bn[pbk]], writes=[on])
                else:
                    S.op("dve", I("tensor_copy", out=o_[:, 512:1024], in_=banks[pbk][:]),
                         reads=[bn[pbk]], writes=[on])
            tg = t0 // 128 + ti
            dst = yp[tg * 128:(tg + 1) * 128, :] if tg < 16 else ys
            finals.append(S.dma("sp", I("dma_start", out=dst, in_=o_[:]), key="o%d" % (oc % 2), reads=[on]))
            oc += 1

    last = {}
    for t in finals:
        last[(t[0], t[1])] = t
    S.emit(final_waits=list(last.values()))
    return nc


def pack_vecs(inp):
    rows = []
    for l in range(L):
        rows.append(inp["b_ada"][l].reshape(72, 128))
        rows.append(inp["g_ffn1"][l].reshape(8, 128))
        rows.append(inp["g_mix"][l].reshape(8, 128))
        rows.append(inp["g_ffn2"][l].reshape(8, 128))
        rows.append(inp["b_gate"][l].reshape(24, 128))
        rows.append(inp["ret_gn_g"][l].reshape(8, 128))
        rows.append(inp["conf_conv_w"][l].reshape(124, 128))
        rows.append(inp["conf_conv_b"][l].reshape(4, 128))
        rows.append(inp["conf_ln_g"][l].reshape(4, 128))
        rows.append(inp["conf_ln_b"][l].reshape(4, 128))
        rows.append(inp["sc_conv_w"][l].reshape(12, 128))
    rows.append(inp["g_final"].reshape(8, 128))
    v = np.ascontiguousarray(np.concatenate(rows, axis=0), dtype=np.float32)
    assert v.shape == (VR_TOTAL, 128)
    return v


_NC_CACHE = {}


def make_consts():
    f32 = np.float32
    lg = np.log(f32(1.0) - np.exp2(-5.0 - np.arange(8, dtype=f32))).astype(f32)
    freqs = (f32(10000.0) ** (-np.arange(32, dtype=f32) / f32(32))).astype(f32)
    p = np.arange(128)
    rot = np.zeros((17, 128, 3, 32), f32)
    for ti in range(17):
        pos = (ti * 128 + p).astype(f32) if ti < 16 else (16384 + (p % 8)).astype(f32)
        ang = (pos[:, None] * freqs[None, :]).astype(f32)
        rot[ti, :, 0] = np.cos(ang)
        rot[ti, :, 1] = np.sin(ang)
        rot[ti, :, 2] = -np.sin(ang)
    dmask = np.zeros((2, 128, 8, 128), f32)
    jj, ii = np.meshgrid(p, p, indexing="ij")
    diff = (ii - jj).astype(f32)
    for h in range(8):
        dec = np.where(diff >= 0, np.exp(np.maximum(diff, 0) * lg[h]), 0.0).astype(f32)
        dmask[0, :, h, :] = dec * f32(0.125)
        d8 = ((ii % 8) - (jj % 8)).astype(f32)
        same = (ii // 8) == (jj // 8)
        dmask[1, :, h, :] = np.where(same & (d8 >= 0), np.exp(np.maximum(d8, 0) * lg[h]), 0.0).astype(f32) * f32(0.125)
    qkdec = np.zeros((128, 2, 2, 8), f32)
    for h in range(8):
        qkdec[:, 0, 0, h] = np.exp((p + 1).astype(f32) * lg[h]) * f32(0.125)
        qkdec[:, 0, 1, h] = np.exp((127 - p).astype(f32) * lg[h])
        qkdec[:, 1, 0, h] = np.exp(((p % 8) + 1).astype(f32) * lg[h]) * f32(0.125)
        qkdec[:, 1, 1, h] = np.exp((7 - (p % 8)).astype(f32) * lg[h])
    zmask = np.zeros((128, 8, 128), f32)
    for pr in range(8):
        for s2 in range(2):
            zmask[s2 * 64:(s2 + 1) * 64, pr, (2 * pr + s2) * 8:(2 * pr + s2 + 1) * 8] = 1.0
    kmask = np.zeros((128, 16), f32)
    kmask[p, p // 8] = 1.0
    return dict(rot=rot, dmask=dmask, qkdec=qkdec, zmask=zmask, kmask=kmask)


def make_in_maps(inp):
    vecs = pack_vecs(inp)
    ident = np.eye(128, dtype=np.float32)
    consts = make_consts()
    maps = []
    for b in range(8):
        m = dict(
            xp=np.ascontiguousarray(inp["x_prompt"][b]),
            xs=np.ascontiguousarray(inp["x_sample"][16 * b:16 * b + 16].reshape(128, D)),
            cc=np.ascontiguousarray(np.concatenate([inp["c_prompt"][b:b + 1], inp["c_sample"][16 * b:16 * b + 16]], axis=0)),
            vecs=vecs, ident=ident,
            w_ada=inp["w_ada"], w1_a=inp["w1_a"], w3_a=inp["w3_a"], w2_a=inp["w2_a"],
            w1_b=inp["w1_b"], w3_b=inp["w3_b"], w2_b=inp["w2_b"],
            w_in=inp["w_in"], w_ret_out=inp["w_ret_out"], w_conf_out=inp["w_conf_out"], w_sc_out=inp["w_sc_out"], w_o=inp["w_o"],
            st_ret=np.ascontiguousarray(inp["state_ret"][:, 16 * b:16 * b + 16]),
            st_conf=np.ascontiguousarray(inp["state_conf"][:, 16 * b:16 * b + 16]),
            st_sc=np.ascontiguousarray(inp["state_sconv"][:, 16 * b:16 * b + 16]),
            **consts,
        )
        maps.append(m)
    return maps


def kernel(**inputs):
    inp = {k: np.asarray(v) for k, v in inputs.items()}
    if "nc" not in _NC_CACHE:
        _NC_CACHE["nc"] = build_program()
    nc = _NC_CACHE["nc"]
    maps = make_in_maps(inp)
    res = run_bass_kernel_spmd(nc, maps, core_ids=list(range(8)))
    r = res.results
    y_prompt = np.stack([r[b]["yp"] for b in range(8)], axis=0)
    y_sample = np.concatenate([r[b]["ys"].reshape(16, 8, D) for b in range(8)], axis=0)
    ret_p = np.stack([r[b]["retp"] for b in range(8)], axis=1)
    ret_s = np.concatenate([r[b]["rets"] for b in range(8)], axis=1)
    conf_p = np.stack([r[b]["confp"] for b in range(8)], axis=1)
    conf_s = np.concatenate([r[b]["confs"] for b in range(8)], axis=1)
    sc_p = np.stack([r[b]["scp"] for b in range(8)], axis=1)
    sc_s = np.concatenate([r[b]["scs"] for b in range(8)], axis=1)
    return (y_prompt, y_sample, ret_p, ret_s, conf_p, conf_s, sc_p, sc_s)
```

```python
import os
import numpy as np
import concourse.bass as bass
import concourse.mybir as mybir
from concourse.bass_utils import run_bass_kernel_spmd

F32 = mybir.dt.float32
BF16 = mybir.dt.bfloat16
AF = mybir.ActivationFunctionType
ALU = mybir.AluOpType

D = 1024
DFF = 2816
NT = 2176
NP = 2048
NS = 17
L = 2
INC = 8704
GROUPS = [(0, 512), (512, 512), (1024, 512), (1536, 512), (2048, 128)]
SB_BASE = 16640
SB_END = 229376
ANNOTATE = bool(int(os.environ.get('ANNOTATE', '0')))

VR_PER_LAYER = 276
VR_TOTAL = 2 * VR_PER_LAYER + 8
VO = dict(b_ada=0, g_ffn1=72, g_mix=80, g_ffn2=88, b_gate=96, gn=120, ccw=128, ccb=252, lng=256, lnb=260, scw=264)


def vrow(l, name, i=0):
    return l * VR_PER_LAYER + VO[name] + i


class Sched:
    ENGS = ("pe", "act", "dve", "pool", "sp")

    def __init__(self, nc):
        self.nc = nc
        self.ops = {e: [] for e in self.ENGS}
        self.cnt = {e: 0 for e in self.ENGS}
        self.last_w = {}
        self.readers = {}
        self.dma_sems = {}
        self.eng_sem = {}
        self.bar = []
        self.phase = 'setup'

    def _deps(self, reads, writes):
        deps = list(self.bar)
        for r in reads:
            t = self.last_w.get(r)
            if t is not None:
                deps.append(t)
        for w in writes:
            t = self.last_w.get(w)
            if t is not None:
                deps.append(t)
            deps.extend(self.readers.get(w, ()))
        return deps

    def _commit(self, tok, reads, writes):
        for r in reads:
            self.readers.setdefault(r, []).append(tok)
        for w in writes:
            self.last_w[w] = tok
            self.readers[w] = []

    @staticmethod
    def _split(reads, writes):
        r2 = [r for r in reads if not r.startswith("ps")]
        w2 = list(writes) + [r for r in reads if r.startswith("ps")]
        return r2, w2

    def op(self, eng, fn, reads=(), writes=()):
        reads, writes = self._split(reads, writes)
        deps = self._deps(reads, writes)
        self.cnt[eng] += 1
        tok = ("c", eng, self.cnt[eng])
        self.ops[eng].append(dict(fn=fn, deps=deps, tok=tok, ph=self.phase))
        self._commit(tok, reads, writes)
        return tok

    def dma(self, eng, fn, key, reads=(), writes=()):
        deps = self._deps(reads, writes)
        if key not in self.dma_sems:
            self.dma_sems[key] = [self.nc.alloc_semaphore("d_" + key), 0]
        ent = self.dma_sems[key]
        ent[1] += 16
        tok = ("d", key, ent[1])
        self.ops[eng].append(dict(fn=fn, deps=deps, tok=tok, ph=self.phase))
        self._commit(tok, reads, writes)
        return tok

    def barrier(self):
        toks = []
        for e in self.ENGS:
            if self.cnt[e]:
                toks.append(("c", e, self.cnt[e]))
        for k, (s, v) in self.dma_sems.items():
            if v:
                toks.append(("d", k, v))
        self.bar = toks

    def emit(self, final_waits=()):
        nc = self.nc
        for e in self.ENGS:
            self.eng_sem[e] = nc.alloc_semaphore("e_" + e)

        def run(ename, eng):
            waited = {}
            for o in self.ops[ename]:
                need = {}
                for t in o["deps"]:
                    if t[0] == "c":
                        if t[1] == ename and ename == "pe":
                            continue
                        sem = self.eng_sem[t[1]]
                    else:
                        sem = self.dma_sems[t[1]][0]
                    k = sem.num
                    if t[2] > need.get(k, (None, 0))[1]:
                        need[k] = (sem, t[2])
                for k, (sem, v) in need.items():
                    if waited.get(k, 0) >= v:
                        continue
                    eng.wait_ge(sem, v)
                    waited[k] = v
                ins = o["fn"](eng)
                if ANNOTATE:
                    ins.annotate(o["ph"])
                t = o["tok"]
                if t[0] == "c":
                    ins.then_inc(self.eng_sem[ename], 1)
                else:
                    ins.then_inc(self.dma_sems[t[1]][0], 16)
            if ename == "sp":
                for t in final_waits:
                    if t[0] == "c":
                        eng.wait_ge(self.eng_sem[t[1]], t[2])
                    else:
                        eng.wait_ge(self.dma_sems[t[1]][0], t[2])

        with nc.Block() as block:
            @block.tensor
            def _(e):
                run("pe", e)

            @block.scalar
            def _(e):
                run("act", e)

            @block.vector
            def _(e):
                run("dve", e)

            @block.gpsimd
            def _(e):
                run("pool", e)

            @block.sync
            def _(e):
                run("sp", e)


class Arena:
    def __init__(self, nc, base, end, tag):
        self.nc, self.base, self.end, self.tag, self.cur, self.n = nc, base, end, tag, base, 0

    def alloc(self, name, shape, dtype):
        esz = 4 if dtype == F32 else 2
        nbytes = int(np.prod(shape[1:])) * esz
        nbytes = (nbytes + 63) // 64 * 64
        assert self.cur + nbytes <= self.end, (self.tag, name, self.cur + nbytes - self.end)
        t = self.nc.alloc_sbuf_tensor_at("%s_%s_%d" % (self.tag, name, self.n), list(shape), dtype, offset=self.cur)
        self.n += 1
        self.cur += nbytes
        return t


def I(name, *a, **kw):
    return lambda e: getattr(e, name)(*a, **kw)


def bc(ap, shape):
    return ap.to_broadcast(list(shape))


def build_program(mixer_on=True, n_layers=L):
    nc = bass.Bass("TRN2", target_bir_lowering=False)
    S = Sched(nc)

    def din(name, shape, dt=F32):
        return nc.dram_tensor(name, list(shape), dt, kind="ExternalInput").ap()

    def dout(name, shape):
        return nc.dram_tensor(name, list(shape), F32, kind="ExternalOutput").ap()

    xp = din("xp", [NP, D])
    xs = din("xs", [128, D])
    cc = din("cc", [NS, D])
    vecs = din("vecs", [VR_TOTAL, 128])
    ident_d = din("ident", [128, 128])
    w_ada = din("w_ada", [L, D, 9 * D])
    w1 = [din("w1_a", [L, D, DFF]), din("w1_b", [L, D, DFF])]
    w3 = [din("w3_a", [L, D, DFF]), din("w3_b", [L, D, DFF])]
    w2 = [din("w2_a", [L, DFF, D]), din("w2_b", [L, DFF, D])]
    w_in = din("w_in", [L, D, INC])
    w_ret_out = din("w_ret_out", [L, D, D])
    w_conf_out = din("w_conf_out", [L, 512, D])
    w_sc_out = din("w_sc_out", [L, 512, D])
    w_o = din("w_o", [L, D, D])
    st_ret = din("st_ret", [L, 16, 8, 64, 128])
    st_conf = din("st_conf", [L, 16, 30, 512])
    st_sc = din("st_sc", [L, 16, 2, 512])
    rot_d = din("rot", [17, 128, 3, 32])
    dmask_d = din("dmask", [2, 128, 8, 128])
    qk_dec_d = din("qkdec", [128, 2, 2, 8])
    zmask_d = din("zmask", [128, 8, 128])
    kmask_d = din("kmask", [128, 16])
    yp = dout("yp", [NP, D])
    ys = dout("ys", [128, D])
    retp = dout("retp", [L, 8, 64, 128])
    rets = dout("rets", [L, 16, 8, 64, 128])
    confp = dout("confp", [L, 30, 512])
    confs = dout("confs", [L, 16, 30, 512])
    scp = dout("scp", [L, 2, 512])
    scs = dout("scs", [L, 16, 2, 512])
    finals = []

    P = Arena(nc, SB_BASE, SB_END, "P")
    xT = P.alloc("xT", [128, 8, NT], F32)
    hT = P.alloc("hT", [128, 8, NT], BF16)
    mod = P.alloc("mod", [128, 9, 8, NS], F32)
    vecT = P.alloc("vecT", [128, VR_TOTAL], F32)
    ident = P.alloc("ident", [128, 128], F32)
    identb = P.alloc("identb", [128, 128], BF16)
    onesb = P.alloc("onesb", [128, 128], BF16)
    scT = P.alloc("scT", [128, 8, NS], BF16)
    mhalf = P.alloc("mhalf", [128, 512], F32)
    MT_BASE = P.cur
    mT = P.alloc("mT", [128, 8, NT], BF16)
    PH_BASE = P.cur

    banks = [nc.alloc_psum_tensor("bank%d" % i, [128, 512], F32) for i in range(8)]
    bn = ["ps%d" % i for i in range(8)]

    S.dma("sp", I("dma_start", out=ident[:], in_=ident_d), key="c_id", writes=["ident"])
    S.op("act", I("copy", out=identb[:], in_=ident[:]), reads=["ident"], writes=["identb"])
    S.op("pool", I("memset", onesb[:], 1.0), writes=["onesb"])
    S.op("pool", I("memset", mhalf[:], -0.5), writes=["mhalf"])

    A0 = Arena(nc, PH_BASE, SB_END, "A0")
    xstage = [A0.alloc("xst%d" % i, [128, D], F32) for i in range(2)]
    vst = A0.alloc("vst", [112, 5, 128], F32)
    cst = A0.alloc("cst", [NS, D], F32)
    cT = A0.alloc("cT", [128, 8, NS], F32)

    for ti in range(17):
        st = xstage[ti % 2]
        src = xp[ti * 128:(ti + 1) * 128, :] if ti < 16 else xs
        S.dma("sp", I("dma_start", out=st[:], in_=src), key="xst%d" % (ti % 2),
              writes=["xst%d" % (ti % 2)])
        for hb in range(2):
            b = banks[(ti % 2) * 2 + hb]
            for c4 in range(4):
                c = hb * 4 + c4
                S.op("pe", I("transpose", out=b[:, c4 * 128:(c4 + 1) * 128], in_=st[:, c * 128:(c + 1) * 128], identity=ident[:]),
                    reads=["xst%d" % (ti % 2), "ident"], writes=[bn[(ti % 2) * 2 + hb]])
            eng = "act" if hb == 0 else "dve"
            dst = xT[:, hb * 4:hb * 4 + 4, ti * 128:(ti + 1) * 128]
            srcp = b[:].rearrange("p (a b) -> p a b", a=4)
            if eng == "act":
                S.op("act", I("copy", out=dst, in_=srcp),
                     reads=[bn[(ti % 2) * 2 + hb]], writes=["xT:%d" % (ti // 4)])
            else:
                S.op("dve", I("tensor_copy", out=dst, in_=srcp),
                     reads=[bn[(ti % 2) * 2 + hb]], writes=["xT:%d" % (ti // 4)])

    S.dma("sp", I("dma_start", out=vst[:], in_=vecs.rearrange("(a r) f -> r a f", r=112)), key="vst", writes=["vst"])
    for a in range(5):
        b = banks[4 + (a % 2)]
        S.op("pe", I("transpose", out=b[:, 0:112], in_=vst[:, a, :], identity=ident[0:112, 0:112]),
             reads=["vst", "ident"], writes=[bn[4 + (a % 2)]])
        S.op("dve", I("tensor_copy", out=vecT[:, a * 112:(a + 1) * 112], in_=b[:, 0:112]),
             reads=[bn[4 + (a % 2)]], writes=["vecT"])
    S.dma("sp", I("dma_start", out=cst[:], in_=cc), key="cst", writes=["cst"])
    for c in range(8):
        S.op("pe", I("transpose", out=banks[6][:, c * NS:(c + 1) * NS], in_=cst[:, c * 128:(c + 1) * 128],
                                              identity=ident[0:NS, 0:NS]), reads=["cst", "ident"], writes=[bn[6]])
    S.op("act", I("activation", out=scT[:].rearrange("p a b -> p (a b)"), in_=banks[6][:, 0:8 * NS], func=AF.Silu),
         reads=[bn[6]], writes=["scT"])

    wq = {"n": 0}

    def xg(g):
        return "xT:%d" % g

    def norm_mod(AR, ia, ib, tag, lazy=False):
        sq = [AR.alloc("sq%d" % i, [128, 512], BF16) for i in range(2)]
        vs = [AR.alloc("v%d" % i, [128, 512], F32) for i in range(2)]
        rstds = [AR.alloc("rstd%d" % i, [128, 512], F32) for i in range(2)]
        tt = [AR.alloc("tt%d" % i, [128, 512], F32) for i in range(2)]

        def n_sq(g):
            t0, n = GROUPS[g]
            pb = 6 + (g % 2)
            for kc in range(8):
                s_ = sq[kc % 2]
                S.op("act", I("activation", out=s_[:, 0:n], in_=xT[:, kc, t0:t0 + n], func=AF.Square),
                     reads=[xg(g)], writes=["sq%d" % (kc % 2)])
                S.op("pe", I("matmul", banks[pb][:, 0:n], lhsT=onesb[:], rhs=s_[:, 0:n], start=(kc == 0), stop=(kc == 7)),
                     reads=["sq%d" % (kc % 2), "onesb"], writes=[bn[pb]])

        def n_rstd(g):
            t0, n = GROUPS[g]
            pb = 6 + (g % 2)
            v, rstd = vs[g % 2], rstds[g % 2]
            S.op("dve", I("tensor_scalar", out=v[:, 0:n], in0=banks[pb][:, 0:n], scalar1=1.0 / D, scalar2=1e-6,
                          op0=ALU.mult, op1=ALU.add), reads=[bn[pb]], writes=["v%d" % (g % 2)])
            S.op("act", I("activation", out=rstd[:, 0:n], in_=v[:, 0:n], func=AF.Sqrt), reads=["v%d" % (g % 2)], writes=["rstd%d" % (g % 2)])
            S.op("dve", I("reciprocal", out=rstd[:, 0:n], in_=rstd[:, 0:n]), writes=["rstd%d" % (g % 2)])

        def n_apply(g):
            t0, n = GROUPS[g]
            rstd = rstds[g % 2]
            rsn = "rstd%d" % (g % 2)
            for kc in range(8):
                t_ = tt[kc % 2]
                S.op("dve", I("tensor_tensor", out=t_[:, 0:n], in0=xT[:, kc, t0:t0 + n], in1=rstd[:, 0:n], op=ALU.mult),
                     reads=[xg(g), rsn], writes=["tt%d" % (kc % 2)])
                if g < 4:
                    S.op("act", I("activation", out=hT[:, kc, t0:t0 + n], in_=t_[:, 0:n], func=AF.Identity,
                                  scale=mod[:, ia, kc, 0:1], bias=mod[:, ib, kc, 0:1]),
                         reads=["tt%d" % (kc % 2), "mod"], writes=["hT:%d" % g])
                else:
                    tv = t_[:, 0:128].rearrange("p (s i) -> p s i", s=16)
                    S.op("dve", I("tensor_tensor", out=tv, in0=tv, in1=bc(mod[:, ia, kc, 1:17].unsqueeze(2), [128, 16, 8]), op=ALU.mult),
                         reads=["mod"], writes=["tt%d" % (kc % 2)])
                    S.op("dve", I("tensor_tensor", out=hT[:, kc, t0:t0 + n].rearrange("p (s i) -> p s i", s=16), in0=tv,
                                  in1=bc(mod[:, ib, kc, 1:17].unsqueeze(2), [128, 16, 8]), op=ALU.add),
                         reads=["mod", "tt%d" % (kc % 2)], writes=["hT:%d" % g])

        def norm_group(g):
            n_sq(g)
            n_rstd(g)
            n_apply(g)

        if lazy:
            return norm_group
        ng = len(GROUPS)
        n_sq(0)
        for g in range(ng):
            n_rstd(g)
            if g + 1 < ng:
                n_sq(g + 1)
            n_apply(g)

    def resid_add(pbank, d, g, ig):
        t0, n = GROUPS[g]
        if g < 4:
            S.op("dve", I("scalar_tensor_tensor", out=xT[:, d, t0:t0 + n], in0=banks[pbank][:, 0:n],
                                                          scalar=mod[:, ig, d, 0:1], in1=xT[:, d, t0:t0 + n],
                                                          op0=ALU.mult, op1=ALU.add),
                 reads=[bn[pbank], "mod", xg(g)], writes=[xg(g)])
        else:
            xv = xT[:, d, t0:t0 + n].rearrange("p (s i) -> p s i", s=16)
            pv = banks[pbank][:, 0:n].rearrange("p (s i) -> p s i", s=16)
            S.op("dve", I("tensor_tensor", out=pv, in0=pv, in1=bc(mod[:, ig, d, 1:17].unsqueeze(2), [128, 16, 8]),
                                                   op=ALU.mult), reads=[bn[pbank], "mod"], writes=[bn[pbank]])
            S.op("dve", I("tensor_tensor", out=xv, in0=xv, in1=pv, op=ALU.add),
                 reads=[bn[pbank], xg(g)], writes=[xg(g)])

    def ada_layer(l):
        if l > 0:
            S.barrier()
        S.phase = 'ada%d' % l
        AR = Arena(nc, PH_BASE if l > 0 else A0.cur, SB_END, "ada%d" % l)
        ring = [AR.alloc("w%d" % i, [128, 8, 768], BF16) for i in range(3)]
        adaT = AR.alloc("adaT", [128, 72, NS], F32)
        for blk in range(12):
            slot = wq["n"] % 3
            wq["n"] += 1
            wt = ring[slot]
            src = w_ada[l, :, blk * 768:(blk + 1) * 768].rearrange("(kc p) n -> p kc n", p=128)
            S.dma("pool", I("dma_start", out=wt[:], in_=src), key="wr%d" % slot, writes=["wr%d" % slot])
            pb = 4 + (blk % 2)
            for j6 in range(6):
                for kc in range(8):
                    S.op("pe", I("matmul", banks[pb][:, j6 * NS:(j6 + 1) * NS], lhsT=wt[:, kc, j6 * 128:(j6 + 1) * 128], rhs=scT[:, kc, :],
                        start=(kc == 0), stop=(kc == 7)), reads=["wr%d" % slot, "scT"], writes=[bn[pb]])
            j0 = blk * 6
            S.op("dve", I("tensor_tensor", out=adaT[:, j0:j0 + 6, :], in0=banks[pb][:, 0:6 * NS].rearrange("p (a b) -> p a b", a=6),
                in1=bc(vecT[:, vrow(l, "b_ada", j0):vrow(l, "b_ada", j0) + 6].unsqueeze(2), [128, 6, NS]), op=ALU.add),
                reads=[bn[pb], "vecT"], writes=["adaT"])
        for k, (gname, half) in enumerate((("g_ffn1", 0.5), ("g_mix", 1.0), ("g_ffn2", 0.5))):
            sh = adaT[:, (3 * k) * 8:(3 * k) * 8 + 8, :]
            sc = adaT[:, (3 * k + 1) * 8:(3 * k + 1) * 8 + 8, :]
            gt = adaT[:, (3 * k + 2) * 8:(3 * k + 2) * 8 + 8, :]
            gv = bc(vecT[:, vrow(l, gname):vrow(l, gname) + 8].unsqueeze(2), [128, 8, NS])
            S.op("dve", I("scalar_tensor_tensor", out=mod[:, 3 * k, :, :], in0=sc, scalar=1.0, in1=gv,
                                                                          op0=ALU.add, op1=ALU.mult),
                 reads=["adaT", "vecT"], writes=["mod"])
            S.op("dve", I("tensor_copy", out=mod[:, 3 * k + 1, :, :], in_=sh), reads=["adaT"], writes=["mod"])
            S.op("dve", I("tensor_scalar", out=mod[:, 3 * k + 2, :, :], in0=gt, scalar1=half,
                                                                       scalar2=None, op0=ALU.mult),
                 reads=["adaT"], writes=["mod"])

    def ffn(l, which):
        S.barrier()
        S.phase = 'ffn%d%d' % (l, which)
        AR = Arena(nc, MT_BASE, SB_END, "ffn%d%d" % (l, which))
        FS = 4
        ring = [(AR.alloc("w1_%d" % i, [128, 8, FS * 128], BF16), AR.alloc("w3_%d" % i, [128, 8, FS * 128], BF16),
                 AR.alloc("w2_%d" % i, [128, FS, D], BF16)) for i in range(2)]
        gT = AR.alloc("gT", [128, FS, NT], BF16)
        sl = [AR.alloc("s%d" % i, [128, 512], F32) for i in range(2)]
        k3 = 0 if which == 0 else 2
        ia, ib, ig = 3 * k3, 3 * k3 + 1, 3 * k3 + 2
        W1, W3, W2 = w1[which], w3[which], w2[which]
        stages = [(f0, min(FS, 22 - f0)) for f0 in range(0, 22, FS)]

        def load(si):
            f0, nf = stages[si]
            slot = si % 2
            a, b_, c_ = ring[slot]
            s1 = W1[l, :, f0 * 128:(f0 + nf) * 128].rearrange("(kc p) n -> p kc n", p=128)
            s3 = W3[l, :, f0 * 128:(f0 + nf) * 128].rearrange("(kc p) n -> p kc n", p=128)
            s2 = W2[l, f0 * 128:(f0 + nf) * 128, :].rearrange("(f p) n -> p f n", p=128)
            for dst, src in ((a[:, :, 0:nf * 128], s1), (b_[:, :, 0:nf * 128], s3), (c_[:, 0:nf, :], s2)):
                S.dma("pool", I("dma_start", out=dst, in_=src), key="fw%d" % slot, writes=["fw%d" % slot])

        load(0)
        norm_group = norm_mod(AR, ia, ib, "f", lazy=True)
        norm_group(0)
        norm_group(1)
        load(1)
        cnt = 0
        yc = 0
        for si, (f0, nf) in enumerate(stages):
            slot = si % 2
            a, b_, c_ = ring[slot]
            wn = "fw%d" % slot
            for g, (t0, n) in enumerate(GROUPS):
                if si == 0 and g + 2 < len(GROUPS):
                    norm_group(g + 2)
                for f in range(nf):
                    pu1, pu3 = (cnt % 2) * 2, (cnt % 2) * 2 + 1
                    s_ = sl[cnt % 2]
                    sn = "sl%d" % (cnt % 2)
                    cnt += 1
                    for kc in range(8):
                        S.op("pe", I("matmul", banks[pu1][:, 0:n], lhsT=a[:, kc, f * 128:(f + 1) * 128], rhs=hT[:, kc, t0:t0 + n],
                                     start=(kc == 0), stop=(kc == 7)), reads=[wn, "hT:%d" % g], writes=[bn[pu1]])
                    for kc in range(8):
                        S.op("pe", I("matmul", banks[pu3][:, 0:n], lhsT=b_[:, kc, f * 128:(f + 1) * 128], rhs=hT[:, kc, t0:t0 + n],
                                     start=(kc == 0), stop=(kc == 7)), reads=[wn, "hT:%d" % g], writes=[bn[pu3]])
                    S.op("act", I("activation", out=s_[:, 0:n], in_=banks[pu1][:, 0:n], func=AF.Silu), reads=[bn[pu1]], writes=[sn])
                    S.op("dve", I("tensor_tensor", out=gT[:, f, t0:t0 + n], in0=banks[pu3][:, 0:n], in1=s_[:, 0:n], op=ALU.mult),
                         reads=[bn[pu3], sn], writes=["gT:%d:%d" % (f, g)])
                for d in range(8):
                    pb = 4 + (yc % 4)
                    yc += 1
                    for f in range(nf):
                        S.op("pe", I("matmul", banks[pb][:, 0:n], lhsT=c_[:, f, d * 128:(d + 1) * 128], rhs=gT[:, f, t0:t0 + n],
                                     start=(f == 0), stop=(f == nf - 1)), reads=[wn, "gT:%d:%d" % (f, g)], writes=[bn[pb]])
                    resid_add(pb, d, g, ig)
            if si + 2 < len(stages):
                load(si + 2)

    COL = dict(q=0, k=512, v=1024, g=2048, ca=3072, cb=3584, sb=4096, scc=4608, sx=5120, gl=5632)
    BRS = os.environ.get("BR", "RSC")

    def wcols(l, c0, n):
        return w_in[l, :, c0:c0 + n].rearrange("(kc p) n -> p kc n", p=128)

    def merge(l, b, srcT, nk, Wout, first, inplace, base):
        S.barrier()
        S.phase = 'merge%d%d' % (l, b)
        AR = Arena(nc, base, SB_END, "mg%d%d" % (l, b))
        wo_t = AR.alloc("wo", [128, nk, D], BF16)
        wg_t = AR.alloc("wg", [128, 8, D], BF16)
        sg = [AR.alloc("sg%d" % i, [128, 512], F32) for i in range(2)]
        tmp = AR.alloc("tmp", [128, 512], F32)
        mtmp = AR.alloc("mtmp", [128, 8, 512], BF16) if inplace else None
        for j in range(4):
            S.dma("pool", I("dma_start", out=wo_t[:, :, j * 256:(j + 1) * 256],
                            in_=Wout[l, :, j * 256:(j + 1) * 256].rearrange("(kc p) n -> p kc n", p=128)), key="mwo%d" % j, writes=["mwo%d" % j])
            S.dma("pool", I("dma_start", out=wg_t[:, :, j * 256:(j + 1) * 256], in_=wcols(l, COL["gl"] + b * D + j * 256, 256)),
                  key="mwg%d" % j, writes=["mwg%d" % j])
        cnt = 0
        for g, (t0, n) in enumerate(GROUPS):
            for d in range(8):
                pa, pb = (cnt % 2) * 2, (cnt % 2) * 2 + 1
                sg_ = sg[cnt % 2]
                sgn = "msg%d" % (cnt % 2)
                cnt += 1
                for kc in range(nk):
                    S.op("pe", I("matmul", banks[pa][:, 0:n], lhsT=wo_t[:, kc, d * 128:(d + 1) * 128], rhs=srcT[:, kc, t0:t0 + n],
                                 start=(kc == 0), stop=(kc == nk - 1)), reads=["mwo%d" % (d // 2), "src:%d" % g], writes=[bn[pa]])
                for kc in range(8):
                    S.op("pe", I("matmul", banks[pb][:, 0:n], lhsT=wg_t[:, kc, d * 128:(d + 1) * 128], rhs=hT[:, kc, t0:t0 + n],
                                 start=(kc == 0), stop=(kc == 7)), reads=["mwg%d" % (d // 2), "hT:%d" % g], writes=[bn[pb]])
                br = vrow(l, "b_gate", b * 8 + d)
                S.op("act", I("activation", out=sg_[:, 0:n], in_=banks[pb][:, 0:n], func=AF.Sigmoid, bias=vecT[:, br:br + 1]),
                     reads=[bn[pb], "vecT"], writes=[sgn])
                if inplace:
                    S.op("dve", I("tensor_tensor", out=mtmp[:, d, 0:n], in0=banks[pa][:, 0:n], in1=sg_[:, 0:n], op=ALU.mult),
                         reads=[bn[pa], sgn], writes=["mtmp"])
                elif first:
                    S.op("dve", I("tensor_tensor", out=mT[:, d, t0:t0 + n], in0=banks[pa][:, 0:n], in1=sg_[:, 0:n], op=ALU.mult),
                         reads=[bn[pa], sgn], writes=["m:%d" % g])
                else:
                    S.op("dve", I("tensor_tensor", out=tmp[:, 0:n], in0=banks[pa][:, 0:n], in1=sg_[:, 0:n], op=ALU.mult),
                         reads=[bn[pa], sgn], writes=["mtmpf"])
                    S.op("dve", I("tensor_tensor", out=mT[:, d, t0:t0 + n], in0=mT[:, d, t0:t0 + n], in1=tmp[:, 0:n], op=ALU.add),
                         reads=["mtmpf", "m:%d" % g], writes=["m:%d" % g])
            if inplace:
                S.op("act", I("copy", out=mT[:, :, t0:t0 + n], in_=mtmp[:, :, 0:n]), reads=["mtmp", "src:%d" % g],
                     writes=["m:%d" % g, "src:%d" % g])


    LG = np.log(np.float32(1.0) - np.exp2(-5.0 - np.arange(8, dtype=np.float32))).astype(np.float32)

    def branch_r(l):
        S.barrier()
        S.phase = 'brR%d' % l
        AR = Arena(nc, PH_BASE, SB_END, "br%d" % l)
        WT_OFF = AR.cur
        wt = AR.alloc("wqkvg", [128, 8, 1536], BF16)
        S0all = nc.alloc_sbuf_tensor_at("br%d_S0all" % l, [128, 4, 8, 128], F32, offset=WT_OFF)
        dm = AR.alloc("dm", [128, 4, 128], F32)
        zm = AR.alloc("zm", [128, 8, 128], BF16)
        km = AR.alloc("km", [128, 16], BF16)
        qkd = AR.alloc("qkd", [128, 2, 2, 8], F32)
        rot = [AR.alloc("rot%d" % i, [128, 3, 32], F32) for i in range(2)]
        ta = AR.alloc("ta", [128, 512], F32)
        tb = AR.alloc("tb", [128, 512], F32)
        qkrot = AR.alloc("qkrot", [128, 512], BF16)
        qrot, krot = qkrot[:, 0:256], qkrot[:, 256:512]
        qkdt = [AR.alloc("qkdt%d" % i, [128, 512], BF16) for i in range(2)]
        qd = [t[:, 0:256] for t in qkdt]
        kd = [t[:, 256:512] for t in qkdt]
        nmr = AR.alloc("nmr", [128, 4], F32)
        qT = [AR.alloc("qT%d" % i, [64, 4, 128], BF16) for i in range(2)]
        qdT = [AR.alloc("qdT%d" % i, [64, 4, 128], BF16) for i in range(2)]
        kT = [AR.alloc("kT%d" % i, [64, 4, 128], BF16) for i in range(2)]
        vb = [AR.alloc("vb%d" % i, [128, 512], BF16) for i in range(2)]
        sg = AR.alloc("sg", [128, 512], F32)
        AT = AR.alloc("AT", [128, 4, 128], BF16)
        st6 = AR.alloc("st6", [128, 4, 6], F32)
        mv = AR.alloc("mv", [128, 4, 2], F32)
        rsd = AR.alloc("rsd", [128, 4], F32)
        rrs = [AR.alloc("rr%d" % i, [128, 512], BF16) for i in range(2)]
        Sst = AR.alloc("Sst", [64, 4, 128], F32)
        Sb = AR.alloc("Sb", [64, 4, 128], BF16)
        S0bs = [AR.alloc("S0b%d" % i, [128, 8, 128], BF16) for i in range(2)]
        Zqs = [AR.alloc("Zq%d" % i, [128, 8, 128], BF16) for i in range(2)]
        qdd = AR.alloc("qdd", [128, 2, 64], BF16)
        B3b = banks[3][:].bitcast(BF16)
        B4b = banks[4][:].bitcast(BF16)
        B5b = banks[5][:].bitcast(BF16)
        S.dma("pool", I("dma_start", out=zm[:], in_=zmask_d), key="c_zm", writes=["zm"])
        S.dma("pool", I("dma_start", out=km[:], in_=kmask_d), key="c_km", writes=["km"])
        S.dma("sp", I("dma_start", out=qkd[:], in_=qk_dec_d), key="c_qkd", writes=["qkd"])

        def stage_a(hh, ti, part):
            S.phase = 'brR%d:%d:%02d' % (l, hh, ti)
            g = ti // 4
            c0 = ti * 128
            pi = 0 if ti < 16 else 1
            i2 = ti % 2
            rt = rot[i2]
            rn = "rot%d" % i2
            if part == 1:
              S.dma("sp", I("dma_start", out=rt[:], in_=rot_d[ti]), key=rn, writes=[rn])
              for (pb, o0, n_, c_) in ((0, 0, 256, 0), (0, 256, 256, 256), (1, 512, 512, 0)):
                for kc in range(8):
                    S.op("pe", I("matmul", banks[pb][:, c_:c_ + n_], lhsT=hT[:, kc, c0:c0 + 128], rhs=wt[:, kc, o0:o0 + n_],
                                 start=(kc == 0), stop=(kc == 7)), reads=["rw", "hT:%d" % g], writes=[bn[pb]])
              S.op("act", I("copy", out=vb[i2][:], in_=banks[1][:]), reads=[bn[1]], writes=["vb%d" % i2])
            if part == 2:
                X = banks[0][:]
                X16 = X.rearrange("p (a r) -> p a r", r=32)
                X8 = X.rearrange("p (h t r) -> p h t r", h=8, t=2)
                ta16 = ta[:].rearrange("p (a r) -> p a r", r=32)
                tb8 = tb[:].rearrange("p (h t r) -> p h t r", h=8, t=2)
                S.op("dve", I("tensor_tensor", out=ta16, in0=X16, in1=bc(rt[:, 0, :].unsqueeze(1), [128, 16, 32]), op=ALU.mult),
                     reads=[bn[0], rn], writes=["ta"])
                S.op("dve", I("tensor_tensor", out=tb8[:, :, 0, :], in0=X8[:, :, 1, :], in1=bc(rt[:, 2, :].unsqueeze(1), [128, 8, 32]),
                              op=ALU.mult), reads=[bn[0], rn], writes=["tb"])
                S.op("dve", I("tensor_tensor", out=tb8[:, :, 1, :], in0=X8[:, :, 0, :], in1=bc(rt[:, 1, :].unsqueeze(1), [128, 8, 32]),
                              op=ALU.mult), reads=[bn[0], rn], writes=["tb"])
                S.op("dve", I("tensor_tensor", out=ta[:], in0=ta[:], in1=tb[:], op=ALU.add), reads=["tb"], writes=["ta"])
                S.op("act", I("copy", out=qkrot[:], in_=ta[:]), reads=["ta"], writes=["qkrot"])
                S.op("dve", I("tensor_tensor", out=qkdt[i2][:].rearrange("p (w h d) -> p w h d", w=2, h=4),
                              in0=ta[:].rearrange("p (w h d) -> p w h d", w=2, h=4),
                              in1=bc(qkd[:, pi, :, 4 * hh:4 * hh + 4].unsqueeze(3), [128, 2, 4, 64]), op=ALU.mult),
                     reads=["ta", "qkd"], writes=["qkdt%d" % i2])
            if part != 3:
                return
            for h in range(4):
                S.op("pe", I("transpose", out=B3b[0:64, h * 128:(h + 1) * 128], in_=qkrot[:, h * 64:(h + 1) * 64], identity=identb[:]),
                     reads=["qkrot", "identb"], writes=[bn[3]])
                S.op("pe", I("transpose", out=B3b[0:64, 512 + h * 128:512 + (h + 1) * 128], in_=qkdt[i2][:, h * 64:(h + 1) * 64],
                             identity=identb[:]), reads=["qkdt%d" % i2, "identb"], writes=[bn[3]])
                S.op("pe", I("transpose", out=B4b[0:64, h * 128:(h + 1) * 128], in_=qkrot[:, 256 + h * 64:256 + (h + 1) * 64], identity=identb[:]),
                     reads=["qkrot", "identb"], writes=[bn[4]])
            S.op("act", I("copy", out=qT[i2][:].rearrange("p h t -> p (h t)"), in_=B3b[0:64, 0:512]), reads=[bn[3]], writes=["qT%d" % i2])
            S.op("act", I("copy", out=qdT[i2][:].rearrange("p h t -> p (h t)"), in_=B3b[0:64, 512:1024]), reads=[bn[3]], writes=["qdT%d" % i2])
            S.op("act", I("copy", out=kT[i2][:].rearrange("p h t -> p (h t)"), in_=B4b[0:64, 0:512]), reads=[bn[4]], writes=["kT%d" % i2])

        def stage_b(hh, ti, part):
            S.phase = 'brR%d:%d:%02d' % (l, hh, ti)
            g = ti // 4
            c0 = ti * 128
            pi = 0 if ti < 16 else 1
            i2 = ti % 2
            vbn, qTn, qdTn, kTn, qdn, kdn = ("vb%d" % i2, "qT%d" % i2, "qdT%d" % i2, "kT%d" % i2, "qkdt%d" % i2, "qkdt%d" % i2)
            rr = rrs[i2]
            rrn = "rr%d" % i2
            if part == 1:
              if ti == 16:
                S.dma("sp", I("dma_start", out=dm[:], in_=dmask_d[1, :, 4 * hh:4 * hh + 4, :]), key="c_dm", writes=["dm"])
              for kc in range(8):
                S.op("pe", I("matmul", banks[2][:], lhsT=hT[:, kc, c0:c0 + 128], rhs=wt[:, kc, 1024:1536], start=(kc == 0), stop=(kc == 7)),
                     reads=["rw", "hT:%d" % g], writes=[bn[2]])
              S.op("act", I("activation", out=sg[:], in_=banks[2][:], func=AF.Silu), reads=[bn[2]], writes=["sg"])
              if ti == 16:
                for h in range(4):
                    for s2 in range(2):
                        S.dma("sp", I("dma_start", out=S0all[s2 * 64:(s2 + 1) * 64, h, :, :],
                                      in_=st_ret[l, s2::2, 4 * hh + h, :, :].rearrange("pr d e -> d pr e")), key="S0_%d" % h,
                              reads=([] if (h == 0 and s2 == 0) else ["rw"]), writes=(["S0_%d" % h, "rw"] if (h == 0 and s2 == 0) else ["S0_%d" % h]))
              for h in range(4):
                S.op("pe", I("matmul", banks[5][:, h * 128:(h + 1) * 128], lhsT=kT[i2][:, h, :], rhs=qT[i2][:, h, :], start=True, stop=True),
                     reads=[kTn, qTn], writes=[bn[5]])
              S.op("dve", I("tensor_tensor", out=AT[:].rearrange("p h t -> p (h t)"), in0=banks[5][:],
                          in1=dm[:].rearrange("p h t -> p (h t)"), op=ALU.mult), reads=[bn[5], "dm"], writes=["AT"])
            for h in (range(4) if part == 2 else ()):
                H = 4 * hh + h
                hb = slice(h * 128, (h + 1) * 128)
                if pi == 0:
                    S.op("pe", I("matmul", banks[6][:, hb], lhsT=AT[:, h, :], rhs=vb[i2][:, hb], start=True, stop=(ti == 0)),
                         reads=["AT", vbn], writes=[bn[6]])
                    if ti > 0:
                        S.op("pe", I("matmul", banks[6][:, hb], lhsT=qdT[i2][:, h, :], rhs=Sb[:, h, :], start=False, stop=True),
                             reads=[qdTn, "Sb"], writes=[bn[6]])
                    S.op("pe", I("matmul", banks[7][0:64, hb], lhsT=qkdt[i2][:, 256 + h * 64:256 + (h + 1) * 64], rhs=vb[i2][:, hb], start=True, stop=True),
                         reads=[kdn, vbn], writes=[bn[7]])
                else:
                    S0 = S0all[:, h, :, :]
                    S0n = "S0_%d" % h
                    S0b, S0bn = S0bs[h % 2], "S0b%d" % (h % 2)
                    Zq, Zqn = Zqs[h % 2], "Zq%d" % (h % 2)
                    Zk = Zq[:].rearrange("p (a b) d -> p a (b d)", b=1).rearrange("p a (s d) -> p (a s) d", d=64)
                    S.op("act", I("copy", out=S0b[:], in_=S0), reads=[S0n, "rw"], writes=[S0bn])
                    for t_ in range(2):
                        S.op("dve", I("tensor_copy", out=qdd[:, t_, :], in_=qkdt[i2][:, h * 64:(h + 1) * 64]), reads=[qdn], writes=["qdd"])
                    S.op("pe", I("transpose", out=B4b[:, 512:640], in_=qdd[:].rearrange("p t d -> p (t d)"), identity=identb[:]),
                         reads=["qdd", "identb"], writes=[bn[4]])
                    S.op("dve", I("tensor_tensor", out=Zq[:], in0=bc(B4b[:, 512:640].unsqueeze(1), [128, 8, 128]), in1=zm[:], op=ALU.mult),
                         reads=[bn[4], "zm"], writes=[Zqn])
                    S.op("pe", I("matmul", banks[6][:, hb], lhsT=AT[:, h, :], rhs=vb[i2][:, hb], start=True, stop=False),
                         reads=["AT", vbn], writes=[bn[6]])
                    for pr in range(8):
                        S.op("pe", I("matmul", banks[6][:, hb], lhsT=Zq[:, pr, :], rhs=S0b[:, pr, :], start=False, stop=(pr == 7)),
                             reads=[Zqn, S0bn], writes=[bn[6]])
                    S.op("dve", I("tensor_tensor", out=Zk, in0=bc(qkdt[i2][:, 256 + h * 64:256 + (h + 1) * 64].unsqueeze(1), [128, 16, 64]),
                                  in1=bc(km[:].unsqueeze(2), [128, 16, 64]), op=ALU.mult), reads=[kdn, "km"], writes=[Zqn])
                    for pr in range(8):
                        S.op("pe", I("matmul", banks[pr // 4][:, (pr % 4) * 128:(pr % 4 + 1) * 128], lhsT=Zq[:, pr, :], rhs=vb[i2][:, hb],
                                     start=True, stop=True), reads=[Zqn, vbn], writes=[bn[pr // 4]])
                    cd8 = float(np.exp(np.float32(8.0) * LG[H]))
                    for q_ in range(2):
                        S.op("dve", I("scalar_tensor_tensor", out=S0[:, 4 * q_:4 * q_ + 4, :].rearrange("p a e -> p (a e)"),
                                      in0=S0[:, 4 * q_:4 * q_ + 4, :].rearrange("p a e -> p (a e)"), scalar=cd8, in1=banks[q_][:],
                                      op0=ALU.mult, op1=ALU.add), reads=[bn[q_], "rw"], writes=[S0n])
                    for s2 in range(2):
                        finals.append(S.dma("sp", I("dma_start", out=rets[l, s2::2, H, :, :].rearrange("pr d e -> d pr e"),
                                                    in_=S0all[s2 * 64:(s2 + 1) * 64, h, :, :]), key="o_rets%d" % h, reads=[S0n, "rw"]))
            if pi == 0 and part == 2:
                for h in range(4):
                    H = 4 * hh + h
                    cd = float(np.exp(np.float32(128.0) * LG[H]))
                    S.op("dve", I("scalar_tensor_tensor", out=Sst[:, h, :], in0=Sst[:, h, :], scalar=cd,
                                  in1=banks[7][0:64, h * 128:(h + 1) * 128], op0=ALU.mult, op1=ALU.add),
                         reads=[bn[7]], writes=["Sst"])
                S.op("act", I("copy", out=Sb[:], in_=Sst[:]), reads=["Sst"], writes=["Sb"])
            if part == 4:
                for h in range(4):
                    S.op("pe", I("transpose", out=B4b[:, 512 + h * 128:512 + (h + 1) * 128], in_=rr[:, h * 128:(h + 1) * 128], identity=identb[:]),
                         reads=[rrn, "identb"], writes=[bn[4]])
                for h in range(4):
                    r = vrow(l, "gn", 4 * hh + h)
                    S.op("act", I("activation", out=mT[:, 4 * hh + h, c0:c0 + 128], in_=B4b[:, 512 + h * 128:512 + (h + 1) * 128], func=AF.Identity,
                                  scale=vecT[:, r:r + 1]), reads=[bn[4], "vecT"], writes=["src:%d" % g])
            if part != 3:
                return
            for h in range(4):
                S.op("dve", I("bn_stats", out=st6[:, h, :], in_=banks[6][:, h * 128:(h + 1) * 128]), reads=[bn[6]], writes=["st6"])
            for h in range(4):
                S.op("dve", I("bn_aggr", out=mv[:, h, :], in_=st6[:, h, :]), reads=["st6"], writes=["mv"])
            S.op("dve", I("tensor_scalar", out=rsd[:], in0=mv[:, :, 1], scalar1=1e-5, scalar2=None, op0=ALU.add), reads=["mv"], writes=["rsd"])
            S.op("pool", I("tensor_tensor", out=rsd[:], in0=rsd[:], in1=mhalf[:, 0:4], op=ALU.pow), reads=["mhalf"], writes=["rsd"])
            S.op("dve", I("scalar_tensor_tensor", out=nmr[:], in0=mv[:, :, 0], scalar=-1.0, in1=rsd[:], op0=ALU.mult, op1=ALU.mult),
                 reads=["mv", "rsd"], writes=["nmr"])
            for h in range(4):
                S.op("act", I("activation", out=AT[:, h, :], in_=banks[6][:, h * 128:(h + 1) * 128], func=AF.Identity,
                              scale=rsd[:, h:h + 1], bias=nmr[:, h:h + 1]), reads=[bn[6], "rsd", "nmr"], writes=["AT"])
            S.op("dve", I("tensor_tensor", out=rr[:], in0=AT[:].rearrange("p h t -> p (h t)"), in1=sg[:], op=ALU.mult),
                 reads=["AT", "sg"], writes=[rrn])

        for hh in range(2):
            for j, (c0, n_) in enumerate(((COL["q"] + hh * 256, 256), (COL["k"] + hh * 256, 256), (COL["v"] + hh * 512, 512),
                                          (COL["g"] + hh * 512, 512))):
                o0 = (0, 256, 512, 1024)[j]
                S.dma("pool", I("dma_start", out=wt[:, :, o0:o0 + n_], in_=wcols(l, c0, n_)), key="rw", writes=["rw"])
            S.dma("sp", I("dma_start", out=dm[:], in_=dmask_d[0, :, 4 * hh:4 * hh + 4, :]), key="c_dm", writes=["dm"])
            S.op("pool", I("memset", Sst[:], 0.0), writes=["Sst"])
            S.op("pool", I("memset", Sb[:], 0.0), writes=["Sb"])
            for part in (1, 2, 3):
                stage_a(hh, 0, part)
            for ti in range(17):
                nxt = ti + 1 < 17
                if nxt:
                    stage_a(hh, ti + 1, 1)
                stage_b(hh, ti, 1)
                if nxt:
                    stage_a(hh, ti + 1, 2)
                stage_b(hh, ti, 2)
                if nxt:
                    stage_a(hh, ti + 1, 3)
                if ti > 0:
                    stage_b(hh, ti - 1, 4)
                stage_b(hh, ti, 3)
            stage_b(hh, 16, 4)
            finals.append(S.dma("sp", I("dma_start", out=retp[l, 4 * hh:4 * hh + 4].rearrange("h d e -> d h e"), in_=Sst[:]), key="o_retp",
                                reads=["Sst"]))
        merge(l, 0, mT, 8, w_ret_out, True, True, PH_BASE)

    def branch_s(l, first):
        S.barrier()
        S.phase = 'brS%d' % l
        AR = Arena(nc, PH_BASE, SB_END, "bs%d" % l)
        pST = AR.alloc("pST", [128, 4, NT], BF16)
        S_BASE = AR.cur
        ring = [AR.alloc("w%d" % i, [128, 8, 384], BF16) for i in range(3)]
        uspad = AR.alloc("uspad", [128, 2 + NP], F32)
        us_s = AR.alloc("us_s", [128, 4, 16, 10], F32)
        sxs = [AR.alloc("sx%d" % i, [128, 512], F32) for i in range(2)]
        acc = [AR.alloc("acc%d" % i, [128, 512], F32) for i in range(2)]
        stst = AR.alloc("stst", [32, 512], F32)
        so_p = AR.alloc("so_p", [2, 512], F32)
        ust = AR.alloc("ust", [128, 4, 32], F32)
        so_s = AR.alloc("so_s", [32, 512], F32)
        S.dma("sp", I("dma_start", out=stst[:], in_=st_sc[l].rearrange("s i c -> (s i) c")), key="stst", writes=["stst"])
        for cc in range(4):
            S.op("pe", I("transpose", out=banks[7][:, cc * 32:(cc + 1) * 32], in_=stst[:, cc * 128:(cc + 1) * 128],
                         identity=ident[0:32, 0:32]), reads=["stst", "ident"], writes=[bn[7]])
        S.op("dve", I("tensor_copy", out=us_s[:, :, :, 0:2], in_=banks[7][:, 0:128].rearrange("p (c s i) -> p c s i", c=4, s=16)),
             reads=[bn[7]], writes=["us_s"])
        S.op("pool", I("memset", uspad[:, 0:2], 0.0), writes=["uspad"])
        cnt = 0
        for cc in range(4):
            slot = wq["n"] % 3
            wq["n"] += 1
            wt = ring[slot]
            wn = "wr%d" % slot
            for j, nm in enumerate(("sb", "scc", "sx")):
                S.dma("pool", I("dma_start", out=wt[:, :, j * 128:(j + 1) * 128], in_=wcols(l, COL[nm] + cc * 128, 128)), key=wn, writes=[wn])
            for g, (t0, n) in enumerate(GROUPS):
                i2 = cnt % 2
                cnt += 1
                B0, B1, B2 = 3 * i2, 3 * i2 + 1, 3 * i2 + 2
                for j in range(3):
                    for kc in range(8):
                        S.op("pe", I("matmul", banks[3 * i2 + j][:, 0:n], lhsT=wt[:, kc, j * 128:(j + 1) * 128], rhs=hT[:, kc, t0:t0 + n],
                                     start=(kc == 0), stop=(kc == 7)), reads=[wn, "hT:%d" % g], writes=[bn[3 * i2 + j]])
                S.op("act", I("copy", out=sxs[i2][:, 0:n], in_=banks[B2][:, 0:n]), reads=[bn[B2]], writes=["sxs%d" % i2])
                if g < 4:
                    S.op("dve", I("tensor_tensor", out=uspad[:, 2 + t0:2 + t0 + n], in0=banks[B1][:, 0:n], in1=sxs[i2][:, 0:n], op=ALU.mult),
                         reads=[bn[B1], "sxs%d" % i2], writes=["uspad"])
                    taps = [uspad[:, t0 + k:t0 + k + n] for k in range(3)]
                    accv = acc[i2][:, 0:n]
                    pv = banks[B0][:, 0:n]
                    ov = pST[:, cc, t0:t0 + n]
                    srcn = "uspad"
                else:
                    S.op("dve", I("tensor_tensor", out=us_s[:, cc, :, 2:10], in0=banks[B1][:, 0:n].rearrange("p (s i) -> p s i", s=16),
                                  in1=sxs[i2][:, 0:n].rearrange("p (s i) -> p s i", s=16), op=ALU.mult),
                         reads=[bn[B1], "sxs%d" % i2], writes=["us_s"])
                    taps = [us_s[:, cc, :, k:k + 8] for k in range(3)]
                    accv = acc[i2][:, 0:n].rearrange("p (s i) -> p s i", s=16)
                    pv = banks[B0][:, 0:n].rearrange("p (s i) -> p s i", s=16)
                    ov = pST[:, cc, t0:t0 + n].rearrange("p (s i) -> p s i", s=16)
                    srcn = "us_s"
                wr = [vrow(l, "scw", k * 4 + cc) for k in range(3)]
                S.op("dve", I("tensor_scalar", out=accv, in0=taps[2], scalar1=vecT[:, wr[2]:wr[2] + 1], scalar2=None, op0=ALU.mult),
                     reads=[srcn, "vecT"], writes=["acc%d" % i2])
                for k in (1, 0):
                    S.op("dve", I("scalar_tensor_tensor", out=accv, in0=taps[k], scalar=vecT[:, wr[k]:wr[k] + 1], in1=accv,
                                  op0=ALU.mult, op1=ALU.add), reads=[srcn, "vecT"], writes=["acc%d" % i2])
                S.op("dve", I("tensor_tensor", out=ov, in0=pv, in1=accv, op=ALU.mult), reads=[bn[B0], "acc%d" % i2], writes=["src:%d" % g])
            S.op("pe", I("transpose", out=banks[6][0:2, cc * 128:(cc + 1) * 128], in_=uspad[:, NP:NP + 2], identity=ident[:]),
                 reads=["uspad", "ident"], writes=[bn[6]])
            S.op("act", I("copy", out=ust[:, cc, :].rearrange("p (s i) -> p s i", s=16), in_=us_s[:, cc, :, 8:10]), reads=["us_s"], writes=["ust"])
            S.op("pe", I("transpose", out=banks[7][0:32, cc * 128:(cc + 1) * 128], in_=ust[:, cc, :], identity=ident[:]),
                 reads=["ust", "ident"], writes=[bn[7]])
        S.op("act", I("copy", out=so_p[:], in_=banks[6][0:2, :]), reads=[bn[6]], writes=["so_p"])
        S.op("act", I("copy", out=so_s[:], in_=banks[7][0:32, :]), reads=[bn[7]], writes=["so_s"])
        finals.append(S.dma("sp", I("dma_start", out=scp[l], in_=so_p[:]), key="o_scp", reads=["so_p"]))
        finals.append(S.dma("sp", I("dma_start", out=scs[l].rearrange("s i c -> (s i) c"), in_=so_s[:]), key="o_scs", reads=["so_s"]))
        merge(l, 2, pST, 4, w_sc_out, first, False, S_BASE)

    def branch_c(l, first):
        S.barrier()
        S.phase = 'brC%d' % l
        AR = Arena(nc, PH_BASE, SB_END, "bc%d" % l)
        cvb = AR.alloc("cvb", [128, 4, NT], BF16)
        LN_BASE = AR.cur
        ring = [AR.alloc("w%d" % i, [128, 8, 256], BF16) for i in range(2)]
        upad = AR.alloc("upad", [128, 30 + NP], BF16)
        us_c = AR.alloc("us_c", [128, 4, 16, 38], BF16)
        usf = AR.alloc("usf", [128, 4, 128], F32)
        u32p = AR.alloc("u32p", [128, 4, 32], F32)
        dg = AR.alloc("dg", [128, 31, 128], BF16)
        sgs = [AR.alloc("sg%d" % i, [128, 512], F32) for i in range(2)]
        uf = [AR.alloc("uf%d" % i, [128, 512], F32) for i in range(2)]
        cst4 = [AR.alloc("cst%d" % i, [120, 512], F32) for i in range(2)]
        co_p = AR.alloc("co_p", [32, 512], F32)
        co_s = AR.alloc("co_s", [128, 512], F32)
        for j in range(4):
            cs = cst4[j % 2]
            S.dma("sp", I("dma_start", out=cs[:], in_=st_conf[l, 4 * j:4 * j + 4].rearrange("s r c -> (s r) c")), key="cst4%d" % (j % 2),
                  writes=["cst4%d" % (j % 2)])
            pb = 6 + (j % 2)
            for cc in range(4):
                S.op("pe", I("transpose", out=banks[pb][:, cc * 120:(cc + 1) * 120], in_=cs[:, cc * 128:(cc + 1) * 128],
                             identity=ident[0:120, 0:120]), reads=["cst4%d" % (j % 2), "ident"], writes=[bn[pb]])
            S.op("dve", I("tensor_copy", out=us_c[:, :, 4 * j:4 * j + 4, 0:30],
                          in_=banks[pb][:, 0:480].rearrange("p (c s r) -> p c s r", c=4, s=4)), reads=[bn[pb]], writes=["us_c"])
        finals.append(S.dma("sp", I("dma_start", out=confs[l, :, 0:22, :], in_=st_conf[l, :, 8:30, :]), key="o_cfs0"))
        S.op("pool", I("memset", upad[:, 0:30], 0.0), writes=["upad"])
        cnt = 0
        for cc in range(4):
            slot = cc % 2
            wt = ring[slot]
            wn = "cw%d" % slot
            for j, nm in enumerate(("ca", "cb")):
                S.dma("pool", I("dma_start", out=wt[:, :, j * 128:(j + 1) * 128], in_=wcols(l, COL[nm] + cc * 128, 128)), key=wn, writes=[wn])
            for k in range(31):
                r = vrow(l, "ccw", k * 4 + cc)
                S.op("dve", I("tensor_scalar", out=dg[:, k, :], in0=identb[:], scalar1=vecT[:, r:r + 1], scalar2=None, op0=ALU.mult),
                     reads=["identb", "vecT"], writes=["dg"])
            for g, (t0, n) in enumerate(GROUPS):
                i2 = cnt % 2
                cnt += 1
                Ba, Bb = 4 * i2, 4 * i2 + 1
                for j in range(2):
                    for kc in range(8):
                        S.op("pe", I("matmul", banks[4 * i2 + j][:, 0:n], lhsT=wt[:, kc, j * 128:(j + 1) * 128], rhs=hT[:, kc, t0:t0 + n],
                                     start=(kc == 0), stop=(kc == 7)), reads=[wn, "hT:%d" % g], writes=[bn[4 * i2 + j]])
                S.op("act", I("activation", out=sgs[i2][:, 0:n], in_=banks[Bb][:, 0:n], func=AF.Sigmoid), reads=[bn[Bb]], writes=["csg%d" % i2])
                S.op("dve", I("tensor_tensor", out=uf[i2][:, 0:n], in0=banks[Ba][:, 0:n], in1=sgs[i2][:, 0:n], op=ALU.mult),
                     reads=[bn[Ba], "csg%d" % i2], writes=["uf%d" % i2])
                if g < 4:
                    S.op("act", I("copy", out=upad[:, 30 + t0:30 + t0 + n], in_=uf[i2][:, 0:n]), reads=["uf%d" % i2], writes=["upad"])
                    if g == 3:
                        S.op("act", I("copy", out=u32p[:, cc, :], in_=uf[i2][:, 480:512]), reads=["uf%d" % i2], writes=["u32p"])
                else:
                    S.op("act", I("copy", out=us_c[:, cc, :, 30:38], in_=uf[i2][:, 0:n].rearrange("p (s i) -> p s i", s=16)),
                         reads=["uf%d" % i2], writes=["us_c"])
                    S.op("act", I("copy", out=usf[:, cc, :], in_=uf[i2][:, 0:n]), reads=["uf%d" % i2], writes=["usf"])
            r = vrow(l, "ccb", cc)
            for g, (t0, n) in enumerate(GROUPS[:4]):
                pb = 2 + (g % 2)
                for k in range(31):
                    S.op("pe", I("matmul", banks[pb][:, 0:n], lhsT=dg[:, k, :], rhs=upad[:, t0 + k:t0 + k + n], start=(k == 0), stop=(k == 30)),
                         reads=["dg", "upad"], writes=[bn[pb]])
                S.op("act", I("activation", out=cvb[:, cc, t0:t0 + n], in_=banks[pb][:, 0:n], func=AF.Identity, bias=vecT[:, r:r + 1]),
                     reads=[bn[pb], "vecT"], writes=["src:%d" % g])
            flat = us_c[:, cc, :, :].rearrange("p s r -> p (s r)")
            for hf in range(2):
                pb = 2 + hf
                for k in range(31):
                    S.op("pe", I("matmul", banks[pb][:, 0:274], lhsT=dg[:, k, :], rhs=flat[:, hf * 304 + k:hf * 304 + k + 274],
                                 start=(k == 0), stop=(k == 30)), reads=["dg", "us_c"], writes=[bn[pb]])
                S.op("act", I("activation", out=cvb[:, cc, NP + hf * 64:NP + (hf + 1) * 64].rearrange("p (s i) -> p s i", s=8),
                              in_=banks[pb][:, 0:304].rearrange("p (s r) -> p s r", s=8)[:, :, 0:8], func=AF.Identity, bias=vecT[:, r:r + 1]),
                     reads=[bn[pb], "vecT"], writes=["src:4"])
        for cc in range(4):
            S.op("pe", I("transpose", out=banks[6][0:32, cc * 128:(cc + 1) * 128], in_=u32p[:, cc, :], identity=ident[:]),
                 reads=["u32p", "ident"], writes=[bn[6]])
            S.op("pe", I("transpose", out=banks[7][:, cc * 128:(cc + 1) * 128], in_=usf[:, cc, :], identity=ident[:]),
                 reads=["usf", "ident"], writes=[bn[7]])
        S.op("act", I("copy", out=co_p[:], in_=banks[6][0:32, :]), reads=[bn[6]], writes=["co_p"])
        S.op("act", I("copy", out=co_s[:], in_=banks[7][:]), reads=[bn[7]], writes=["co_s"])
        finals.append(S.dma("sp", I("dma_start", out=confp[l], in_=co_p[2:32, :]), key="o_cfp", reads=["co_p"]))
        for s_ in range(16):
            finals.append(S.dma("sp", I("dma_start", out=confs[l, s_, 22:30, :], in_=co_s[s_ * 8:(s_ + 1) * 8, :]), key="o_cfs", reads=["co_s"]))
        S.barrier()
        S.phase = 'brCln%d' % l
        AR = Arena(nc, LN_BASE, SB_END, "bcl%d" % l)
        sqb = [AR.alloc("sqb%d" % i, [128, 512], BF16) for i in range(2)]
        means = [AR.alloc("mean%d" % i, [128, 512], F32) for i in range(2)]
        var = AR.alloc("var", [128, 512], F32)
        rss = [AR.alloc("rs%d" % i, [128, 512], F32) for i in range(2)]
        tt = [AR.alloc("tt%d" % i, [128, 512], F32) for i in range(2)]

        def ln_stats(g):
            t0, n = GROUPS[g]
            p2 = g % 2
            mean, rs = means[p2], rss[p2]
            ba, bb = 4 + 2 * p2, 5 + 2 * p2
            for cc in range(4):
                S.op("act", I("activation", out=sqb[cc % 2][:, 0:n], in_=cvb[:, cc, t0:t0 + n], func=AF.Square),
                     reads=["src:%d" % g], writes=["sqb%d" % (cc % 2)])
                S.op("pe", I("matmul", banks[ba][:, 0:n], lhsT=onesb[:], rhs=cvb[:, cc, t0:t0 + n], start=(cc == 0), stop=(cc == 3)),
                     reads=["src:%d" % g, "onesb"], writes=[bn[ba]])
                S.op("pe", I("matmul", banks[bb][:, 0:n], lhsT=onesb[:], rhs=sqb[cc % 2][:, 0:n], start=(cc == 0), stop=(cc == 3)),
                     reads=["sqb%d" % (cc % 2), "onesb"], writes=[bn[bb]])
            S.op("dve", I("tensor_scalar", out=mean[:, 0:n], in0=banks[ba][:, 0:n], scalar1=1.0 / 512, scalar2=None, op0=ALU.mult),
                 reads=[bn[ba]], writes=["mean%d" % p2])
            S.op("dve", I("tensor_tensor", out=var[:, 0:n], in0=mean[:, 0:n], in1=mean[:, 0:n], op=ALU.mult), reads=["mean%d" % p2], writes=["var"])
            S.op("dve", I("scalar_tensor_tensor", out=var[:, 0:n], in0=banks[bb][:, 0:n], scalar=1.0 / 512, in1=var[:, 0:n],
                          op0=ALU.mult, op1=ALU.subtract), reads=[bn[bb]], writes=["var"])
            S.op("dve", I("tensor_scalar", out=var[:, 0:n], in0=var[:, 0:n], scalar1=1e-5, scalar2=None, op0=ALU.add), writes=["var"])
            S.op("act", I("activation", out=rs[:, 0:n], in_=var[:, 0:n], func=AF.Sqrt), reads=["var"], writes=["rs%d" % p2])
            S.op("dve", I("reciprocal", out=rs[:, 0:n], in_=rs[:, 0:n]), writes=["rs%d" % p2])

        def ln_apply(g):
            t0, n = GROUPS[g]
            p2 = g % 2
            mean, rs = means[p2], rss[p2]
            for cc in range(4):
                t_ = tt[cc % 2]
                S.op("dve", I("tensor_tensor", out=t_[:, 0:n], in0=cvb[:, cc, t0:t0 + n], in1=mean[:, 0:n], op=ALU.subtract),
                     reads=["src:%d" % g, "mean%d" % p2], writes=["ctt%d" % (cc % 2)])
                S.op("dve", I("tensor_tensor", out=t_[:, 0:n], in0=t_[:, 0:n], in1=rs[:, 0:n], op=ALU.mult), reads=["rs%d" % p2],
                     writes=["ctt%d" % (cc % 2)])
                rg, rb = vrow(l, "lng", cc), vrow(l, "lnb", cc)
                S.op("act", I("activation", out=cvb[:, cc, t0:t0 + n], in_=t_[:, 0:n], func=AF.Silu, scale=vecT[:, rg:rg + 1],
                              bias=vecT[:, rb:rb + 1]), reads=["ctt%d" % (cc % 2), "vecT"], writes=["src:%d" % g])

        ln_stats(0)
        for g in range(len(GROUPS)):
            if g + 1 < len(GROUPS):
                ln_stats(g + 1)
            ln_apply(g)
        merge(l, 1, cvb, 4, w_conf_out, first, False, LN_BASE)

    def wo_resid(l):
        S.barrier()
        S.phase = 'wo%d' % l
        AR = Arena(nc, PH_BASE, SB_END, "wo%d" % l)
        wt = AR.alloc("wo", [128, 8, D], BF16)
        for j in range(4):
            S.dma("pool", I("dma_start", out=wt[:, :, j * 256:(j + 1) * 256],
                            in_=w_o[l, :, j * 256:(j + 1) * 256].rearrange("(kc p) n -> p kc n", p=128)), key="mwo%d" % j, writes=["mwo%d" % j])
        c = 0
        for g, (t0, n) in enumerate(GROUPS):
            for d in range(8):
                pb = 4 + (c % 4)
                c += 1
                for kc in range(8):
                    S.op("pe", I("matmul", banks[pb][:, 0:n], lhsT=wt[:, kc, d * 128:(d + 1) * 128], rhs=mT[:, kc, t0:t0 + n],
                                 start=(kc == 0), stop=(kc == 7)), reads=["mwo%d" % (d // 2), "m:%d" % g], writes=[bn[pb]])
                resid_add(pb, d, g, 5)

    def mixer(l):
        S.barrier()
        S.phase = 'mixnorm%d' % l
        AR = Arena(nc, PH_BASE, SB_END, "mx%d" % l)
        norm_mod(AR, 3, 4, "m")
        first = True
        if "R" in BRS:
            branch_r(l)
            first = False
        if "S" in BRS:
            branch_s(l, first)
            first = False
        if "C" in BRS:
            branch_c(l, first)
            first = False
        wo_resid(l)

    for l in range(n_layers):
        ada_layer(l)
        ffn(l, 0)
        if mixer_on:
            mixer(l)
        ffn(l, 1)

    S.barrier()
    S.phase = 'final'
    AR = Arena(nc, PH_BASE, SB_END, "fin")
    sq = [AR.alloc("sq%d" % i, [128, 512], BF16) for i in range(2)]
    v = AR.alloc("v", [128, 512], F32)
    rstd = AR.alloc("rstd", [128, 512], F32)
    yT = AR.alloc("yT", [128, 8, 512], F32)
    ost = [AR.alloc("ost%d" % i, [128, D], F32) for i in range(2)]
    gfin = 2 * VR_PER_LAYER
    oc = 0
    for g, (t0, n) in enumerate(GROUPS):
        pb = 6 + (g % 2)
        for kc in range(8):
            s_ = sq[kc % 2]
            S.op("act", I("activation", out=s_[:, 0:n], in_=xT[:, kc, t0:t0 + n], func=AF.Square),
                 reads=[xg(g)], writes=["sq%d" % (kc % 2)])
            S.op("pe", I("matmul", banks[pb][:, 0:n], lhsT=onesb[:], rhs=s_[:, 0:n],
                                                             start=(kc == 0), stop=(kc == 7)),
                 reads=["sq%d" % (kc % 2), "onesb"], writes=[bn[pb]])
        S.op("dve", I("tensor_scalar", out=v[:, 0:n], in0=banks[pb][:, 0:n], scalar1=1.0 / D, scalar2=1e-6,
                                                      op0=ALU.mult, op1=ALU.add), reads=[bn[pb]], writes=["v"])
        S.op("act", I("activation", out=rstd[:, 0:n], in_=v[:, 0:n], func=AF.Sqrt), reads=["v"], writes=["rstd"])
        S.op("dve", I("reciprocal", out=rstd[:, 0:n], in_=rstd[:, 0:n]), writes=["rstd"])
        for kc in range(8):
            S.op("dve", I("scalar_tensor_tensor", out=yT[:, kc, 0:n], in0=xT[:, kc, t0:t0 + n],
                                                               scalar=vecT[:, gfin + kc:gfin + kc + 1], in1=rstd[:, 0:n],
                                                               op0=ALU.mult, op1=ALU.mult),
                 reads=[xg(g), "rstd", "vecT"], writes=["yT"])
        for ti in range(n // 128):
            o_ = ost[oc % 2]
            on = "ost%d" % (oc % 2)
            for hb in range(2):
                pbk = (oc % 2) * 2 + hb
                for c4 in range(4):
                    c = hb * 4 + c4
                    S.op("pe", I("transpose", out=banks[pbk][:, c4 * 128:(c4 + 1) * 128], in_=yT[:, c, ti * 128:(ti + 1) * 128], identity=ident[:]),
                        reads=["yT", "ident"], writes=[bn[pbk]])
                if hb == 0:
                    S.op("act", I("copy", out=o_[:, 0:512], in_=banks[pbk][:]), reads=[bn[pbk]], writes=[on])
                else:
                    S.op("dve", I("tensor_copy", out=o_[:, 512:1024], in_=banks[pbk][:]),
                         reads=[bn[pbk]], writes=[on])
            tg = t0 // 128 + ti
            dst = yp[tg * 128:(tg + 1) * 128, :] if tg < 16 else ys
            finals.append(S.dma("sp", I("dma_start", out=dst, in_=o_[:]), key="o%d" % (oc % 2), reads=[on]))
            oc += 1

    last = {}
    for t in finals:
        last[(t[0], t[1])] = t
    S.emit(final_waits=list(last.values()))
    return nc


def pack_vecs(inp):
    rows = []
    for l in range(L):
        rows.append(inp["b_ada"][l].reshape(72, 128))
        rows.append(inp["g_ffn1"][l].reshape(8, 128))
        rows.append(inp["g_mix"][l].reshape(8, 128))
        rows.append(inp["g_ffn2"][l].reshape(8, 128))
        rows.append(inp["b_gate"][l].reshape(24, 128))
        rows.append(inp["ret_gn_g"][l].reshape(8, 128))
        rows.append(inp["conf_conv_w"][l].reshape(124, 128))
        rows.append(inp["conf_conv_b"][l].reshape(4, 128))
        rows.append(inp["conf_ln_g"][l].reshape(4, 128))
        rows.append(inp["conf_ln_b"][l].reshape(4, 128))
        rows.append(inp["sc_conv_w"][l].reshape(12, 128))
    rows.append(inp["g_final"].reshape(8, 128))
    v = np.ascontiguousarray(np.concatenate(rows, axis=0), dtype=np.float32)
    assert v.shape == (VR_TOTAL, 128)
    return v


_NC_CACHE = {}


def make_consts():
    f32 = np.float32
    lg = np.log(f32(1.0) - np.exp2(-5.0 - np.arange(8, dtype=f32))).astype(f32)
    freqs = (f32(10000.0) ** (-np.arange(32, dtype=f32) / f32(32))).astype(f32)
    p = np.arange(128)
    rot = np.zeros((17, 128, 3, 32), f32)
    for ti in range(17):
        pos = (ti * 128 + p).astype(f32) if ti < 16 else (16384 + (p % 8)).astype(f32)
        ang = (pos[:, None] * freqs[None, :]).astype(f32)
        rot[ti, :, 0] = np.cos(ang)
        rot[ti, :, 1] = np.sin(ang)
        rot[ti, :, 2] = -np.sin(ang)
    dmask = np.zeros((2, 128, 8, 128), f32)
    jj, ii = np.meshgrid(p, p, indexing="ij")
    diff = (ii - jj).astype(f32)
    for h in range(8):
        dec = np.where(diff >= 0, np.exp(np.maximum(diff, 0) * lg[h]), 0.0).astype(f32)
        dmask[0, :, h, :] = dec * f32(0.125)
        d8 = ((ii % 8) - (jj % 8)).astype(f32)
        same = (ii // 8) == (jj // 8)
        dmask[1, :, h, :] = np.where(same & (d8 >= 0), np.exp(np.maximum(d8, 0) * lg[h]), 0.0).astype(f32) * f32(0.125)
    qkdec = np.zeros((128, 2, 2, 8), f32)
    for h in range(8):
        qkdec[:, 0, 0, h] = np.exp((p + 1).astype(f32) * lg[h]) * f32(0.125)
        qkdec[:, 0, 1, h] = np.exp((127 - p).astype(f32) * lg[h])
        qkdec[:, 1, 0, h] = np.exp(((p % 8) + 1).astype(f32) * lg[h]) * f32(0.125)
        qkdec[:, 1, 1, h] = np.exp((7 - (p % 8)).astype(f32) * lg[h])
    zmask = np.zeros((128, 8, 128), f32)
    for pr in range(8):
        for s2 in range(2):
            zmask[s2 * 64:(s2 + 1) * 64, pr, (2 * pr + s2) * 8:(2 * pr + s2 + 1) * 8] = 1.0
    kmask = np.zeros((128, 16), f32)
    kmask[p, p // 8] = 1.0
    return dict(rot=rot, dmask=dmask, qkdec=qkdec, zmask=zmask, kmask=kmask)


def make_in_maps(inp):
    vecs = pack_vecs(inp)
    ident = np.eye(128, dtype=np.float32)
    consts = make_consts()
    maps = []
    for b in range(8):
        m = dict(
            xp=np.ascontiguousarray(inp["x_prompt"][b]),
            xs=np.ascontiguousarray(inp["x_sample"][16 * b:16 * b + 16].reshape(128, D)),
            cc=np.ascontiguousarray(np.concatenate([inp["c_prompt"][b:b + 1], inp["c_sample"][16 * b:16 * b + 16]], axis=0)),
            vecs=vecs, ident=ident,
            w_ada=inp["w_ada"], w1_a=inp["w1_a"], w3_a=inp["w3_a"], w2_a=inp["w2_a"],
            w1_b=inp["w1_b"], w3_b=inp["w3_b"], w2_b=inp["w2_b"],
            w_in=inp["w_in"], w_ret_out=inp["w_ret_out"], w_conf_out=inp["w_conf_out"], w_sc_out=inp["w_sc_out"], w_o=inp["w_o"],
            st_ret=np.ascontiguousarray(inp["state_ret"][:, 16 * b:16 * b + 16]),
            st_conf=np.ascontiguousarray(inp["state_conf"][:, 16 * b:16 * b + 16]),
            st_sc=np.ascontiguousarray(inp["state_sconv"][:, 16 * b:16 * b + 16]),
            **consts,
        )
        maps.append(m)
    return maps


def kernel(**inputs):
    inp = {k: np.asarray(v) for k, v in inputs.items()}
    if "nc" not in _NC_CACHE:
        _NC_CACHE["nc"] = build_program()
    nc = _NC_CACHE["nc"]
    maps = make_in_maps(inp)
    res = run_bass_kernel_spmd(nc, maps, core_ids=list(range(8)))
    r = res.results
    y_prompt = np.stack([r[b]["yp"] for b in range(8)], axis=0)
    y_sample = np.concatenate([r[b]["ys"].reshape(16, 8, D) for b in range(8)], axis=0)
    ret_p = np.stack([r[b]["retp"] for b in range(8)], axis=1)
    ret_s = np.concatenate([r[b]["rets"] for b in range(8)], axis=1)
    conf_p = np.stack([r[b]["confp"] for b in range(8)], axis=1)
    conf_s = np.concatenate([r[b]["confs"] for b in range(8)], axis=1)
    sc_p = np.stack([r[b]["scp"] for b in range(8)], axis=1)
    sc_s = np.concatenate([r[b]["scs"] for b in range(8)], axis=1)
    return (y_prompt, y_sample, ret_p, ret_s, conf_p, conf_s, sc_p, sc_s)
```

```python
import os
import numpy as np
import concourse.bass as bass
import concourse.mybir as mybir
from concourse.bass_utils import run_bass_kernel_spmd

F32 = mybir.dt.float32
BF16 = mybir.dt.bfloat16
AF = mybir.ActivationFunctionType
ALU = mybir.AluOpType

D = 1024
DFF = 2816
NT = 2176
NP = 2048
NS = 17
L = 2
INC = 8704
GROUPS = [(0, 512), (512, 512), (1024, 512), (1536, 512), (2048, 128)]
SB_BASE = 16640
SB_END = 229376
ANNOTATE = bool(int(os.environ.get('ANNOTATE', '0')))

VR_PER_LAYER = 276
VR_TOTAL = 2 * VR_PER_LAYER + 8
VO = dict(b_ada=0, g_ffn1=72, g_mix=80, g_ffn2=88, b_gate=96, gn=120, ccw=128, ccb=252, lng=256, lnb=260, scw=264)


def vrow(l, name, i=0):
    return l * VR_PER_LAYER + VO[name] + i


class Sched:
    ENGS = ("pe", "act", "dve", "pool", "sp")

    def __init__(self, nc):
        self.nc = nc
        self.ops = {e: [] for e in self.ENGS}
        self.cnt = {e: 0 for e in self.ENGS}
        self.last_w = {}
        self.readers = {}
        self.dma_sems = {}
        self.eng_sem = {}
        self.bar = []
        self.phase = 'setup'

    def _deps(self, reads, writes):
        deps = list(self.bar)
        for r in reads:
            t = self.last_w.get(r)
            if t is not None:
                deps.append(t)
        for w in writes:
            t = self.last_w.get(w)
            if t is not None:
                deps.append(t)
            deps.extend(self.readers.get(w, ()))
        return deps

    def _commit(self, tok, reads, writes):
        for r in reads:
            self.readers.setdefault(r, []).append(tok)
        for w in writes:
            self.last_w[w] = tok
            self.readers[w] = []

    @staticmethod
    def _split(reads, writes):
        r2 = [r for r in reads if not r.startswith("ps")]
        w2 = list(writes) + [r for r in reads if r.startswith("ps")]
        return r2, w2

    def op(self, eng, fn, reads=(), writes=()):
        reads, writes = self._split(reads, writes)
        deps = self._deps(reads, writes)
        self.cnt[eng] += 1
        tok = ("c", eng, self.cnt[eng])
        self.ops[eng].append(dict(fn=fn, deps=deps, tok=tok, ph=self.phase))
        self._commit(tok, reads, writes)
        return tok

    def dma(self, eng, fn, key, reads=(), writes=()):
        deps = self._deps(reads, writes)
        if key not in self.dma_sems:
            self.dma_sems[key] = [self.nc.alloc_semaphore("d_" + key), 0]
        ent = self.dma_sems[key]
        ent[1] += 16
        tok = ("d", key, ent[1])
        self.ops[eng].append(dict(fn=fn, deps=deps, tok=tok, ph=self.phase))
        self._commit(tok, reads, writes)
        return tok

    def barrier(self):
        toks = []
        for e in self.ENGS:
            if self.cnt[e]:
                toks.append(("c", e, self.cnt[e]))
        for k, (s, v) in self.dma_sems.items():
            if v:
                toks.append(("d", k, v))
        self.bar = toks

    def emit(self, final_waits=()):
        nc = self.nc
        for e in self.ENGS:
            self.eng_sem[e] = nc.alloc_semaphore("e_" + e)

        def run(ename, eng):
            waited = {}
            for o in self.ops[ename]:
                need = {}
                for t in o["deps"]:
                    if t[0] == "c":
                        if t[1] == ename and ename == "pe":
                            continue
                        sem = self.eng_sem[t[1]]
                    else:
                        sem = self.dma_sems[t[1]][0]
                    k = sem.num
                    if t[2] > need.get(k, (None, 0))[1]:
                        need[k] = (sem, t[2])
                for k, (sem, v) in need.items():
                    if waited.get(k, 0) >= v:
                        continue
                    eng.wait_ge(sem, v)
                    waited[k] = v
                ins = o["fn"](eng)
                if ANNOTATE:
                    ins.annotate(o["ph"])
                t = o["tok"]
                if t[0] == "c":
                    ins.then_inc(self.eng_sem[ename], 1)
                else:
                    ins.then_inc(self.dma_sems[t[1]][0], 16)
            if ename == "sp":
                for t in final_waits:
                    if t[0] == "c":
                        eng.wait_ge(self.eng_sem[t[1]], t[2])
                    else:
                        eng.wait_ge(self.dma_sems[t[1]][0], t[2])

        with nc.Block() as block:
            @block.tensor
            def _(e):
                run("pe", e)

            @block.scalar
            def _(e):
                run("act", e)

            @block.vector
            def _(e):
                run("dve", e)

            @block.gpsimd
            def _(e):
                run("pool", e)

            @block.sync
            def _(e):
                run("sp", e)


class Arena:
    def __init__(self, nc, base, end, tag):
        self.nc, self.base, self.end, self.tag, self.cur, self.n = nc, base, end, tag, base, 0

    def alloc(self, name, shape, dtype):
        esz = 4 if dtype == F32 else 2
        nbytes = int(np.prod(shape[1:])) * esz
        nbytes = (nbytes + 63) // 64 * 64
        assert self.cur + nbytes <= self.end, (self.tag, name, self.cur + nbytes - self.end)
        t = self.nc.alloc_sbuf_tensor_at("%s_%s_%d" % (self.tag, name, self.n), list(shape), dtype, offset=self.cur)
        self.n += 1
        self.cur += nbytes
        return t


def I(name, *a, **kw):
    return lambda e: getattr(e, name)(*a, **kw)


def bc(ap, shape):
    return ap.to_broadcast(list(shape))


def build_program(mixer_on=True, n_layers=L):
    nc = bass.Bass("TRN2", target_bir_lowering=False)
    S = Sched(nc)

    def din(name, shape, dt=F32):
        return nc.dram_tensor(name, list(shape), dt, kind="ExternalInput").ap()

    def dout(name, shape):
        return nc.dram_tensor(name, list(shape), F32, kind="ExternalOutput").ap()

    xp = din("xp", [NP, D])
    xs = din("xs", [128, D])
    cc = din("cc", [NS, D])
    vecs = din("vecs", [VR_TOTAL, 128])
    ident_d = din("ident", [128, 128])
    w_ada = din("w_ada", [L, D, 9 * D])
    w1 = [din("w1_a", [L, D, DFF]), din("w1_b", [L, D, DFF])]
    w3 = [din("w3_a", [L, D, DFF]), din("w3_b", [L, D, DFF])]
    w2 = [din("w2_a", [L, DFF, D]), din("w2_b", [L, DFF, D])]
    w_in = din("w_in", [L, D, INC])
    w_ret_out = din("w_ret_out", [L, D, D])
    w_conf_out = din("w_conf_out", [L, 512, D])
    w_sc_out = din("w_sc_out", [L, 512, D])
    w_o = din("w_o", [L, D, D])
    st_ret = din("st_ret", [L, 16, 8, 64, 128])
    st_conf = din("st_conf", [L, 16, 30, 512])
    st_sc = din("st_sc", [L, 16, 2, 512])
    rot_d = din("rot", [17, 128, 3, 32])
    dmask_d = din("dmask", [2, 128, 8, 128])
    qk_dec_d = din("qkdec", [128, 2, 2, 8])
    zmask_d = din("zmask", [128, 8, 128])
    kmask_d = din("kmask", [128, 16])
    yp = dout("yp", [NP, D])
    ys = dout("ys", [128, D])
    retp = dout("retp", [L, 8, 64, 128])
    rets = dout("rets", [L, 16, 8, 64, 128])
    confp = dout("confp", [L, 30, 512])
    confs = dout("confs", [L, 16, 30, 512])
    scp = dout("scp", [L, 2, 512])
    scs = dout("scs", [L, 16, 2, 512])
    finals = []

    P = Arena(nc, SB_BASE, SB_END, "P")
    xT = P.alloc("xT", [128, 8, NT], F32)
    hT = P.alloc("hT", [128, 8, NT], BF16)
    mod = P.alloc("mod", [128, 9, 8, NS], F32)
    vecT = P.alloc("vecT", [128, VR_TOTAL], F32)
    ident = P.alloc("ident", [128, 128], F32)
    identb = P.alloc("identb", [128, 128], BF16)
    onesb = P.alloc("onesb", [128, 128], BF16)
    scT = P.alloc("scT", [128, 8, NS], BF16)
    mhalf = P.alloc("mhalf", [128, 512], F32)
    MT_BASE = P.cur
    mT = P.alloc("mT", [128, 8, NT], BF16)
    PH_BASE = P.cur

    banks = [nc.alloc_psum_tensor("bank%d" % i, [128, 512], F32) for i in range(8)]
    bn = ["ps%d" % i for i in range(8)]

    S.dma("sp", I("dma_start", out=ident[:], in_=ident_d), key="c_id", writes=["ident"])
    S.op("act", I("copy", out=identb[:], in_=ident[:]), reads=["ident"], writes=["identb"])
    S.op("pool", I("memset", onesb[:], 1.0), writes=["onesb"])
    S.op("pool", I("memset", mhalf[:], -0.5), writes=["mhalf"])

    A0 = Arena(nc, PH_BASE, SB_END, "A0")
    xstage = [A0.alloc("xst%d" % i, [128, D], F32) for i in range(2)]
    vst = A0.alloc("vst", [112, 5, 128], F32)
    cst = A0.alloc("cst", [NS, D], F32)
    cT = A0.alloc("cT", [128, 8, NS], F32)

    for ti in range(17):
        st = xstage[ti % 2]
        src = xp[ti * 128:(ti + 1) * 128, :] if ti < 16 else xs
        S.dma("sp", I("dma_start", out=st[:], in_=src), key="xst%d" % (ti % 2),
              writes=["xst%d" % (ti % 2)])
        for hb in range(2):
            b = banks[(ti % 2) * 2 + hb]
            for c4 in range(4):
                c = hb * 4 + c4
                S.op("pe", I("transpose", out=b[:, c4 * 128:(c4 + 1) * 128], in_=st[:, c * 128:(c + 1) * 128], identity=ident[:]),
                    reads=["xst%d" % (ti % 2), "ident"], writes=[bn[(ti % 2) * 2 + hb]])
            eng = "act" if hb == 0 else "dve"
            dst = xT[:, hb * 4:hb * 4 + 4, ti * 128:(ti + 1) * 128]
            srcp = b[:].rearrange("p (a b) -> p a b", a=4)
            if eng == "act":
                S.op("act", I("copy", out=dst, in_=srcp),
                     reads=[bn[(ti % 2) * 2 + hb]], writes=["xT:%d" % (ti // 4)])
            else:
                S.op("dve", I("tensor_copy", out=dst, in_=srcp),
                     reads=[bn[(ti % 2) * 2 + hb]], writes=["xT:%d" % (ti // 4)])

    S.dma("sp", I("dma_start", out=vst[:], in_=vecs.rearrange("(a r) f -> r a f", r=112)), key="vst", writes=["vst"])
    for a in range(5):
        b = banks[4 + (a % 2)]
        S.op("pe", I("transpose", out=b[:, 0:112], in_=vst[:, a, :], identity=ident[0:112, 0:112]),
             reads=["vst", "ident"], writes=[bn[4 + (a % 2)]])
        S.op("dve", I("tensor_copy", out=vecT[:, a * 112:(a + 1) * 112], in_=b[:, 0:112]),
             reads=[bn[4 + (a % 2)]], writes=["vecT"])
    S.dma("sp", I("dma_start", out=cst[:], in_=cc), key="cst", writes=["cst"])
    for c in range(8):
        S.op("pe", I("transpose", out=banks[6][:, c * NS:(c + 1) * NS], in_=cst[:, c * 128:(c + 1) * 128],
                                              identity=ident[0:NS, 0:NS]), reads=["cst", "ident"], writes=[bn[6]])
    S.op("act", I("activation", out=scT[:].rearrange("p a b -> p (a b)"), in_=banks[6][:, 0:8 * NS], func=AF.Silu),
         reads=[bn[6]], writes=["scT"])

    wq = {"n": 0}

    def xg(g):
        return "xT:%d" % g

    def norm_mod(AR, ia, ib, tag, lazy=False):
        sq = [AR.alloc("sq%d" % i, [128, 512], BF16) for i in range(2)]
        vs = [AR.alloc("v%d" % i, [128, 512], F32) for i in range(2)]
        rstds = [AR.alloc("rstd%d" % i, [128, 512], F32) for i in range(2)]
        tt = [AR.alloc("tt%d" % i, [128, 512], F32) for i in range(2)]

        def n_sq(g):
            t0, n = GROUPS[g]
            pb = 6 + (g % 2)
            for kc in range(8):
                s_ = sq[kc % 2]
                S.op("act", I("activation", out=s_[:, 0:n], in_=xT[:, kc, t0:t0 + n], func=AF.Square),
                     reads=[xg(g)], writes=["sq%d" % (kc % 2)])
                S.op("pe", I("matmul", banks[pb][:, 0:n], lhsT=onesb[:], rhs=s_[:, 0:n], start=(kc == 0), stop=(kc == 7)),
                     reads=["sq%d" % (kc % 2), "onesb"], writes=[bn[pb]])

        def n_rstd(g):
            t0, n = GROUPS[g]
            pb = 6 + (g % 2)
            v, rstd = vs[g % 2], rstds[g % 2]
            S.op("dve", I("tensor_scalar", out=v[:, 0:n], in0=banks[pb][:, 0:n], scalar1=1.0 / D, scalar2=1e-6,
                          op0=ALU.mult, op1=ALU.add), reads=[bn[pb]], writes=["v%d" % (g % 2)])
            S.op("act", I("activation", out=rstd[:, 0:n], in_=v[:, 0:n], func=AF.Sqrt), reads=["v%d" % (g % 2)], writes=["rstd%d" % (g % 2)])
            S.op("dve", I("reciprocal", out=rstd[:, 0:n], in_=rstd[:, 0:n]), writes=["rstd%d" % (g % 2)])

        def n_apply(g):
            t0, n = GROUPS[g]
            rstd = rstds[g % 2]
            rsn = "rstd%d" % (g % 2)
            for kc in range(8):
                t_ = tt[kc % 2]
                S.op("dve", I("tensor_tensor", out=t_[:, 0:n], in0=xT[:, kc, t0:t0 + n], in1=rstd[:, 0:n], op=ALU.mult),
                     reads=[xg(g), rsn], writes=["tt%d" % (kc % 2)])
                if g < 4:
                    S.op("act", I("activation", out=hT[:, kc, t0:t0 + n], in_=t_[:, 0:n], func=AF.Identity,
                                  scale=mod[:, ia, kc, 0:1], bias=mod[:, ib, kc, 0:1]),
                         reads=["tt%d" % (kc % 2), "mod"], writes=["hT:%d" % g])
                else:
                    tv = t_[:, 0:128].rearrange("p (s i) -> p s i", s=16)
                    S.op("dve", I("tensor_tensor", out=tv, in0=tv, in1=bc(mod[:, ia, kc, 1:17].unsqueeze(2), [128, 16, 8]), op=ALU.mult),
                         reads=["mod"], writes=["tt%d" % (kc % 2)])
                    S.op("dve", I("tensor_tensor", out=hT[:, kc, t0:t0 + n].rearrange("p (s i) -> p s i", s=16), in0=tv,
                                  in1=bc(mod[:, ib, kc, 1:17].unsqueeze(2), [128, 16, 8]), op=ALU.add),
                         reads=["mod", "tt%d" % (kc % 2)], writes=["hT:%d" % g])

        def norm_group(g):
            n_sq(g)
            n_rstd(g)
            n_apply(g)

        if lazy:
            return norm_group
        ng = len(GROUPS)
        n_sq(0)
        for g in range(ng):
            n_rstd(g)
            if g + 1 < ng:
                n_sq(g + 1)
            n_apply(g)

    def resid_add(pbank, d, g, ig):
        t0, n = GROUPS[g]
        if g < 4:
            S.op("dve", I("scalar_tensor_tensor", out=xT[:, d, t0:t0 + n], in0=banks[pbank][:, 0:n],
                                                          scalar=mod[:, ig, d, 0:1], in1=xT[:, d, t0:t0 + n],
                                                          op0=ALU.mult, op1=ALU.add),
                 reads=[bn[pbank], "mod", xg(g)], writes=[xg(g)])
        else:
            xv = xT[:, d, t0:t0 + n].rearrange("p (s i) -> p s i", s=16)
            pv = banks[pbank][:, 0:n].rearrange("p (s i) -> p s i", s=16)
            S.op("dve", I("tensor_tensor", out=pv, in0=pv, in1=bc(mod[:, ig, d, 1:17].unsqueeze(2), [128, 16, 8]),
                                                   op=ALU.mult), reads=[bn[pbank], "mod"], writes=[bn[pbank]])
            S.op("dve", I("tensor_tensor", out=xv, in0=xv, in1=pv, op=ALU.add),
                 reads=[bn[pbank], xg(g)], writes=[xg(g)])

    def ada_layer(l):
        if l > 0:
            S.barrier()
        S.phase = 'ada%d' % l
        AR = Arena(nc, PH_BASE if l > 0 else A0.cur, SB_END, "ada%d" % l)
        ring = [AR.alloc("w%d" % i, [128, 8, 768], BF16) for i in range(3)]
        adaT = AR.alloc("adaT", [128, 72, NS], F32)
        for blk in range(12):
            slot = wq["n"] % 3
            wq["n"] += 1
            wt = ring[slot]
            src = w_ada[l, :, blk * 768:(blk + 1) * 768].rearrange("(kc p) n -> p kc n", p=128)
            S.dma("pool", I("dma_start", out=wt[:], in_=src), key="wr%d" % slot, writes=["wr%d" % slot])
            pb = 4 + (blk % 2)
            for j6 in range(6):
                for kc in range(8):
                    S.op("pe", I("matmul", banks[pb][:, j6 * NS:(j6 + 1) * NS], lhsT=wt[:, kc, j6 * 128:(j6 + 1) * 128], rhs=scT[:, kc, :],
                        start=(kc == 0), stop=(kc == 7)), reads=["wr%d" % slot, "scT"], writes=[bn[pb]])
            j0 = blk * 6
            S.op("dve", I("tensor_tensor", out=adaT[:, j0:j0 + 6, :], in0=banks[pb][:, 0:6 * NS].rearrange("p (a b) -> p a b", a=6),
                in1=bc(vecT[:, vrow(l, "b_ada", j0):vrow(l, "b_ada", j0) + 6].unsqueeze(2), [128, 6, NS]), op=ALU.add),
                reads=[bn[pb], "vecT"], writes=["adaT"])
        for k, (gname, half) in enumerate((("g_ffn1", 0.5), ("g_mix", 1.0), ("g_ffn2", 0.5))):
            sh = adaT[:, (3 * k) * 8:(3 * k) * 8 + 8, :]
            sc = adaT[:, (3 * k + 1) * 8:(3 * k + 1) * 8 + 8, :]
            gt = adaT[:, (3 * k + 2) * 8:(3 * k + 2) * 8 + 8, :]
            gv = bc(vecT[:, vrow(l, gname):vrow(l, gname) + 8].unsqueeze(2), [128, 8, NS])
            S.op("dve", I("scalar_tensor_tensor", out=mod[:, 3 * k, :, :], in0=sc, scalar=1.0, in1=gv,
                                                                          op0=ALU.add, op1=ALU.mult),
                 reads=["adaT", "vecT"], writes=["mod"])
            S.op("dve", I("tensor_copy", out=mod[:, 3 * k + 1, :, :], in_=sh), reads=["adaT"], writes=["mod"])
            S.op("dve", I("tensor_scalar", out=mod[:, 3 * k + 2, :, :], in0=gt, scalar1=half,
                                                                       scalar2=None, op0=ALU.mult),
                 reads=["adaT"], writes=["mod"])

    def ffn(l, which):
        S.barrier()
        S.phase = 'ffn%d%d' % (l, which)
        AR = Arena(nc, MT_BASE, SB_END, "ffn%d%d" % (l, which))
        FS = 4
        ring = [(AR.alloc("w1_%d" % i, [128, 8, FS * 128], BF16), AR.alloc("w3_%d" % i, [128, 8, FS * 128], BF16),
                 AR.alloc("w2_%d" % i, [128, FS, D], BF16)) for i in range(2)]
        gT = AR.alloc("gT", [128, FS, NT], BF16)
        sl = [AR.alloc("s%d" % i, [128, 512], F32) for i in range(2)]
        k3 = 0 if which == 0 else 2
        ia, ib, ig = 3 * k3, 3 * k3 + 1, 3 * k3 + 2
        W1, W3, W2 = w1[which], w3[which], w2[which]
        stages = [(f0, min(FS, 22 - f0)) for f0 in range(0, 22, FS)]

        def load(si):
            f0, nf = stages[si]
            slot = si % 2
            a, b_, c_ = ring[slot]
            s1 = W1[l, :, f0 * 128:(f0 + nf) * 128].rearrange("(kc p) n -> p kc n", p=128)
            s3 = W3[l, :, f0 * 128:(f0 + nf) * 128].rearrange("(kc p) n -> p kc n", p=128)
            s2 = W2[l, f0 * 128:(f0 + nf) * 128, :].rearrange("(f p) n -> p f n", p=128)
            for dst, src in ((a[:, :, 0:nf * 128], s1), (b_[:, :, 0:nf * 128], s3), (c_[:, 0:nf, :], s2)):
                S.dma("pool", I("dma_start", out=dst, in_=src), key="fw%d" % slot, writes=["fw%d" % slot])

        load(0)
        norm_group = norm_mod(AR, ia, ib, "f", lazy=True)
        norm_group(0)
        norm_group(1)
        load(1)
        cnt = 0
        yc = 0
        for si, (f0, nf) in enumerate(stages):
            slot = si % 2
            a, b_, c_ = ring[slot]
            wn = "fw%d" % slot
            for g, (t0, n) in enumerate(GROUPS):
                if si == 0 and g + 2 < len(GROUPS):
                    norm_group(g + 2)
                for f in range(nf):
                    pu1, pu3 = (cnt % 2) * 2, (cnt % 2) * 2 + 1
                    s_ = sl[cnt % 2]
                    sn = "sl%d" % (cnt % 2)
                    cnt += 1
                    for kc in range(8):
                        S.op("pe", I("matmul", banks[pu1][:, 0:n], lhsT=a[:, kc, f * 128:(f + 1) * 128], rhs=hT[:, kc, t0:t0 + n],
                                     start=(kc == 0), stop=(kc == 7)), reads=[wn, "hT:%d" % g], writes=[bn[pu1]])
                    for kc in range(8):
                        S.op("pe", I("matmul", banks[pu3][:, 0:n], lhsT=b_[:, kc, f * 128:(f + 1) * 128], rhs=hT[:, kc, t0:t0 + n],
                                     start=(kc == 0), stop=(kc == 7)), reads=[wn, "hT:%d" % g], writes=[bn[pu3]])
                    S.op("act", I("activation", out=s_[:, 0:n], in_=banks[pu1][:, 0:n], func=AF.Silu), reads=[bn[pu1]], writes=[sn])
                    S.op("dve", I("tensor_tensor", out=gT[:, f, t0:t0 + n], in0=banks[pu3][:, 0:n], in1=s_[:, 0:n], op=ALU.mult),
                         reads=[bn[pu3], sn], writes=["gT:%d:%d" % (f, g)])
                for d in range(8):
                    pb = 4 + (yc % 4)
                    yc += 1
                    for f in range(nf):
                        S.op("pe", I("matmul", banks[pb][:, 0:n], lhsT=c_[:, f, d * 128:(d + 1) * 128], rhs=gT[:, f, t0:t0 + n],
                                     start=(f == 0), stop=(f == nf - 1)), reads=[wn, "gT:%d:%d" % (f, g)], writes=[bn[pb]])
                    resid_add(pb, d, g, ig)
            if si + 2 < len(stages):
                load(si + 2)

    COL = dict(q=0, k=512, v=1024, g=2048, ca=3072, cb=3584, sb=4096, scc=4608, sx=5120, gl=5632)
    BRS = "RSC"

    def wcols(l, c0, n):
        return w_in[l, :, c0:c0 + n].rearrange("(kc p) n -> p kc n", p=128)

    def merge(l, b, srcT, nk, Wout, first, inplace, base):
        S.barrier()
        S.phase = 'merge%d%d' % (l, b)
        AR = Arena(nc, base, SB_END, "mg%d%d" % (l, b))
        wo_t = AR.alloc("wo", [128, nk, D], BF16)
        wg_t = AR.alloc("wg", [128, 8, D], BF16)
        sg = [AR.alloc("sg%d" % i, [128, 512], F32) for i in range(2)]
        tmp = AR.alloc("tmp", [128, 512], F32)
        mtmp = AR.alloc("mtmp", [128, 8, 512], BF16) if inplace else None
        for j in range(4):
            S.dma("pool", I("dma_start", out=wo_t[:, :, j * 256:(j + 1) * 256],
                            in_=Wout[l, :, j * 256:(j + 1) * 256].rearrange("(kc p) n -> p kc n", p=128)), key="mwo%d" % j, writes=["mwo%d" % j])
            S.dma("pool", I("dma_start", out=wg_t[:, :, j * 256:(j + 1) * 256], in_=wcols(l, COL["gl"] + b * D + j * 256, 256)),
                  key="mwg%d" % j, writes=["mwg%d" % j])
        cnt = 0
        for g, (t0, n) in enumerate(GROUPS):
            for d in range(8):
                pa, pb = (cnt % 2) * 2, (cnt % 2) * 2 + 1
                sg_ = sg[cnt % 2]
                sgn = "msg%d" % (cnt % 2)
                cnt += 1
                for kc in range(nk):
                    S.op("pe", I("matmul", banks[pa][:, 0:n], lhsT=wo_t[:, kc, d * 128:(d + 1) * 128], rhs=srcT[:, kc, t0:t0 + n],
                                 start=(kc == 0), stop=(kc == nk - 1)), reads=["mwo%d" % (d // 2), "src:%d" % g], writes=[bn[pa]])
                for kc in range(8):
                    S.op("pe", I("matmul", banks[pb][:, 0:n], lhsT=wg_t[:, kc, d * 128:(d + 1) * 128], rhs=hT[:, kc, t0:t0 + n],
                                 start=(kc == 0), stop=(kc == 7)), reads=["mwg%d" % (d // 2), "hT:%d" % g], writes=[bn[pb]])
                br = vrow(l, "b_gate", b * 8 + d)
                S.op("act", I("activation", out=sg_[:, 0:n], in_=banks[pb][:, 0:n], func=AF.Sigmoid, bias=vecT[:, br:br + 1]),
                     reads=[bn[pb], "vecT"], writes=[sgn])
                if inplace:
                    S.op("dve", I("tensor_tensor", out=mtmp[:, d, 0:n], in0=banks[pa][:, 0:n], in1=sg_[:, 0:n], op=ALU.mult),
                         reads=[bn[pa], sgn], writes=["mtmp"])
                elif first:
                    S.op("dve", I("tensor_tensor", out=mT[:, d, t0:t0 + n], in0=banks[pa][:, 0:n], in1=sg_[:, 0:n], op=ALU.mult),
                         reads=[bn[pa], sgn], writes=["m:%d" % g])
                else:
                    S.op("dve", I("tensor_tensor", out=tmp[:, 0:n], in0=banks[pa][:, 0:n], in1=sg_[:, 0:n], op=ALU.mult),
                         reads=[bn[pa], sgn], writes=["mtmpf"])
                    S.op("dve", I("tensor_tensor", out=mT[:, d, t0:t0 + n], in0=mT[:, d, t0:t0 + n], in1=tmp[:, 0:n], op=ALU.add),
                         reads=["mtmpf", "m:%d" % g], writes=["m:%d" % g])
            if inplace:
                S.op("act", I("copy", out=mT[:, :, t0:t0 + n], in_=mtmp[:, :, 0:n]), reads=["mtmp", "src:%d" % g],
                     writes=["m:%d" % g, "src:%d" % g])


    LG = np.log(np.float32(1.0) - np.exp2(-5.0 - np.arange(8, dtype=np.float32))).astype(np.float32)

    def branch_r(l):
        S.barrier()
        S.phase = 'brR%d' % l
        AR = Arena(nc, PH_BASE, SB_END, "br%d" % l)
        WT_OFF = AR.cur
        wt = AR.alloc("wqkvg", [128, 8, 1536], BF16)
        S0all = nc.alloc_sbuf_tensor_at("br%d_S0all" % l, [128, 4, 8, 128], F32, offset=WT_OFF)
        dm = AR.alloc("dm", [128, 4, 128], F32)
        zm = AR.alloc("zm", [128, 8, 128], BF16)
        km = AR.alloc("km", [128, 16], BF16)
        qkd = AR.alloc("qkd", [128, 2, 2, 8], F32)
        rot = [AR.alloc("rot%d" % i, [128, 3, 32], F32) for i in range(2)]
        ta = AR.alloc("ta", [128, 512], F32)
        tb = AR.alloc("tb", [128, 512], F32)
        qkrot = AR.alloc("qkrot", [128, 512], BF16)
        qrot, krot = qkrot[:, 0:256], qkrot[:, 256:512]
        qkdt = [AR.alloc("qkdt%d" % i, [128, 512], BF16) for i in range(2)]
        qd = [t[:, 0:256] for t in qkdt]
        kd = [t[:, 256:512] for t in qkdt]
        nmr = AR.alloc("nmr", [128, 4], F32)
        qT = [AR.alloc("qT%d" % i, [64, 4, 128], BF16) for i in range(2)]
        qdT = [AR.alloc("qdT%d" % i, [64, 4, 128], BF16) for i in range(2)]
        kT = [AR.alloc("kT%d" % i, [64, 4, 128], BF16) for i in range(2)]
        vb = [AR.alloc("vb%d" % i, [128, 512], BF16) for i in range(2)]
        sg = AR.alloc("sg", [128, 512], F32)
        AT = AR.alloc("AT", [128, 4, 128], BF16)
        st6 = AR.alloc("st6", [128, 4, 6], F32)
        mv = AR.alloc("mv", [128, 4, 2], F32)
        rsd = AR.alloc("rsd", [128, 4], F32)
        rrs = [AR.alloc("rr%d" % i, [128, 512], BF16) for i in range(2)]
        Sst = AR.alloc("Sst", [64, 4, 128], F32)
        Sb = AR.alloc("Sb", [64, 4, 128], BF16)
        S0bs = [AR.alloc("S0b%d" % i, [128, 8, 128], BF16) for i in range(2)]
        Zqs = [AR.alloc("Zq%d" % i, [128, 8, 128], BF16) for i in range(2)]
        qdd = AR.alloc("qdd", [128, 2, 64], BF16)
        B3b = banks[3][:].bitcast(BF16)
        B4b = banks[4][:].bitcast(BF16)
        B5b = banks[5][:].bitcast(BF16)
        S.dma("pool", I("dma_start", out=zm[:], in_=zmask_d), key="c_zm", writes=["zm"])
        S.dma("pool", I("dma_start", out=km[:], in_=kmask_d), key="c_km", writes=["km"])
        S.dma("sp", I("dma_start", out=qkd[:], in_=qk_dec_d), key="c_qkd", writes=["qkd"])

        def stage_a(hh, ti, part):
            S.phase = 'brR%d:%d:%02d' % (l, hh, ti)
            g = ti // 4
            c0 = ti * 128
            pi = 0 if ti < 16 else 1
            i2 = ti % 2
            rt = rot[i2]
            rn = "rot%d" % i2
            if part == 1:
              S.dma("sp", I("dma_start", out=rt[:], in_=rot_d[ti]), key=rn, writes=[rn])
              for (pb, o0, n_, c_) in ((0, 0, 256, 0), (0, 256, 256, 256), (1, 512, 512, 0)):
                for kc in range(8):
                    S.op("pe", I("matmul", banks[pb][:, c_:c_ + n_], lhsT=hT[:, kc, c0:c0 + 128], rhs=wt[:, kc, o0:o0 + n_],
                                 start=(kc == 0), stop=(kc == 7)), reads=["rw", "hT:%d" % g], writes=[bn[pb]])
              S.op("act", I("copy", out=vb[i2][:], in_=banks[1][:]), reads=[bn[1]], writes=["vb%d" % i2])
            if part == 2:
                X = banks[0][:]
                X16 = X.rearrange("p (a r) -> p a r", r=32)
                X8 = X.rearrange("p (h t r) -> p h t r", h=8, t=2)
                ta16 = ta[:].rearrange("p (a r) -> p a r", r=32)
                tb8 = tb[:].rearrange("p (h t r) -> p h t r", h=8, t=2)
                S.op("dve", I("tensor_tensor", out=ta16, in0=X16, in1=bc(rt[:, 0, :].unsqueeze(1), [128, 16, 32]), op=ALU.mult),
                     reads=[bn[0], rn], writes=["ta"])
                S.op("dve", I("tensor_tensor", out=tb8[:, :, 0, :], in0=X8[:, :, 1, :], in1=bc(rt[:, 2, :].unsqueeze(1), [128, 8, 32]),
                              op=ALU.mult), reads=[bn[0], rn], writes=["tb"])
                S.op("dve", I("tensor_tensor", out=tb8[:, :, 1, :], in0=X8[:, :, 0, :], in1=bc(rt[:, 1, :].unsqueeze(1), [128, 8, 32]),
                              op=ALU.mult), reads=[bn[0], rn], writes=["tb"])
                S.op("dve", I("tensor_tensor", out=ta[:], in0=ta[:], in1=tb[:], op=ALU.add), reads=["tb"], writes=["ta"])
                S.op("act", I("copy", out=qkrot[:], in_=ta[:]), reads=["ta"], writes=["qkrot"])
                S.op("dve", I("tensor_tensor", out=qkdt[i2][:].rearrange("p (w h d) -> p w h d", w=2, h=4),
                              in0=ta[:].rearrange("p (w h d) -> p w h d", w=2, h=4),
                              in1=bc(qkd[:, pi, :, 4 * hh:4 * hh + 4].unsqueeze(3), [128, 2, 4, 64]), op=ALU.mult),
                     reads=["ta", "qkd"], writes=["qkdt%d" % i2])
            if part != 3:
                return
            for h in range(4):
                S.op("pe", I("transpose", out=B3b[0:64, h * 128:(h + 1) * 128], in_=qkrot[:, h * 64:(h + 1) * 64], identity=identb[:]),
                     reads=["qkrot", "identb"], writes=[bn[3]])
                S.op("pe", I("transpose", out=B3b[0:64, 512 + h * 128:512 + (h + 1) * 128], in_=qkdt[i2][:, h * 64:(h + 1) * 64],
                             identity=identb[:]), reads=["qkdt%d" % i2, "identb"], writes=[bn[3]])
                S.op("pe", I("transpose", out=B4b[0:64, h * 128:(h + 1) * 128], in_=qkrot[:, 256 + h * 64:256 + (h + 1) * 64], identity=identb[:]),
                     reads=["qkrot", "identb"], writes=[bn[4]])
            S.op("act", I("copy", out=qT[i2][:].rearrange("p h t -> p (h t)"), in_=B3b[0:64, 0:512]), reads=[bn[3]], writes=["qT%d" % i2])
            S.op("act", I("copy", out=qdT[i2][:].rearrange("p h t -> p (h t)"), in_=B3b[0:64, 512:1024]), reads=[bn[3]], writes=["qdT%d" % i2])
            S.op("act", I("copy", out=kT[i2][:].rearrange("p h t -> p (h t)"), in_=B4b[0:64, 0:512]), reads=[bn[4]], writes=["kT%d" % i2])

        def stage_b(hh, ti, part):
            S.phase = 'brR%d:%d:%02d' % (l, hh, ti)
            g = ti // 4
            c0 = ti * 128
            pi = 0 if ti < 16 else 1
            i2 = ti % 2
            vbn, qTn, qdTn, kTn, qdn, kdn = ("vb%d" % i2, "qT%d" % i2, "qdT%d" % i2, "kT%d" % i2, "qkdt%d" % i2, "qkdt%d" % i2)
            rr = rrs[i2]
            rrn = "rr%d" % i2
            if part == 1:
              if ti == 16:
                S.dma("sp", I("dma_start", out=dm[:], in_=dmask_d[1, :, 4 * hh:4 * hh + 4, :]), key="c_dm", writes=["dm"])
              for kc in range(8):
                S.op("pe", I("matmul", banks[2][:], lhsT=hT[:, kc, c0:c0 + 128], rhs=wt[:, kc, 1024:1536], start=(kc == 0), stop=(kc == 7)),
                     reads=["rw", "hT:%d" % g], writes=[bn[2]])
              S.op("act", I("activation", out=sg[:], in_=banks[2][:], func=AF.Silu), reads=[bn[2]], writes=["sg"])
              if ti == 16:
                for h in range(4):
                    for s2 in range(2):
                        S.dma("sp", I("dma_start", out=S0all[s2 * 64:(s2 + 1) * 64, h, :, :],
                                      in_=st_ret[l, s2::2, 4 * hh + h, :, :].rearrange("pr d e -> d pr e")), key="S0_%d" % h,
                              reads=([] if (h == 0 and s2 == 0) else ["rw"]), writes=(["S0_%d" % h, "rw"] if (h == 0 and s2 == 0) else ["S0_%d" % h]))
              for h in range(4):
                S.op("pe", I("matmul", banks[5][:, h * 128:(h + 1) * 128], lhsT=kT[i2][:, h, :], rhs=qT[i2][:, h, :], start=True, stop=True),
                     reads=[kTn, qTn], writes=[bn[5]])
              S.op("dve", I("tensor_tensor", out=AT[:].rearrange("p h t -> p (h t)"), in0=banks[5][:],
                          in1=dm[:].rearrange("p h t -> p (h t)"), op=ALU.mult), reads=[bn[5], "dm"], writes=["AT"])
            for h in (range(4) if part == 2 else ()):
                H = 4 * hh + h
                hb = slice(h * 128, (h + 1) * 128)
                if pi == 0:
                    S.op("pe", I("matmul", banks[6][:, hb], lhsT=AT[:, h, :], rhs=vb[i2][:, hb], start=True, stop=(ti == 0)),
                         reads=["AT", vbn], writes=[bn[6]])
                    if ti > 0:
                        S.op("pe", I("matmul", banks[6][:, hb], lhsT=qdT[i2][:, h, :], rhs=Sb[:, h, :], start=False, stop=True),
                             reads=[qdTn, "Sb"], writes=[bn[6]])
                    S.op("pe", I("matmul", banks[7][0:64, hb], lhsT=qkdt[i2][:, 256 + h * 64:256 + (h + 1) * 64], rhs=vb[i2][:, hb], start=True, stop=True),
                         reads=[kdn, vbn], writes=[bn[7]])
                else:
                    S0 = S0all[:, h, :, :]
                    S0n = "S0_%d" % h
                    S0b, S0bn = S0bs[h % 2], "S0b%d" % (h % 2)
                    Zq, Zqn = Zqs[h % 2], "Zq%d" % (h % 2)
                    Zk = Zq[:].rearrange("p (a b) d -> p a (b d)", b=1).rearrange("p a (s d) -> p (a s) d", d=64)
                    S.op("act", I("copy", out=S0b[:], in_=S0), reads=[S0n, "rw"], writes=[S0bn])
                    for t_ in range(2):
                        S.op("dve", I("tensor_copy", out=qdd[:, t_, :], in_=qkdt[i2][:, h * 64:(h + 1) * 64]), reads=[qdn], writes=["qdd"])
                    S.op("pe", I("transpose", out=B4b[:, 512:640], in_=qdd[:].rearrange("p t d -> p (t d)"), identity=identb[:]),
                         reads=["qdd", "identb"], writes=[bn[4]])
                    S.op("dve", I("tensor_tensor", out=Zq[:], in0=bc(B4b[:, 512:640].unsqueeze(1), [128, 8, 128]), in1=zm[:], op=ALU.mult),
                         reads=[bn[4], "zm"], writes=[Zqn])
                    S.op("pe", I("matmul", banks[6][:, hb], lhsT=AT[:, h, :], rhs=vb[i2][:, hb], start=True, stop=False),
                         reads=["AT", vbn], writes=[bn[6]])
                    for pr in range(8):
                        S.op("pe", I("matmul", banks[6][:, hb], lhsT=Zq[:, pr, :], rhs=S0b[:, pr, :], start=False, stop=(pr == 7)),
                             reads=[Zqn, S0bn], writes=[bn[6]])
                    S.op("dve", I("tensor_tensor", out=Zk, in0=bc(qkdt[i2][:, 256 + h * 64:256 + (h + 1) * 64].unsqueeze(1), [128, 16, 64]),
                                  in1=bc(km[:].unsqueeze(2), [128, 16, 64]), op=ALU.mult), reads=[kdn, "km"], writes=[Zqn])
                    for pr in range(8):
                        S.op("pe", I("matmul", banks[pr // 4][:, (pr % 4) * 128:(pr % 4 + 1) * 128], lhsT=Zq[:, pr, :], rhs=vb[i2][:, hb],
                                     start=True, stop=True), reads=[Zqn, vbn], writes=[bn[pr // 4]])
                    cd8 = float(np.exp(np.float32(8.0) * LG[H]))
                    for q_ in range(2):
                        S.op("dve", I("scalar_tensor_tensor", out=S0[:, 4 * q_:4 * q_ + 4, :].rearrange("p a e -> p (a e)"),
                                      in0=S0[:, 4 * q_:4 * q_ + 4, :].rearrange("p a e -> p (a e)"), scalar=cd8, in1=banks[q_][:],
                                      op0=ALU.mult, op1=ALU.add), reads=[bn[q_], "rw"], writes=[S0n])
                    for s2 in range(2):
                        finals.append(S.dma("sp", I("dma_start", out=rets[l, s2::2, H, :, :].rearrange("pr d e -> d pr e"),
                                                    in_=S0all[s2 * 64:(s2 + 1) * 64, h, :, :]), key="o_rets%d" % h, reads=[S0n, "rw"]))
            if pi == 0 and part == 2:
                for h in range(4):
                    H = 4 * hh + h
                    cd = float(np.exp(np.float32(128.0) * LG[H]))
                    S.op("dve", I("scalar_tensor_tensor", out=Sst[:, h, :], in0=Sst[:, h, :], scalar=cd,
                                  in1=banks[7][0:64, h * 128:(h + 1) * 128], op0=ALU.mult, op1=ALU.add),
                         reads=[bn[7]], writes=["Sst"])
                S.op("act", I("copy", out=Sb[:], in_=Sst[:]), reads=["Sst"], writes=["Sb"])
            if part == 4:
                for h in range(4):
                    S.op("pe", I("transpose", out=B4b[:, 512 + h * 128:512 + (h + 1) * 128], in_=rr[:, h * 128:(h + 1) * 128], identity=identb[:]),
                         reads=[rrn, "identb"], writes=[bn[4]])
                for h in range(4):
                    r = vrow(l, "gn", 4 * hh + h)
                    S.op("act", I("activation", out=mT[:, 4 * hh + h, c0:c0 + 128], in_=B4b[:, 512 + h * 128:512 + (h + 1) * 128], func=AF.Identity,
                                  scale=vecT[:, r:r + 1]), reads=[bn[4], "vecT"], writes=["src:%d" % g])
            if part != 3:
                return
            for h in range(4):
                S.op("dve", I("bn_stats", out=st6[:, h, :], in_=banks[6][:, h * 128:(h + 1) * 128]), reads=[bn[6]], writes=["st6"])
            for h in range(4):
                S.op("dve", I("bn_aggr", out=mv[:, h, :], in_=st6[:, h, :]), reads=["st6"], writes=["mv"])
            S.op("dve", I("tensor_scalar", out=rsd[:], in0=mv[:, :, 1], scalar1=1e-5, scalar2=None, op0=ALU.add), reads=["mv"], writes=["rsd"])
            S.op("pool", I("tensor_tensor", out=rsd[:], in0=rsd[:], in1=mhalf[:, 0:4], op=ALU.pow), reads=["mhalf"], writes=["rsd"])
            S.op("dve", I("scalar_tensor_tensor", out=nmr[:], in0=mv[:, :, 0], scalar=-1.0, in1=rsd[:], op0=ALU.mult, op1=ALU.mult),
                 reads=["mv", "rsd"], writes=["nmr"])
            for h in range(4):
                S.op("act", I("activation", out=AT[:, h, :], in_=banks[6][:, h * 128:(h + 1) * 128], func=AF.Identity,
                              scale=rsd[:, h:h + 1], bias=nmr[:, h:h + 1]), reads=[bn[6], "rsd", "nmr"], writes=["AT"])
            S.op("dve", I("tensor_tensor", out=rr[:], in0=AT[:].rearrange("p h t -> p (h t)"), in1=sg[:], op=ALU.mult),
                 reads=["AT", "sg"], writes=[rrn])

        for hh in range(2):
            for j, (c0, n_) in enumerate(((COL["q"] + hh * 256, 256), (COL["k"] + hh * 256, 256), (COL["v"] + hh * 512, 512),
                                          (COL["g"] + hh * 512, 512))):
                o0 = (0, 256, 512, 1024)[j]
                S.dma("pool", I("dma_start", out=wt[:, :, o0:o0 + n_], in_=wcols(l, c0, n_)), key="rw", writes=["rw"])
            S.dma("sp", I("dma_start", out=dm[:], in_=dmask_d[0, :, 4 * hh:4 * hh + 4, :]), key="c_dm", writes=["dm"])
            S.op("pool", I("memset", Sst[:], 0.0), writes=["Sst"])
            S.op("pool", I("memset", Sb[:], 0.0), writes=["Sb"])
            for part in (1, 2, 3):
                stage_a(hh, 0, part)
            for ti in range(17):
                nxt = ti + 1 < 17
                if nxt:
                    stage_a(hh, ti + 1, 1)
                stage_b(hh, ti, 1)
                if nxt:
                    stage_a(hh, ti + 1, 2)
                stage_b(hh, ti, 2)
                if nxt:
                    stage_a(hh, ti + 1, 3)
                if ti > 0:
                    stage_b(hh, ti - 1, 4)
                stage_b(hh, ti, 3)
            stage_b(hh, 16, 4)
            finals.append(S.dma("sp", I("dma_start", out=retp[l, 4 * hh:4 * hh + 4].rearrange("h d e -> d h e"), in_=Sst[:]), key="o_retp",
                                reads=["Sst"]))
        merge(l, 0, mT, 8, w_ret_out, True, True, PH_BASE)

    def branch_s(l, first):
        S.barrier()
        S.phase = 'brS%d' % l
        AR = Arena(nc, PH_BASE, SB_END, "bs%d" % l)
        pST = AR.alloc("pST", [128, 4, NT], BF16)
        S_BASE = AR.cur
        ring = [AR.alloc("w%d" % i, [128, 8, 384], BF16) for i in range(3)]
        uspad = AR.alloc("uspad", [128, 2 + NP], F32)
        us_s = AR.alloc("us_s", [128, 4, 16, 10], F32)
        sxs = [AR.alloc("sx%d" % i, [128, 512], F32) for i in range(2)]
        acc = [AR.alloc("acc%d" % i, [128, 512], F32) for i in range(2)]
        stst = AR.alloc("stst", [32, 512], F32)
        so_p = AR.alloc("so_p", [2, 512], F32)
        ust = AR.alloc("ust", [128, 4, 32], F32)
        so_s = AR.alloc("so_s", [32, 512], F32)
        S.dma("sp", I("dma_start", out=stst[:], in_=st_sc[l].rearrange("s i c -> (s i) c")), key="stst", writes=["stst"])
        for cc in range(4):
            S.op("pe", I("transpose", out=banks[7][:, cc * 32:(cc + 1) * 32], in_=stst[:, cc * 128:(cc + 1) * 128],
                         identity=ident[0:32, 0:32]), reads=["stst", "ident"], writes=[bn[7]])
        S.op("dve", I("tensor_copy", out=us_s[:, :, :, 0:2], in_=banks[7][:, 0:128].rearrange("p (c s i) -> p c s i", c=4, s=16)),
             reads=[bn[7]], writes=["us_s"])
        S.op("pool", I("memset", uspad[:, 0:2], 0.0), writes=["uspad"])
        cnt = 0
        for cc in range(4):
            slot = wq["n"] % 3
            wq["n"] += 1
            wt = ring[slot]
            wn = "wr%d" % slot
            for j, nm in enumerate(("sb", "scc", "sx")):
                S.dma("pool", I("dma_start", out=wt[:, :, j * 128:(j + 1) * 128], in_=wcols(l, COL[nm] + cc * 128, 128)), key=wn, writes=[wn])
            for g, (t0, n) in enumerate(GROUPS):
                i2 = cnt % 2
                cnt += 1
                B0, B1, B2 = 3 * i2, 3 * i2 + 1, 3 * i2 + 2
                for j in range(3):
                    for kc in range(8):
                        S.op("pe", I("matmul", banks[3 * i2 + j][:, 0:n], lhsT=wt[:, kc, j * 128:(j + 1) * 128], rhs=hT[:, kc, t0:t0 + n],
                                     start=(kc == 0), stop=(kc == 7)), reads=[wn, "hT:%d" % g], writes=[bn[3 * i2 + j]])
                S.op("act", I("copy", out=sxs[i2][:, 0:n], in_=banks[B2][:, 0:n]), reads=[bn[B2]], writes=["sxs%d" % i2])
                if g < 4:
                    S.op("dve", I("tensor_tensor", out=uspad[:, 2 + t0:2 + t0 + n], in0=banks[B1][:, 0:n], in1=sxs[i2][:, 0:n], op=ALU.mult),
                         reads=[bn[B1], "sxs%d" % i2], writes=["uspad"])
                    taps = [uspad[:, t0 + k:t0 + k + n] for k in range(3)]
                    accv = acc[i2][:, 0:n]
                    pv = banks[B0][:, 0:n]
                    ov = pST[:, cc, t0:t0 + n]
                    srcn = "uspad"
                else:
                    S.op("dve", I("tensor_tensor", out=us_s[:, cc, :, 2:10], in0=banks[B1][:, 0:n].rearrange("p (s i) -> p s i", s=16),
                                  in1=sxs[i2][:, 0:n].rearrange("p (s i) -> p s i", s=16), op=ALU.mult),
                         reads=[bn[B1], "sxs%d" % i2], writes=["us_s"])
                    taps = [us_s[:, cc, :, k:k + 8] for k in range(3)]
                    accv = acc[i2][:, 0:n].rearrange("p (s i) -> p s i", s=16)
                    pv = banks[B0][:, 0:n].rearrange("p (s i) -> p s i", s=16)
                    ov = pST[:, cc, t0:t0 + n].rearrange("p (s i) -> p s i", s=16)
                    srcn = "us_s"
                wr = [vrow(l, "scw", k * 4 + cc) for k in range(3)]
                S.op("dve", I("tensor_scalar", out=accv, in0=taps[2], scalar1=vecT[:, wr[2]:wr[2] + 1], scalar2=None, op0=ALU.mult),
                     reads=[srcn, "vecT"], writes=["acc%d" % i2])
                for k in (1, 0):
                    S.op("dve", I("scalar_tensor_tensor", out=accv, in0=taps[k], scalar=vecT[:, wr[k]:wr[k] + 1], in1=accv,
                                  op0=ALU.mult, op1=ALU.add), reads=[srcn, "vecT"], writes=["acc%d" % i2])
                S.op("dve", I("tensor_tensor", out=ov, in0=pv, in1=accv, op=ALU.mult), reads=[bn[B0], "acc%d" % i2], writes=["src:%d" % g])
            S.op("pe", I("transpose", out=banks[6][0:2, cc * 128:(cc + 1) * 128], in_=uspad[:, NP:NP + 2], identity=ident[:]),
                 reads=["uspad", "ident"], writes=[bn[6]])
            S.op("act", I("copy", out=ust[:, cc, :].rearrange("p (s i) -> p s i", s=16), in_=us_s[:, cc, :, 8:10]), reads=["us_s"], writes=["ust"])
            S.op("pe", I("transpose", out=banks[7][0:32, cc * 128:(cc + 1) * 128], in_=ust[:, cc, :], identity=ident[:]),
                 reads=["ust", "ident"], writes=[bn[7]])
        S.op("act", I("copy", out=so_p[:], in_=banks[6][0:2, :]), reads=[bn[6]], writes=["so_p"])
        S.op("act", I("copy", out=so_s[:], in_=banks[7][0:32, :]), reads=[bn[7]], writes=["so_s"])
        finals.append(S.dma("sp", I("dma_start", out=scp[l], in_=so_p[:]), key="o_scp", reads=["so_p"]))
        finals.append(S.dma("sp", I("dma_start", out=scs[l].rearrange("s i c -> (s i) c"), in_=so_s[:]), key="o_scs", reads=["so_s"]))
        merge(l, 2, pST, 4, w_sc_out, first, False, S_BASE)

    def branch_c(l, first):
        S.barrier()
        S.phase = 'brC%d' % l
        AR = Arena(nc, PH_BASE, SB_END, "bc%d" % l)
        cvb = AR.alloc("cvb", [128, 4, NT], BF16)
        LN_BASE = AR.cur
        ring = [AR.alloc("w%d" % i, [128, 8, 256], BF16) for i in range(2)]
        upad = AR.alloc("upad", [128, 30 + NP], BF16)
        us_c = AR.alloc("us_c", [128, 4, 16, 38], BF16)
        usf = AR.alloc("usf", [128, 4, 128], F32)
        u32p = AR.alloc("u32p", [128, 4, 32], F32)
        dg = AR.alloc("dg", [128, 31, 128], BF16)
        sgs = [AR.alloc("sg%d" % i, [128, 512], F32) for i in range(2)]
        uf = [AR.alloc("uf%d" % i, [128, 512], F32) for i in range(2)]
        cst4 = [AR.alloc("cst%d" % i, [120, 512], F32) for i in range(2)]
        co_p = AR.alloc("co_p", [32, 512], F32)
        co_s = AR.alloc("co_s", [128, 512], F32)
        for j in range(4):
            cs = cst4[j % 2]
            S.dma("sp", I("dma_start", out=cs[:], in_=st_conf[l, 4 * j:4 * j + 4].rearrange("s r c -> (s r) c")), key="cst4%d" % (j % 2),
                  writes=["cst4%d" % (j % 2)])
            pb = 6 + (j % 2)
            for cc in range(4):
                S.op("pe", I("transpose", out=banks[pb][:, cc * 120:(cc + 1) * 120], in_=cs[:, cc * 128:(cc + 1) * 128],
                             identity=ident[0:120, 0:120]), reads=["cst4%d" % (j % 2), "ident"], writes=[bn[pb]])
            S.op("dve", I("tensor_copy", out=us_c[:, :, 4 * j:4 * j + 4, 0:30],
                          in_=banks[pb][:, 0:480].rearrange("p (c s r) -> p c s r", c=4, s=4)), reads=[bn[pb]], writes=["us_c"])
        finals.append(S.dma("sp", I("dma_start", out=confs[l, :, 0:22, :], in_=st_conf[l, :, 8:30, :]), key="o_cfs0"))
        S.op("pool", I("memset", upad[:, 0:30], 0.0), writes=["upad"])
        cnt = 0
        for cc in range(4):
            slot = cc % 2
            wt = ring[slot]
            wn = "cw%d" % slot
            for j, nm in enumerate(("ca", "cb")):
                S.dma("pool", I("dma_start", out=wt[:, :, j * 128:(j + 1) * 128], in_=wcols(l, COL[nm] + cc * 128, 128)), key=wn, writes=[wn])
            for k in range(31):
                r = vrow(l, "ccw", k * 4 + cc)
                S.op("dve", I("tensor_scalar", out=dg[:, k, :], in0=identb[:], scalar1=vecT[:, r:r + 1], scalar2=None, op0=ALU.mult),
                     reads=["identb", "vecT"], writes=["dg"])
            for g, (t0, n) in enumerate(GROUPS):
                i2 = cnt % 2
                cnt += 1
                Ba, Bb = 4 * i2, 4 * i2 + 1
                for j in range(2):
                    for kc in range(8):
                        S.op("pe", I("matmul", banks[4 * i2 + j][:, 0:n], lhsT=wt[:, kc, j * 128:(j + 1) * 128], rhs=hT[:, kc, t0:t0 + n],
                                     start=(kc == 0), stop=(kc == 7)), reads=[wn, "hT:%d" % g], writes=[bn[4 * i2 + j]])
                S.op("act", I("activation", out=sgs[i2][:, 0:n], in_=banks[Bb][:, 0:n], func=AF.Sigmoid), reads=[bn[Bb]], writes=["csg%d" % i2])
                S.op("dve", I("tensor_tensor", out=uf[i2][:, 0:n], in0=banks[Ba][:, 0:n], in1=sgs[i2][:, 0:n], op=ALU.mult),
                     reads=[bn[Ba], "csg%d" % i2], writes=["uf%d" % i2])
                if g < 4:
                    S.op("act", I("copy", out=upad[:, 30 + t0:30 + t0 + n], in_=uf[i2][:, 0:n]), reads=["uf%d" % i2], writes=["upad"])
                    if g == 3:
                        S.op("act", I("copy", out=u32p[:, cc, :], in_=uf[i2][:, 480:512]), reads=["uf%d" % i2], writes=["u32p"])
                else:
                    S.op("act", I("copy", out=us_c[:, cc, :, 30:38], in_=uf[i2][:, 0:n].rearrange("p (s i) -> p s i", s=16)),
                         reads=["uf%d" % i2], writes=["us_c"])
                    S.op("act", I("copy", out=usf[:, cc, :], in_=uf[i2][:, 0:n]), reads=["uf%d" % i2], writes=["usf"])
            r = vrow(l, "ccb", cc)
            for g, (t0, n) in enumerate(GROUPS[:4]):
                pb = 2 + (g % 2)
                for k in range(31):
                    S.op("pe", I("matmul", banks[pb][:, 0:n], lhsT=dg[:, k, :], rhs=upad[:, t0 + k:t0 + k + n], start=(k == 0), stop=(k == 30)),
                         reads=["dg", "upad"], writes=[bn[pb]])
                S.op("act", I("activation", out=cvb[:, cc, t0:t0 + n], in_=banks[pb][:, 0:n], func=AF.Identity, bias=vecT[:, r:r + 1]),
                     reads=[bn[pb], "vecT"], writes=["src:%d" % g])
            flat = us_c[:, cc, :, :].rearrange("p s r -> p (s r)")
            for hf in range(2):
                pb = 2 + hf
                for k in range(31):
                    S.op("pe", I("matmul", banks[pb][:, 0:274], lhsT=dg[:, k, :], rhs=flat[:, hf * 304 + k:hf * 304 + k + 274],
                                 start=(k == 0), stop=(k == 30)), reads=["dg", "us_c"], writes=[bn[pb]])
                S.op("act", I("activation", out=cvb[:, cc, NP + hf * 64:NP + (hf + 1) * 64].rearrange("p (s i) -> p s i", s=8),
                              in_=banks[pb][:, 0:304].rearrange("p (s r) -> p s r", s=8)[:, :, 0:8], func=AF.Identity, bias=vecT[:, r:r + 1]),
                     reads=[bn[pb], "vecT"], writes=["src:4"])
        for cc in range(4):
            S.op("pe", I("transpose", out=banks[6][0:32, cc * 128:(cc + 1) * 128], in_=u32p[:, cc, :], identity=ident[:]),
                 reads=["u32p", "ident"], writes=[bn[6]])
            S.op("pe", I("transpose", out=banks[7][:, cc * 128:(cc + 1) * 128], in_=usf[:, cc, :], identity=ident[:]),
                 reads=["usf", "ident"], writes=[bn[7]])
        S.op("act", I("copy", out=co_p[:], in_=banks[6][0:32, :]), reads=[bn[6]], writes=["co_p"])
        S.op("act", I("copy", out=co_s[:], in_=banks[7][:]), reads=[bn[7]], writes=["co_s"])
        finals.append(S.dma("sp", I("dma_start", out=confp[l], in_=co_p[2:32, :]), key="o_cfp", reads=["co_p"]))
        for s_ in range(16):
            finals.append(S.dma("sp", I("dma_start", out=confs[l, s_, 22:30, :], in_=co_s[s_ * 8:(s_ + 1) * 8, :]), key="o_cfs", reads=["co_s"]))
        S.barrier()
        S.phase = 'brCln%d' % l
        AR = Arena(nc, LN_BASE, SB_END, "bcl%d" % l)
        sqb = [AR.alloc("sqb%d" % i, [128, 512], BF16) for i in range(2)]
        means = [AR.alloc("mean%d" % i, [128, 512], F32) for i in range(2)]
        var = AR.alloc("var", [128, 512], F32)
        rss = [AR.alloc("rs%d" % i, [128, 512], F32) for i in range(2)]
        tt = [AR.alloc("tt%d" % i, [128, 512], F32) for i in range(2)]

        def ln_stats(g):
            t0, n = GROUPS[g]
            p2 = g % 2
            mean, rs = means[p2], rss[p2]
            ba, bb = 4 + 2 * p2, 5 + 2 * p2
            for cc in range(4):
                S.op("act", I("activation", out=sqb[cc % 2][:, 0:n], in_=cvb[:, cc, t0:t0 + n], func=AF.Square),
                     reads=["src:%d" % g], writes=["sqb%d" % (cc % 2)])
                S.op("pe", I("matmul", banks[ba][:, 0:n], lhsT=onesb[:], rhs=cvb[:, cc, t0:t0 + n], start=(cc == 0), stop=(cc == 3)),
                     reads=["src:%d" % g, "onesb"], writes=[bn[ba]])
                S.op("pe", I("matmul", banks[bb][:, 0:n], lhsT=onesb[:], rhs=sqb[cc % 2][:, 0:n], start=(cc == 0), stop=(cc == 3)),
                     reads=["sqb%d" % (cc % 2), "onesb"], writes=[bn[bb]])
            S.op("dve", I("tensor_scalar", out=mean[:, 0:n], in0=banks[ba][:, 0:n], scalar1=1.0 / 512, scalar2=None, op0=ALU.mult),
                 reads=[bn[ba]], writes=["mean%d" % p2])
            S.op("dve", I("tensor_tensor", out=var[:, 0:n], in0=mean[:, 0:n], in1=mean[:, 0:n], op=ALU.mult), reads=["mean%d" % p2], writes=["var"])
            S.op("dve", I("scalar_tensor_tensor", out=var[:, 0:n], in0=banks[bb][:, 0:n], scalar=1.0 / 512, in1=var[:, 0:n],
                          op0=ALU.mult, op1=ALU.subtract), reads=[bn[bb]], writes=["var"])
            S.op("dve", I("tensor_scalar", out=var[:, 0:n], in0=var[:, 0:n], scalar1=1e-5, scalar2=None, op0=ALU.add), writes=["var"])
            S.op("act", I("activation", out=rs[:, 0:n], in_=var[:, 0:n], func=AF.Sqrt), reads=["var"], writes=["rs%d" % p2])
            S.op("dve", I("reciprocal", out=rs[:, 0:n], in_=rs[:, 0:n]), writes=["rs%d" % p2])

        def ln_apply(g):
            t0, n = GROUPS[g]
            p2 = g % 2
            mean, rs = means[p2], rss[p2]
            for cc in range(4):
                t_ = tt[cc % 2]
                S.op("dve", I("tensor_tensor", out=t_[:, 0:n], in0=cvb[:, cc, t0:t0 + n], in1=mean[:, 0:n], op=ALU.subtract),
                     reads=["src:%d" % g, "mean%d" % p2], writes=["ctt%d" % (cc % 2)])
                S.op("dve", I("tensor_tensor", out=t_[:, 0:n], in0=t_[:, 0:n], in1=rs[:, 0:n], op=ALU.mult), reads=["rs%d" % p2],
                     writes=["ctt%d" % (cc % 2)])
                rg, rb = vrow(l, "lng", cc), vrow(l, "lnb", cc)
                S.op("act", I("activation", out=cvb[:, cc, t0:t0 + n], in_=t_[:, 0:n], func=AF.Silu, scale=vecT[:, rg:rg + 1],
                              bias=vecT[:, rb:rb + 1]), reads=["ctt%d" % (cc % 2), "vecT"], writes=["src:%d" % g])

        ln_stats(0)
        for g in range(len(GROUPS)):
            if g + 1 < len(GROUPS):
                ln_stats(g + 1)
            ln_apply(g)
        merge(l, 1, cvb, 4, w_conf_out, first, False, LN_BASE)

    def wo_resid(l):
        S.barrier()
        S.phase = 'wo%d' % l
        AR = Arena(nc, PH_BASE, SB_END, "wo%d" % l)
        wt = AR.alloc("wo", [128, 8, D], BF16)
        for j in range(4):
            S.dma("pool", I("dma_start", out=wt[:, :, j * 256:(j + 1) * 256],
                            in_=w_o[l, :, j * 256:(j + 1) * 256].rearrange("(kc p) n -> p kc n", p=128)), key="mwo%d" % j, writes=["mwo%d" % j])
        c = 0
        for g, (t0, n) in enumerate(GROUPS):
            for d in range(8):
                pb = 4 + (c % 4)
                c += 1
                for kc in range(8):
                    S.op("pe", I("matmul", banks[pb][:, 0:n], lhsT=wt[:, kc, d * 128:(d + 1) * 128], rhs=mT[:, kc, t0:t0 + n],
                                 start=(kc == 0), stop=(kc == 7)), reads=["mwo%d" % (d // 2), "m:%d" % g], writes=[bn[pb]])
                resid_add(pb, d, g, 5)

    def mixer(l):
        S.barrier()
        S.phase = 'mixnorm%d' % l
        AR = Arena(nc, PH_BASE, SB_END, "mx%d" % l)
        norm_mod(AR, 3, 4, "m")
        first = True
        if "R" in BRS:
            branch_r(l)
            first = False
        if "S" in BRS:
            branch_s(l, first)
            first = False
        if "C" in BRS:
            branch_c(l, first)
            first = False
        wo_resid(l)

    for l in range(n_layers):
        ada_layer(l)
        ffn(l, 0)
        if mixer_on:
            mixer(l)
        ffn(l, 1)

    S.barrier()
    S.phase = 'final'
    AR = Arena(nc, PH_BASE, SB_END, "fin")
    sq = [AR.alloc("sq%d" % i, [128, 512], BF16) for i in range(2)]
    vs = [AR.alloc("v%d" % i, [128, 512], F32) for i in range(2)]
    rstds = [AR.alloc("rstd%d" % i, [128, 512], F32) for i in range(2)]
    yTs = [AR.alloc("yT%d" % i, [128, 8, 512], F32) for i in range(2)]
    ost = [AR.alloc("ost%d" % i, [128, D], F32) for i in range(2)]
    gfin = 2 * VR_PER_LAYER
    ocs = {"oc": 0}

    def f_stats(g):
        t0, n = GROUPS[g]
        p2 = g % 2
        pb = 6 + p2
        v, rstd = vs[p2], rstds[p2]
        for kc in range(8):
            s_ = sq[kc % 2]
            S.op("act", I("activation", out=s_[:, 0:n], in_=xT[:, kc, t0:t0 + n], func=AF.Square),
                 reads=[xg(g)], writes=["sq%d" % (kc % 2)])
            S.op("pe", I("matmul", banks[pb][:, 0:n], lhsT=onesb[:], rhs=s_[:, 0:n], start=(kc == 0), stop=(kc == 7)),
                 reads=["sq%d" % (kc % 2), "onesb"], writes=[bn[pb]])
        S.op("dve", I("tensor_scalar", out=v[:, 0:n], in0=banks[pb][:, 0:n], scalar1=1.0 / D, scalar2=1e-6,
                      op0=ALU.mult, op1=ALU.add), reads=[bn[pb]], writes=["v%d" % p2])
        S.op("act", I("activation", out=rstd[:, 0:n], in_=v[:, 0:n], func=AF.Sqrt), reads=["v%d" % p2], writes=["rstd%d" % p2])
        S.op("dve", I("reciprocal", out=rstd[:, 0:n], in_=rstd[:, 0:n]), writes=["rstd%d" % p2])

    def f_apply(g):
        t0, n = GROUPS[g]
        p2 = g % 2
        rstd, yT = rstds[p2], yTs[p2]
        yn = "yT%d" % p2
        for kc in range(8):
            S.op("dve", I("scalar_tensor_tensor", out=yT[:, kc, 0:n], in0=xT[:, kc, t0:t0 + n],
                          scalar=vecT[:, gfin + kc:gfin + kc + 1], in1=rstd[:, 0:n], op0=ALU.mult, op1=ALU.mult),
                 reads=[xg(g), "rstd%d" % p2, "vecT"], writes=[yn])
        for ti in range(n // 128):
            oc = ocs["oc"]
            ocs["oc"] += 1
            o_ = ost[oc % 2]
            on = "ost%d" % (oc % 2)
            for hb in range(2):
                pbk = (oc % 2) * 2 + hb
                for c4 in range(4):
                    c = hb * 4 + c4
                    S.op("pe", I("transpose", out=banks[pbk][:, c4 * 128:(c4 + 1) * 128], in_=yT[:, c, ti * 128:(ti + 1) * 128], identity=ident[:]),
                         reads=[yn, "ident"], writes=[bn[pbk]])
                if hb == 0:
                    S.op("act", I("copy", out=o_[:, 0:512], in_=banks[pbk][:]), reads=[bn[pbk]], writes=[on])
                else:
                    S.op("dve", I("tensor_copy", out=o_[:, 512:1024], in_=banks[pbk][:]), reads=[bn[pbk]], writes=[on])
            tg = t0 // 128 + ti
            dst = yp[tg * 128:(tg + 1) * 128, :] if tg < 16 else ys
            finals.append(S.dma("sp", I("dma_start", out=dst, in_=o_[:]), key="o%d" % (oc % 2), reads=[on]))

    f_stats(0)
    for g in range(len(GROUPS)):
        if g + 1 < len(GROUPS):
            f_stats(g + 1)
        f_apply(g)

    last = {}
    for t in finals:
        last[(t[0], t[1])] = t
    S.emit(final_waits=list(last.values()))
    return nc


def pack_vecs(inp):
    rows = []
    for l in range(L):
        rows.append(inp["b_ada"][l].reshape(72, 128))
        rows.append(inp["g_ffn1"][l].reshape(8, 128))
        rows.append(inp["g_mix"][l].reshape(8, 128))
        rows.append(inp["g_ffn2"][l].reshape(8, 128))
        rows.append(inp["b_gate"][l].reshape(24, 128))
        rows.append(inp["ret_gn_g"][l].reshape(8, 128))
        rows.append(inp["conf_conv_w"][l].reshape(124, 128))
        rows.append(inp["conf_conv_b"][l].reshape(4, 128))
        rows.append(inp["conf_ln_g"][l].reshape(4, 128))
        rows.append(inp["conf_ln_b"][l].reshape(4, 128))
        rows.append(inp["sc_conv_w"][l].reshape(12, 128))
    rows.append(inp["g_final"].reshape(8, 128))
    v = np.ascontiguousarray(np.concatenate(rows, axis=0), dtype=np.float32)
    assert v.shape == (VR_TOTAL, 128)
    return v


_NC_CACHE = {}


def make_consts():
    f32 = np.float32
    lg = np.log(f32(1.0) - np.exp2(-5.0 - np.arange(8, dtype=f32))).astype(f32)
    freqs = (f32(10000.0) ** (-np.arange(32, dtype=f32) / f32(32))).astype(f32)
    p = np.arange(128)
    rot = np.zeros((17, 128, 3, 32), f32)
    for ti in range(17):
        pos = (ti * 128 + p).astype(f32) if ti < 16 else (16384 + (p % 8)).astype(f32)
        ang = (pos[:, None] * freqs[None, :]).astype(f32)
        rot[ti, :, 0] = np.cos(ang)
        rot[ti, :, 1] = np.sin(ang)
        rot[ti, :, 2] = -np.sin(ang)
    dmask = np.zeros((2, 128, 8, 128), f32)
    jj, ii = np.meshgrid(p, p, indexing="ij")
    diff = (ii - jj).astype(f32)
    for h in range(8):
        dec = np.where(diff >= 0, np.exp(np.maximum(diff, 0) * lg[h]), 0.0).astype(f32)
        dmask[0, :, h, :] = dec * f32(0.125)
        d8 = ((ii % 8) - (jj % 8)).astype(f32)
        same = (ii // 8) == (jj // 8)
        dmask[1, :, h, :] = np.where(same & (d8 >= 0), np.exp(np.maximum(d8, 0) * lg[h]), 0.0).astype(f32) * f32(0.125)
    qkdec = np.zeros((128, 2, 2, 8), f32)
    for h in range(8):
        qkdec[:, 0, 0, h] = np.exp((p + 1).astype(f32) * lg[h]) * f32(0.125)
        qkdec[:, 0, 1, h] = np.exp((127 - p).astype(f32) * lg[h])
        qkdec[:, 1, 0, h] = np.exp(((p % 8) + 1).astype(f32) * lg[h]) * f32(0.125)
        qkdec[:, 1, 1, h] = np.exp((7 - (p % 8)).astype(f32) * lg[h])
    zmask = np.zeros((128, 8, 128), f32)
    for pr in range(8):
        for s2 in range(2):
            zmask[s2 * 64:(s2 + 1) * 64, pr, (2 * pr + s2) * 8:(2 * pr + s2 + 1) * 8] = 1.0
    kmask = np.zeros((128, 16), f32)
    kmask[p, p // 8] = 1.0
    return dict(rot=rot, dmask=dmask, qkdec=qkdec, zmask=zmask, kmask=kmask)


def make_in_maps(inp):
    vecs = pack_vecs(inp)
    ident = np.eye(128, dtype=np.float32)
    consts = make_consts()
    maps = []
    for b in range(8):
        m = dict(
            xp=np.ascontiguousarray(inp["x_prompt"][b]),
            xs=np.ascontiguousarray(inp["x_sample"][16 * b:16 * b + 16].reshape(128, D)),
            cc=np.ascontiguousarray(np.concatenate([inp["c_prompt"][b:b + 1], inp["c_sample"][16 * b:16 * b + 16]], axis=0)),
            vecs=vecs, ident=ident,
            w_ada=inp["w_ada"], w1_a=inp["w1_a"], w3_a=inp["w3_a"], w2_a=inp["w2_a"],
            w1_b=inp["w1_b"], w3_b=inp["w3_b"], w2_b=inp["w2_b"],
            w_in=inp["w_in"], w_ret_out=inp["w_ret_out"], w_conf_out=inp["w_conf_out"], w_sc_out=inp["w_sc_out"], w_o=inp["w_o"],
            st_ret=np.ascontiguousarray(inp["state_ret"][:, 16 * b:16 * b + 16]),
            st_conf=np.ascontiguousarray(inp["state_conf"][:, 16 * b:16 * b + 16]),
            st_sc=np.ascontiguousarray(inp["state_sconv"][:, 16 * b:16 * b + 16]),
            **consts,
        )
        maps.append(m)
    return maps


def kernel(**inputs):
    inp = {k: np.asarray(v) for k, v in inputs.items()}
    if "nc" not in _NC_CACHE:
        _NC_CACHE["nc"] = build_program()
    nc = _NC_CACHE["nc"]
    maps = make_in_maps(inp)
    res = run_bass_kernel_spmd(nc, maps, core_ids=list(range(8)))
    r = res.results
    y_prompt = np.stack([r[b]["yp"] for b in range(8)], axis=0)
    y_sample = np.concatenate([r[b]["ys"].reshape(16, 8, D) for b in range(8)], axis=0)
    ret_p = np.stack([r[b]["retp"] for b in range(8)], axis=1)
    ret_s = np.concatenate([r[b]["rets"] for b in range(8)], axis=1)
    conf_p = np.stack([r[b]["confp"] for b in range(8)], axis=1)
    conf_s = np.concatenate([r[b]["confs"] for b in range(8)], axis=1)
    sc_p = np.stack([r[b]["scp"] for b in range(8)], axis=1)
    sc_s = np.concatenate([r[b]["scs"] for b in range(8)], axis=1)
    return (y_prompt, y_sample, ret_p, ret_s, conf_p, conf_s, sc_p, sc_s)
```
